# Optimizing a Trainium2 kernel written in Bass

```python
import math
import jax, jax.numpy as jnp
from jax import lax
import numpy as np

D_MODEL = 2048
BATCH = 4
SEQ = 2048
DEPTH = 1
DEC_BATCH = 128
DEC_SEQ = 1
PAST_LEN = 16384
PAGE_SIZE = 128

RET_WIDTH = 1024
RET_HEADS = 8
RET_DK = RET_WIDTH // RET_HEADS
RET_DV = RET_WIDTH // RET_HEADS
RET_CHUNK = 128
ROPE_BASE = 10000.0
S5_WIDTH = 1024
S5_GROUP = 16
S5_GROUPS = S5_WIDTH // S5_GROUP
S5_P = 64
S5_CHUNK = 128
DT_MIN = 1e-3
DT_MAX = 1e-1
EPS = 1e-6
IN_COLS = 4 * RET_WIDTH + 2 * S5_WIDTH + 2 * D_MODEL

kernel_name = 'hybrid_retention_s5_gated_step'

F32 = jnp.float32


def rmsnorm(x, g):
    xf = x.astype(F32)
    y = xf * lax.rsqrt(jnp.mean(xf * xf, axis=-1, keepdims=True) + EPS) * g.astype(F32)
    return y.astype(x.dtype)


def rope(x, pos):
    half = x.shape[-1] // 2
    inv = ROPE_BASE ** (-jnp.arange(half, dtype=F32) / half)
    ang = pos.astype(F32)[:, None] * inv[None, :]
    cos = jnp.cos(ang)[None, :, None, :]
    sin = jnp.sin(ang)[None, :, None, :]
    x1, x2 = x[..., :half], x[..., half:]
    return jnp.concatenate([x1 * cos - x2 * sin, x1 * sin + x2 * cos], axis=-1)


def ret_log_gamma():
    return jnp.log(1.0 - 2.0 ** (-5.0 - jnp.arange(RET_HEADS, dtype=F32)))


def retention_chunk(q, k, v, s0, lg):
    c = q.shape[2]
    idx = jnp.arange(c, dtype=F32)
    diff = idx[:, None] - idx[None, :]
    mask = jnp.where(diff[None] >= 0, jnp.exp(jnp.maximum(diff, 0.0)[None] * lg[:, None, None]), 0.0)
    scores = jnp.einsum('bhqd,bhkd->bhqk', q, k) * mask[None]
    inner = jnp.einsum('bhqk,bhkv->bhqv', scores, v)
    q_dec = q * jnp.exp((idx[None, :] + 1.0) * lg[:, None])[None, :, :, None]
    cross = jnp.einsum('bhqd,bhdv->bhqv', q_dec, s0)
    k_dec = k * jnp.exp((c - 1.0 - idx[None, :]) * lg[:, None])[None, :, :, None]
    s_new = jnp.exp(c * lg)[None, :, None, None] * s0 + jnp.einsum('bhkd,bhkv->bhdv', k_dec, v)
    return inner + cross, s_new


def retention(q, k, v, pos, s0):
    bn, L, _ = q.shape
    q = rope(q.reshape(bn, L, RET_HEADS, RET_DK).astype(F32), pos)
    k = rope(k.reshape(bn, L, RET_HEADS, RET_DK).astype(F32), pos) * (RET_DK ** -0.5)
    v = v.reshape(bn, L, RET_HEADS, RET_DV).astype(F32)
    lg = ret_log_gamma()
    c = RET_CHUNK if L % RET_CHUNK == 0 else L
    nc = L // c

    def to_chunks(t):
        return t.reshape(bn, nc, c, RET_HEADS, t.shape[-1]).transpose(1, 0, 3, 2, 4)

    def step(s, qkv):
        qc, kc, vc = qkv
        o, s = retention_chunk(qc, kc, vc, s, lg)
        return s, o

    s_fin, o = lax.scan(step, s0.astype(F32), (to_chunks(q), to_chunks(k), to_chunks(v)))
    o = o.transpose(1, 0, 3, 2, 4).reshape(bn, L, RET_HEADS, RET_DV)
    mu = jnp.mean(o, axis=-1, keepdims=True)
    var = jnp.mean(jnp.square(o - mu), axis=-1, keepdims=True)
    o = (o - mu) * lax.rsqrt(var + EPS)
    return o.reshape(bn, L, RET_WIDTH), s_fin


def s5_combine(e1, e2):
    a1r, a1i, b1r, b1i = e1
    a2r, a2i, b2r, b2i = e2
    ar = a1r * a2r - a1i * a2i
    ai = a1r * a2i + a1i * a2r
    br = a2r * b1r - a2i * b1i + b2r
    bi = a2r * b1i + a2i * b1r + b2i
    return ar, ai, br, bi


def s5(u, x0r, x0i, lam_re, lam_im, log_dt, b_re, b_im, c_re, c_im, d_skip, w_glu, b_glu):
    bn, L, _ = u.shape
    u = u.reshape(bn, L, S5_GROUPS, S5_GROUP).astype(F32)
    lr, li = lam_re.astype(F32), lam_im.astype(F32)
    dt = jnp.exp(log_dt.astype(F32))[:, None]
    mag = jnp.exp(lr * dt)
    abar_r = mag * jnp.cos(li * dt)
    abar_i = mag * jnp.sin(li * dt)
    nr, ni = abar_r - 1.0, abar_i
    den = lr * lr + li * li
    coef_r = (nr * lr + ni * li) / den
    coef_i = (ni * lr - nr * li) / den
    br, bi = b_re.astype(F32), b_im.astype(F32)
    bbar_r = coef_r[:, :, None] * br - coef_i[:, :, None] * bi
    bbar_i = coef_r[:, :, None] * bi + coef_i[:, :, None] * br
    cr, ci = c_re.astype(F32), c_im.astype(F32)
    dsk = d_skip.astype(F32)
    c = S5_CHUNK if L % S5_CHUNK == 0 else L
    nc = L // c
    uc = u.reshape(bn, nc, c, S5_GROUPS, S5_GROUP).transpose(1, 0, 2, 3, 4)

    def step(carry, u_c):
        xr0, xi0 = carry
        bur = jnp.einsum('gpn,bcgn->bcgp', bbar_r, u_c)
        bui = jnp.einsum('gpn,bcgn->bcgp', bbar_i, u_c)
        ar = jnp.broadcast_to(abar_r, bur.shape)
        ai = jnp.broadcast_to(abar_i, bur.shape)
        A_r, A_i, X_r, X_i = lax.associative_scan(s5_combine, (ar, ai, bur, bui), axis=1)
        xr = X_r + A_r * xr0[:, None] - A_i * xi0[:, None]
        xi = X_i + A_r * xi0[:, None] + A_i * xr0[:, None]
        y = (jnp.einsum('gnp,bcgp->bcgn', cr, xr) - jnp.einsum('gnp,bcgp->bcgn', ci, xi)
             + dsk[None, None] * u_c)
        return (xr[:, -1], xi[:, -1]), y

    (sr, si), y = lax.scan(step, (x0r.astype(F32), x0i.astype(F32)), uc)
    y = y.transpose(1, 0, 2, 3, 4).reshape(bn, L, S5_WIDTH)
    y = jax.nn.gelu(y)
    y = y * jax.nn.sigmoid(y @ w_glu.astype(F32) + b_glu.astype(F32))
    return y, sr, si


def layer(x, pos, s_ret, s_re, s_im, g_pre, w_in, w_pa, w_pb, w_out, g_post,
          lam_re, lam_im, log_dt, b_re, b_im, c_re, c_im, d_skip, w_glu, b_glu):
    h = rmsnorm(x, g_pre)
    proj = h @ w_in
    splits = np.cumsum([RET_WIDTH, RET_WIDTH, RET_WIDTH, RET_WIDTH, S5_WIDTH, S5_WIDTH, D_MODEL]).tolist()
    q, k, v, za, ub, zb, ga, gb = jnp.split(proj, splits, axis=-1)
    ya, s_ret_new = retention(q, k, v, pos, s_ret)
    ya = (ya * jax.nn.silu(za.astype(F32))).astype(x.dtype) @ w_pa
    yb, s_re_new, s_im_new = s5(ub, s_re, s_im, lam_re, lam_im, log_dt, b_re, b_im, c_re, c_im, d_skip, w_glu, b_glu)
    yb = (yb * jax.nn.silu(zb.astype(F32))).astype(x.dtype) @ w_pb
    merged = jax.nn.sigmoid(ga) * ya + jax.nn.sigmoid(gb) * yb
    out = merged @ w_out
    return x + rmsnorm(out, g_post), s_ret_new, s_re_new, s_im_new


def setup_inputs(seed: int = 0) -> dict:
    key = jax.random.key(seed)
    ks = jax.random.split(key, 24)
    nrm = jax.random.normal
    n = jnp.arange(S5_P, dtype=F32)
    return {
        'x_prompt': nrm(ks[0], (BATCH, SEQ, D_MODEL), F32),
        'x_sample': nrm(ks[1], (DEC_BATCH, DEC_SEQ, D_MODEL), F32),
        'state_ret': 0.5 * nrm(ks[2], (DEPTH, DEC_BATCH, RET_HEADS, RET_DK, RET_DV), F32),
        'state_s5_re': 0.1 * nrm(ks[3], (DEPTH, DEC_BATCH, S5_GROUPS, S5_P), F32),
        'state_s5_im': 0.1 * nrm(ks[4], (DEPTH, DEC_BATCH, S5_GROUPS, S5_P), F32),
        'g_pre': 1.0 + 0.05 * nrm(ks[5], (DEPTH, D_MODEL), F32),
        'w_in': nrm(ks[6], (DEPTH, D_MODEL, IN_COLS), F32) * D_MODEL ** -0.5,
        'w_pa': nrm(ks[7], (DEPTH, RET_WIDTH, D_MODEL), F32) * RET_WIDTH ** -0.5,
        'w_pb': nrm(ks[8], (DEPTH, S5_WIDTH, D_MODEL), F32) * S5_WIDTH ** -0.5,
        'w_out': nrm(ks[9], (DEPTH, D_MODEL, D_MODEL), F32) * D_MODEL ** -0.5,
        'g_post': 1.0 + 0.05 * nrm(ks[10], (DEPTH, D_MODEL), F32),
        's5_lam_re': -0.5 + 0.01 * nrm(ks[11], (DEPTH, S5_GROUPS, S5_P), F32),
        's5_lam_im': math.pi * n + 0.01 * nrm(ks[12], (DEPTH, S5_GROUPS, S5_P), F32),
        's5_log_dt': jax.random.uniform(ks[13], (DEPTH, S5_GROUPS), F32, math.log(DT_MIN), math.log(DT_MAX)),
        's5_b_re': nrm(ks[14], (DEPTH, S5_GROUPS, S5_P, S5_GROUP), F32) * (2 * S5_GROUP) ** -0.5,
        's5_b_im': nrm(ks[15], (DEPTH, S5_GROUPS, S5_P, S5_GROUP), F32) * (2 * S5_GROUP) ** -0.5,
        's5_c_re': nrm(ks[16], (DEPTH, S5_GROUPS, S5_GROUP, S5_P), F32) * S5_P ** -0.5,
        's5_c_im': nrm(ks[17], (DEPTH, S5_GROUPS, S5_GROUP, S5_P), F32) * S5_P ** -0.5,
        's5_d': nrm(ks[18], (DEPTH, S5_GROUPS, S5_GROUP), F32),
        's5_w_glu': nrm(ks[19], (DEPTH, S5_WIDTH, S5_WIDTH), F32) * S5_WIDTH ** -0.5,
        's5_b_glu': 0.01 * nrm(ks[20], (DEPTH, S5_WIDTH), F32),
    }


def reference(x_prompt, x_sample, state_ret, state_s5_re, state_s5_im, g_pre, w_in, w_pa, w_pb, w_out, g_post,
              s5_lam_re, s5_lam_im, s5_log_dt, s5_b_re, s5_b_im, s5_c_re, s5_c_im, s5_d, s5_w_glu, s5_b_glu):
    bp, lp, _ = x_prompt.shape
    ds = x_sample.shape[1]
    pos_p = jnp.arange(lp, dtype=jnp.int32)
    pos_s = PAST_LEN + jnp.arange(ds, dtype=jnp.int32)
    hp, hs = x_prompt, x_sample
    rp, rep, imp, rs, res, ims = [], [], [], [], [], []
    for l in range(DEPTH):
        w = (g_pre[l], w_in[l], w_pa[l], w_pb[l], w_out[l], g_post[l], s5_lam_re[l], s5_lam_im[l], s5_log_dt[l],
             s5_b_re[l], s5_b_im[l], s5_c_re[l], s5_c_im[l], s5_d[l], s5_w_glu[l], s5_b_glu[l])
        zr = jnp.zeros((bp, RET_HEADS, RET_DK, RET_DV), F32)
        zs = jnp.zeros((bp, S5_GROUPS, S5_P), F32)
        hp, a, b, c = layer(hp, pos_p, zr, zs, zs, *w)
        rp.append(a); rep.append(b); imp.append(c)
        hs, a, b, c = layer(hs, pos_s, state_ret[l], state_s5_re[l], state_s5_im[l], *w)
        rs.append(a); res.append(b); ims.append(c)
    return (hp, hs, jnp.stack(rp), jnp.stack(rep), jnp.stack(imp), jnp.stack(rs), jnp.stack(res), jnp.stack(ims))
```

```python
import contextlib
import math
import numpy as np
import concourse.bass as bass
import concourse.mybir as mybir
from concourse.bass_utils import run_bass_kernel_spmd

F32 = mybir.dt.float32
BF16 = mybir.dt.bfloat16
ALU = mybir.AluOpType
AF = mybir.ActivationFunctionType
AX = mybir.AxisListType

D = 2048
NH = 8
G = 64
EPS = 1e-6
PAST = 16384
TWO_PI = 2.0 * math.pi


NOSAME_PHASES = {"p0", "p1", "p1s", "p2h", "ret", "s5", "glu", "mrg"}


class Buf:
    __slots__ = ("name", "w", "r", "dsem", "dcnt", "uid")
    _n = [0]

    def __init__(self, name):
        Buf._n[0] += 1
        self.uid = Buf._n[0]
        self.name = name
        self.w = []
        self.r = []
        self.dsem = None
        self.dcnt = 0


class Sched:
    ENG = ("pe", "act", "dve", "pool", "sp")

    def __init__(self, nc, stack):
        self.nc = nc
        self.stack = stack
        self.e = {"pe": nc.tensor, "act": nc.scalar, "dve": nc.vector,
                  "pool": nc.gpsimd, "sp": nc.sync}
        self.sem = {k: stack.enter_context(nc.semaphore("s_" + k)) for k in self.ENG}
        self.bsem = stack.enter_context(nc.semaphore("s_bar"))
        self.bcnt = 0
        self.cnt = {k: 0 for k in self.ENG}
        self.known = {k: {} for k in self.ENG}
        self.semobj = {("e", k): self.sem[k] for k in self.ENG}
        self.dmas = {}
        self.nsem = 0
        self.phase = ""
        self.force_same = False
        self.nosame = False
        self.waited = {}

    def _dsem(self, buf):
        if buf.dsem is None:
            buf.dsem = self.stack.enter_context(self.nc.semaphore("d%d" % self.nsem))
            self.semobj[("d", buf.uid)] = buf.dsem
            self.nsem += 1
        return buf.dsem

    def _need(self, eng, dep, waits):
        kind, key, val = dep
        k = (kind, key)
        if kind == "e" and key == eng and (eng == "pe" or self.nosame or
                                           (eng == "dve" and self.phase in NOSAME_PHASES and not self.force_same)):
            return
        if self.known[eng].get(k, 0) >= val:
            return
        waits[k] = max(waits.get(k, 0), val)

    def _deps(self, eng, reads, writes, skip_w_key=None):
        waits = {}
        for b in reads:
            for t in b.w:
                self._need(eng, t, waits)
        for b in writes:
            for t in b.w:
                if skip_w_key is not None and t[0] == "d":
                    continue
                self._need(eng, t, waits)
            for d in b.r:
                self._need(eng, d, waits)
        for k, v in waits.items():
            self.e[eng].wait_ge(self.semobj[k], v)
            self.known[eng][k] = v
            if k[0] == "d":
                self.waited[k] = max(self.waited.get(k, 0), v)

    def op(self, eng, fn, reads=(), writes=(), inc=True):
        self._deps(eng, reads, writes)
        ins = fn(self.e[eng])
        if inc:
            self.cnt[eng] += 1
            ins.then_inc(self.sem[eng], 1)
            tag = ("e", eng, self.cnt[eng])
        else:
            tag = ("e", eng, self.cnt[eng] + 1)
        for b in reads:
            b.r.append(tag)
        for b in writes:
            b.w = [tag]
            b.r = []
        return ins

    def dma(self, out_ap, in_ap, reads=(), writes=(), acc=False, eng="sp", sem_of=None, **kw):
        assert len(writes) == 1
        wb = writes[0]
        sb_ = sem_of if sem_of is not None else wb
        self._deps(eng, reads, writes, skip_w_key=(sb_.uid if acc else None))
        sem = self._dsem(sb_)
        kk = ("d", sb_.uid)
        wv = self.waited.get(kk, 0)
        if wv > self.known[eng].get(kk, 0):
            self.e[eng].wait_ge(sem, wv)
            self.known[eng][kk] = wv
        sb_.dcnt += 16
        self.e[eng].dma_start(out=out_ap, in_=in_ap, **kw).then_inc(sem, 16)
        tag = ("d", sb_.uid, sb_.dcnt)
        self.dmas[kk] = sb_.dcnt
        for b in reads:
            b.r.append(tag)
        if acc:
            wb.w = [t for t in wb.w if not (t[0] == "d" and t[1] == sb_.uid)] + [tag]
        else:
            wb.w = [tag]
            wb.r = []

    def barrier(self):
        sp = self.e["sp"]
        for k in ("pe", "act", "dve", "pool"):
            if self.cnt[k] > self.known["sp"].get(("e", k), 0):
                sp.wait_ge(self.sem[k], self.cnt[k])
        for key, cnt in self.dmas.items():
            if cnt > self.known["sp"].get(key, 0):
                sp.wait_ge(self.semobj[key], cnt)
                self.waited[key] = max(self.waited.get(key, 0), cnt)
        self.bcnt += 1
        sp.sem_inc(self.bsem, 1)
        for k in ("pe", "act", "dve", "pool"):
            self.e[k].wait_ge(self.bsem, self.bcnt)
        for k in self.ENG:
            for k2 in self.ENG:
                self.known[k][("e", k2)] = self.cnt[k2]
            for key, cnt in self.dmas.items():
                self.known[k][key] = cnt

    def tt(self, eng, out, in0, in1, op, r, w):
        return self.op(eng, lambda e: e.tensor_tensor(out=out, in0=in0, in1=in1, op=op), r, w)

    def ts(self, eng, out, in0, s1, op0, r, w, s2=None, op1=None):
        if op1 is None:
            return self.op(eng, lambda e: e.tensor_scalar(out=out, in0=in0, scalar1=s1, scalar2=None,
                                                          op0=op0), r, w)
        return self.op(eng, lambda e: e.tensor_scalar(out=out, in0=in0, scalar1=s1, scalar2=s2,
                                                      op0=op0, op1=op1), r, w)

    def stt(self, eng, out, in0, scalar, in1, op0, op1, r, w):
        return self.op(eng, lambda e: e.scalar_tensor_tensor(out=out, in0=in0, scalar=scalar, in1=in1,
                                                             op0=op0, op1=op1), r, w)

    def cp(self, eng, out, in_, r, w):
        if eng == "act":
            return self.op(eng, lambda e: e.copy(out=out, in_=in_), r, w)
        return self.op(eng, lambda e: e.tensor_copy(out=out, in_=in_), r, w)

    def act(self, out, in_, func, r, w, **kw):
        return self.op("act", lambda e: e.activation(out=out, in_=in_, func=func, **kw), r, w)


CT = {}


def _ct_layout():
    off = 0
    for name, n in (("ident", 128), ("maskR", 1024), ("dqM", 8), ("dkM", 8), ("gC", 8), ("dkP", 64),
                    ("gam1", 8), ("ohT", 256), ("cm", 128), ("sel", 128), ("kv", 9), ("pvec", 4),
                    ("oh16", 16), ("kS", 8), ("one", 8)):
        CT[name] = (off, n)
        off += n
    return off


NCT = _ct_layout()


def make_ctab():
    t = np.zeros((128, NCT), np.float64)

    def put(name, arr):
        o, n = CT[name]
        t[:, o:o + n] = np.asarray(arr, np.float64).reshape(128, n)

    gam = 1.0 - 2.0 ** (-5.0 - np.arange(NH))
    kk = np.arange(128)
    put("ident", np.eye(128))
    m = np.zeros((128, NH, 128))
    for h in range(NH):
        m[:, h, :] = (gam[h] ** (-(kk[:, None] + 1.0))) * (128 ** -0.5) * (kk[None, :] >= kk[:, None])
    put("maskR", m)
    put("dqM", gam[None, :] ** (kk[:, None] + 1.0))
    put("dkM", gam[None, :] ** (127.0 - kk[:, None]) * 128 ** -0.5)
    put("gC", np.broadcast_to(gam[None, :] ** 128.0, (128, NH)))
    dkp = np.zeros((128, 8, NH))
    for c in range(8):
        dkp[:, c, :] = gam[None, :] ** (1023.0 - (c * 128 + kk[:, None])) * 128 ** -0.5
    put("dkP", dkp)
    put("gam1", np.broadcast_to(gam[None, :], (128, NH)))
    oh = np.zeros((128, 16, 16))
    oh[:, np.arange(16), np.arange(16)] = 1.0
    put("ohT", oh)
    s_of = np.arange(128) // 16
    m_of = np.arange(128) % 16
    put("cm", (s_of[None, :] >= s_of[:, None]).astype(np.float64))
    put("sel", ((s_of[None, :] == s_of[:, None]) & (m_of[None, :] == m_of[:, None])).astype(np.float64))
    put("kv", np.broadcast_to(np.arange(9.0)[None, :], (128, 9)))
    pv = np.zeros((128, 4))
    pv[:64, 0] = 1.0
    pv[64:, 1] = 1.0
    pv[:64, 2] = -1.0
    pv[64:, 3] = -1.0
    put("pvec", pv)
    o16 = np.zeros((128, 16))
    o16[np.arange(16), np.arange(16)] = 1.0
    put("oh16", o16)
    put("kS", np.full((128, 8), 128 ** -0.5))
    put("one", np.ones((128, 8)))
    return t.astype(np.float32)


def make_rope(pos):
    inv = (np.float32(10000.0) ** (-np.arange(64, dtype=np.float32) / np.float32(64))).astype(np.float32)
    ang = (pos.astype(np.float32)[None, :] * inv[:, None]).astype(np.float32).astype(np.float64)
    c = np.cos(ang)
    s = np.sin(ang)
    cosT = np.concatenate([c, c], 0)
    sinX = np.concatenate([s, -s], 0)
    return np.stack([cosT, sinX]).astype(np.float32)


class _Stop(Exception):
    pass


def build_program(stop=None, debug=False):
    nc = bass.Bass("TRN2", target_bir_lowering=False)

    def din(name, shape, dt=F32):
        return nc.dram_tensor(name, list(shape), dt, kind="ExternalInput").ap()

    def dout(name, shape):
        return nc.dram_tensor(name, list(shape), F32, kind="ExternalOutput").ap()

    def dscr(name, shape, dt=BF16):
        return nc.dram_tensor(name, list(shape), dt, kind="ExternalOutput" if debug else "Internal").ap()

    xP = din("xP", [1024, D]); xM = din("xM", [1024, D]); xS = din("xS", [16, D])
    w_in = din("w_in", [80, 128, 16, 128]); w_pa = din("w_pa", [16, 128, 8, 128]); w_pb = din("w_pb", [16, 128, 8, 128])
    w_out = din("w_out", [16, 128, 16, 128]); w_glu = din("w_glu", [8, 128, 8, 128])
    g_pre = din("g_pre", [D]); g_post = din("g_post", [D]); b_glu = din("b_glu", [1024])
    sret = din("sret", [16, NH, 128, 128]); s5re = din("s5re", [16, G, 64]); s5im = din("s5im", [16, G, 64])
    lam_re = din("lam_re", [G, 64]); lam_im = din("lam_im", [G, 64]); log_dt = din("log_dt", [G])
    b_re = din("b_re", [G, 64, 16]); b_im = din("b_im", [G, 64, 16])
    c_re = din("c_re", [G, 16, 64]); c_im = din("c_im", [G, 16, 64]); s5d = din("s5d", [G, 16])
    ctab = din("ctab", [128, NCT]); ropeP = din("ropeP", [2, 128, 1024]); ropeM = din("ropeM", [2, 128, 1040])

    yM = dout("yM", [1024, D]); yS = dout("yS", [16, D]); retP = dout("retP", [NH, 128, 128])
    s5reP = dout("s5reP", [G, 64]); s5imP = dout("s5imP", [G, 64])
    retS = dout("retS", [16, NH, 128, 128]); s5reS = dout("s5reS", [16, G, 64]); s5imS = dout("s5imS", [16, G, 64])

    DU = dscr("DU", [G, 8, 16, 128]); DUS = dscr("DUS", [G, 16, 16])
    DTm = dscr("DTm", [G, 128, 128]); DGT = dscr("DGT", [G, 128, 128]); DR = dscr("DR", [G, 128, 144])
    DY = dscr("DY", [G, 16, 8, 128]); DYS = dscr("DYS", [G, 16, 16])
    b_DU = Buf("DU"); b_DUS = Buf("DUS"); b_DT = Buf("DT"); b_DGT = Buf("DGT"); b_DR = Buf("DR")
    b_DY = Buf("DY"); b_DYS = Buf("DYS")
    out_bufs = []

    try:
      with contextlib.ExitStack() as st:
        S = Sched(nc, st)

        def checkpoint(k):
            if stop == k:
                S.barrier()
                raise _Stop()

        ARW = 49152 - 1024
        arena = st.enter_context(nc.sbuf_tensor("arena", [128, ARW], F32))
        AR = {"off": 0, "top": ARW, "peak": 0}

        def _view(off, shape, dt):
            n = int(np.prod(shape[1:]))
            words = n if dt == F32 else (n + 1) // 2
            v = arena[0:shape[0], off:off + words]
            if dt != F32:
                v = v.bitcast(dt)[:, 0:n]
            if len(shape) == 3:
                v = v.rearrange("q (a b) -> q a b", a=shape[1])
            elif len(shape) == 4:
                v = v.rearrange("q (a b c) -> q a b c", a=shape[1], b=shape[2])
            elif len(shape) == 5:
                v = v.rearrange("q (a b c d) -> q a b c d", a=shape[1], b=shape[2], c=shape[3])
            return v, (words + 7) // 8 * 8

        def sb(stack, name, shape, dt=F32):
            start = AR["off"]
            v, words = _view(start, shape, dt)
            AR["off"] = start + words
            AR["peak"] = max(AR["peak"], AR["off"])
            assert AR["off"] <= AR["top"], ("SBUF arena overflow", name, AR["off"], AR["top"])
            stack.callback(lambda: AR.__setitem__("off", min(AR["off"], start)))
            return v

        def sb_top(name, shape, dt=F32):
            n = int(np.prod(shape[1:]))
            words = ((n if dt == F32 else (n + 1) // 2) + 7) // 8 * 8
            AR["top"] -= words
            assert AR["off"] <= AR["top"], ("SBUF arena overflow (top)", name)
            return _view(AR["top"], shape, dt)[0]

        PS = [st.enter_context(nc.psum_tensor("ps%d" % i, [128, 512], F32)) for i in range(8)]
        PB = [Buf("ps%d" % i) for i in range(8)]

        def psb(i):
            return PS[i][:, :].bitcast(BF16)

        ct = sb(st, "ct", [128, NCT]); b_ct = Buf("ct")
        S.dma(ct[:, :], ctab, writes=[b_ct])

        def C(name, lo=0, hi=None):
            o, n = CT[name]
            return ct[:, o + lo:o + (n if hi is None else hi)]

        identb = sb(st, "identb", [128, 128], BF16); b_id = Buf("identb")
        S.cp("dve", identb[:, :], C("ident"), [b_ct], [b_id])
        ident = C("ident")
        Sst = sb(st, "Sst", [128, NH, 128]); b_Sst = Buf("Sst")
        Xmid = sb(st, "Xmid", [64, 2, G]); b_Xmid = Buf("Xmid")
        A8 = sb(st, "A8", [64, 4, G]); b_A8 = Buf("A8")
        A1 = sb(st, "A1", [64, 4, G])
        Am7 = sb(st, "Am7", [64, 4, G])
        GREP = [None]; b_grep = Buf("grep")
        NSLOT = 3
        AR_WSLOT = [AR["off"]]
        wst = [sb(st, "wst%d" % i, [128, 16, 128]) for i in range(NSLOT)]
        wbf = [sb(st, "wbf%d" % i, [128, 16, 128], BF16) for i in range(NSLOT)]
        b_wst = [Buf("wst%d" % i) for i in range(NSLOT)]
        b_wbf = [Buf("wbf%d" % i) for i in range(NSLOT)]
        wctr = [0]

        def wload(src, KT):
            i = wctr[0] % NSLOT
            wctr[0] += 1
            S.dma(wst[i][:, 0:KT, :], src, writes=[b_wst[i]])
            S.cp("pool" if wctr[0] % 5 == 0 else "act", wbf[i][:, 0:KT, :], wst[i][:, 0:KT, :], [b_wst[i]], [b_wbf[i]])
            return wbf[i], b_wbf[i]

        WQ = []
        for h_ in range(NH):
            WQ.append((w_in[8 + h_], 16))
        for h_ in range(NH):
            WQ.append((w_in[16 + h_], 16))
        for f_ in range(8):
            WQ.append((w_in[32 + f_], 16))
        for hh_ in range(2):
            for hl_ in range(4):
                for base_ in (0, 8, 16, 24):
                    WQ.append((w_in[base_ + hh_ * 4 + hl_], 16))
        for f_ in range(8):
            WQ.append((w_in[32 + f_], 16))
        WQ.append(None)
        for f_ in range(8):
            WQ.append((w_glu[f_], 8)); WQ.append((w_in[40 + f_], 16))
        for d_ in range(16):
            WQ += [(w_in[48 + d_], 16), (w_pa[d_], 8), (w_in[64 + d_], 16), (w_pb[d_], 8)]
        wq_pos = [0]
        wq_pend = []

        def wq_fill():
            while len(wq_pend) < 2 and wq_pos[0] < len(WQ) and WQ[wq_pos[0]] is not None:
                src_, kt_ = WQ[wq_pos[0]]
                wq_pos[0] += 1
                wq_pend.append(wload(src_, kt_) + (src_,))

        def wnext(expect=None):
            if not wq_pend:
                if wq_pos[0] < len(WQ) and WQ[wq_pos[0]] is None:
                    wq_pos[0] += 1
                wq_fill()
            wb_, bwb_, src_ = wq_pend.pop(0)
            wq_fill()
            return wb_, bwb_

        def mm_group(bank, ncols, pairs, reads, m=128):
            n = len(pairs)
            for i, (l, r) in enumerate(pairs):
                last = i == n - 1
                edge = last or i == 0
                S.op("pe", lambda e: e.matmul(PS[bank][0:m, 0:ncols], lhsT=l, rhs=r, start=(i == 0), stop=last),
                     reads if edge else (), [PB[bank]] if edge else (), inc=last)

        S.phase = "p0"
        with contextlib.ExitStack() as ph:
            lr = sb(ph, "lr", [128, G]); li = sb(ph, "li", [128, G]); ldt = sb(ph, "ldt", [128, G])
            brr = sb(ph, "brr", [128, G, 16]); bii = sb(ph, "bii", [128, G, 16])
            cnat = sb(ph, "cnat", [128, 2, 8, 2, 64])
            crT = sb(ph, "crT", [128, G, 16]); ciT = sb(ph, "ciT", [128, G, 16])
            drep = sb(ph, "drep", [128, G, 16])
            b_par = Buf("par")
            lrT = lam_re.rearrange("g p -> p g"); liT = lam_im.rearrange("g p -> p g")
            for hlf in range(2):
                ps_ = slice(hlf * 64, hlf * 64 + 64)
                S.dma(lr[ps_, :], lrT, writes=[b_par], acc=True, allow_slow_non_contiguous=True)
                S.dma(li[ps_, :], liT, writes=[b_par], acc=True, allow_slow_non_contiguous=True)
                S.dma(brr[ps_, :, :], b_re.rearrange("g p m -> p g m"), writes=[b_par], acc=True)
                S.dma(bii[ps_, :, :], b_im.rearrange("g p m -> p g m"), writes=[b_par], acc=True)
                S.dma(cnat[:, 0, :, hlf, :], c_re.rearrange("(ft gl) n p -> (gl n) ft p", gl=8), writes=[b_par], acc=True)
                S.dma(cnat[:, 1, :, hlf, :], c_im.rearrange("(ft gl) n p -> (gl n) ft p", gl=8), writes=[b_par], acc=True)
            S.dma(ldt[:, :], log_dt.partition_broadcast(128), writes=[b_par], acc=True)
            S.dma(drep[:, :, :], s5d.rearrange("g n -> (g n)").partition_broadcast(128).rearrange("p (g n) -> p g n", n=16),
                  writes=[b_par], acc=True)

            checkpoint(-1)
            b_cT = Buf("cT")
            for ri, dst in ((0, crT), (1, ciT)):
                for ft in range(8):
                    bk = 6 + (ft % 2)
                    S.op("pe", lambda e: e.transpose(out=PS[bk][:, 0:128], in_=cnat[:, ri, ft, :, :].rearrange("q a p -> q (a p)"),
                                                     identity=ident), [b_par, b_ct], [PB[bk]])
                    S.cp("act", dst[:, ft * 8:(ft + 1) * 8, :].rearrange("q g n -> q (g n)"), PS[bk][:, 0:128], [PB[bk]], [b_cT])

            checkpoint(-2)
            b_w = Buf("p0w")
            dtt = sb(ph, "dtt", [128, G]); aa = sb(ph, "aa", [128, G]); th = sb(ph, "th", [128, G])
            S.act(dtt[:, :], ldt[:, :], AF.Exp, [b_par], [b_w])
            S.tt("dve", aa[:, :], lr[:, :], dtt[:, :], ALU.mult, [b_par, b_w], [b_w])
            S.tt("dve", th[:, :], li[:, :], dtt[:, :], ALU.mult, [b_par, b_w], [b_w])
            KA = sb(ph, "KA", [128, 9, G]); ANG = sb(ph, "ANG", [128, 2, 9, G]); RN = sb(ph, "RN", [128, 2, 9, G])
            kvb = C("kv").unsqueeze(2).broadcast_to([128, 9, G])
            S.tt("dve", KA[:, :, :], aa[:, :].unsqueeze(1).broadcast_to([128, 9, G]), kvb, ALU.mult, [b_w, b_ct], [b_w])
            S.tt("dve", ANG[:, 0, :, :], th[:, :].unsqueeze(1).broadcast_to([128, 9, G]), kvb, ALU.mult, [b_w, b_ct], [b_w])
            S.ts("dve", ANG[:, 1, :, :], ANG[:, 0, :, :], math.pi / 2, ALU.add, [b_w], [b_w])
            MAGIC = 12582912.0
            S.ts("dve", RN[:, :, :, :], ANG[:, :, :, :], 1.0 / TWO_PI, ALU.mult, [b_w], [b_w])
            S.ts("dve", RN[:, :, :, :], RN[:, :, :, :], MAGIC, ALU.add, [b_w], [b_w])
            S.ts("dve", RN[:, :, :, :], RN[:, :, :, :], MAGIC, ALU.subtract, [b_w], [b_w])
            S.stt("dve", ANG[:, :, :, :], RN[:, :, :, :], -TWO_PI, ANG[:, :, :, :], ALU.mult, ALU.add, [b_w], [b_w])
            S.ts("dve", ANG[:, :, :, :], ANG[:, :, :, :], 3.1415925, ALU.min, [b_w], [b_w], s2=-3.1415925, op1=ALU.max)
            SC = sb(ph, "SC", [128, 2, 9, G])
            S.act(SC[:, :, :, :], ANG[:, :, :, :], AF.Sin, [b_w], [b_w])
            EP = sb(ph, "EP", [128, 9, G]); EN = sb(ph, "EN", [128, 9, G])
            S.act(EP[:, :, :], KA[:, :, :], AF.Exp, [b_w], [b_w])
            S.act(EN[:, :, :], KA[:, :, :], AF.Exp, [b_w], [b_w], scale=-1.0)
            Pr = sb(ph, "Pr", [128, 9, G]); Pi = sb(ph, "Pi", [128, 9, G])
            Nr = sb(ph, "Nr", [128, 9, G]); Ni = sb(ph, "Ni", [128, 9, G])
            S.tt("dve", Pr[:, :, :], EP[:, :, :], SC[:, 1, :, :], ALU.mult, [b_w], [b_w])
            S.tt("dve", Pi[:, :, :], EP[:, :, :], SC[:, 0, :, :], ALU.mult, [b_w], [b_w])
            S.tt("dve", Nr[:, :, :], EN[:, :, :], SC[:, 1, :, :], ALU.mult, [b_w], [b_w])
            S.stt("dve", Ni[:, :, :], EN[:, :, :], -1.0, SC[:, 0, :, :], ALU.mult, ALU.mult, [b_w], [b_w])
            nr = sb(ph, "nr", [128, G]); den = sb(ph, "den", [128, G]); t0 = sb(ph, "t0", [128, G]); t1 = sb(ph, "t1", [128, G])
            cfr = sb(ph, "cfr", [128, G]); cfi = sb(ph, "cfi", [128, G])
            S.ts("dve", nr[:, :], Pr[:, 1, :], -1.0, ALU.add, [b_w], [b_w])
            S.tt("dve", den[:, :], lr[:, :], lr[:, :], ALU.mult, [b_par], [b_w])
            S.tt("dve", t0[:, :], li[:, :], li[:, :], ALU.mult, [b_par], [b_w])
            S.tt("dve", den[:, :], den[:, :], t0[:, :], ALU.add, [b_w], [b_w])
            S.op("dve", lambda e: e.reciprocal(out=den[:, :], in_=den[:, :]), [b_w], [b_w])
            S.tt("dve", t0[:, :], nr[:, :], lr[:, :], ALU.mult, [b_w, b_par], [b_w])
            S.tt("dve", t1[:, :], Pi[:, 1, :], li[:, :], ALU.mult, [b_w, b_par], [b_w])
            S.tt("dve", t0[:, :], t0[:, :], t1[:, :], ALU.add, [b_w], [b_w])
            S.tt("dve", cfr[:, :], t0[:, :], den[:, :], ALU.mult, [b_w], [b_w])
            S.tt("dve", t0[:, :], Pi[:, 1, :], lr[:, :], ALU.mult, [b_w, b_par], [b_w])
            S.tt("dve", t1[:, :], nr[:, :], li[:, :], ALU.mult, [b_w, b_par], [b_w])
            S.tt("dve", t0[:, :], t0[:, :], t1[:, :], ALU.subtract, [b_w], [b_w])
            S.tt("dve", cfi[:, :], t0[:, :], den[:, :], ALU.mult, [b_w], [b_w])

            def cmul(outr, outi, ar, ai, br_, bi_, shape, tmpa, tmpb):
                S.tt("dve", tmpa, ar, br_, ALU.mult, [b_w], [b_w])
                S.tt("dve", tmpb, ai, bi_, ALU.mult, [b_w], [b_w])
                S.tt("dve", outr, tmpa, tmpb, ALU.subtract, [b_w], [b_w])
                S.tt("dve", tmpa, ar, bi_, ALU.mult, [b_w], [b_w])
                S.tt("dve", tmpb, ai, br_, ALU.mult, [b_w], [b_w])
                S.tt("dve", outi, tmpa, tmpb, ALU.add, [b_w], [b_w])

            Dr = sb(ph, "Dr", [128, 8, G]); Di = sb(ph, "Di", [128, 8, G])
            DPr = sb(ph, "DPr", [128, 8, G]); DPi = sb(ph, "DPi", [128, 8, G])
            ta = sb(ph, "ta", [128, 8, G]); tb = sb(ph, "tb", [128, 8, G])
            cfrb = cfr[:, :].unsqueeze(1).broadcast_to([128, 8, G]); cfib = cfi[:, :].unsqueeze(1).broadcast_to([128, 8, G])
            cmul(Dr[:, :, :], Di[:, :, :], Nr[:, 0:8, :], Ni[:, 0:8, :], cfrb, cfib, None, ta[:, :, :], tb[:, :, :])
            PRr = sb(ph, "PRr", [128, 8, G]); PRi = sb(ph, "PRi", [128, 8, G])
            for s_ in range(8):
                S.cp("dve", PRr[:, s_, :], Pr[:, 7 - s_, :], [b_w], [b_w])
                S.cp("dve", PRi[:, s_, :], Pi[:, 7 - s_, :], [b_w], [b_w])
            cmul(DPr[:, :, :], DPi[:, :, :], PRr[:, :, :], PRi[:, :, :], cfrb, cfib, None, ta[:, :, :], tb[:, :, :])

            mre = C("pvec", 0, 1); mim = C("pvec", 1, 2); nmre = C("pvec", 2, 3); nmim = C("pvec", 3, 4)

            def cat(out, xr, m0, xi, m1):
                S.ts("dve", out, xr, m0, ALU.mult, [b_w, b_ct], [b_w])
                S.stt("dve", out, xi, m1, out, ALU.mult, ALU.add, [b_w, b_ct], [b_w])

            Dcat = sb(ph, "Dcat", [128, 8, G]); Dsw = sb(ph, "Dsw", [128, 8, G])
            Gc = sb(ph, "Gc", [128, 8, G]); Gs = sb(ph, "Gs", [128, 8, G])
            Pc = sb(ph, "Pc", [128, 9, G]); Pw = sb(ph, "Pw", [128, 9, G])
            cat(Dcat[:, :, :], Dr[:, :, :], mre, Di[:, :, :], mim)
            cat(Dsw[:, :, :], Di[:, :, :], nmre, Dr[:, :, :], mim)
            cat(Gc[:, :, :], DPr[:, :, :], mre, DPi[:, :, :], mim)
            cat(Gs[:, :, :], DPi[:, :, :], nmre, DPr[:, :, :], mim)
            cat(Pc[:, :, :], Pr[:, :, :], mre, Pi[:, :, :], nmim)
            cat(Pw[:, :, :], Pi[:, :, :], nmre, Pr[:, :, :], nmim)
            for dst, srcr, srci, k in ((A8, Pr, Pi, 8), (A1, Pr, Pi, 1), (Am7, Nr, Ni, 7)):
                S.cp("dve", dst[:, 0, :], srcr[0:64, k, :], [b_w], [b_A8])
                S.cp("dve", dst[:, 1, :], srci[0:64, k, :], [b_w], [b_A8])
                S.ts("dve", dst[:, 2, :], srci[0:64, k, :], -1.0, ALU.mult, [b_w], [b_A8])
                S.cp("dve", dst[:, 3, :], srcr[0:64, k, :], [b_w], [b_A8])

            checkpoint(-3)
            osets = []
            for i_ in range(2):
                osets.append(dict(
                    Lc=sb(ph, "Lc%d" % i_, [128, 8, 8, 16]), Gq=sb(ph, "Gq%d" % i_, [128, 8, 8, 16]),
                    Rc=sb(ph, "Rc%d" % i_, [128, 8, 9, 16]), tmpL=sb(ph, "tmpL%d" % i_, [128, 8, 9, 16]),
                    Tb=sb(ph, "Tb%d" % i_, [128, 8, 128], BF16), GTb=sb(ph, "GTb%d" % i_, [128, 8, 128], BF16),
                    Rb=sb(ph, "Rb%d" % i_, [128, 8, 144], BF16),
                    b_L=Buf("L"), b_Gq=Buf("Gq"), b_R=Buf("R"), b_t=Buf("tmpL"), b_Tb=Buf("Tb"), b_GTb=Buf("GTb"), b_Rb=Buf("Rb")))
            for hf in range(8):
                gs = slice(hf * 8, hf * 8 + 8)
                if True:
                    o_ = osets[hf % 2]
                    Lc, Gq, Rc, tmpL, Tb, GTb, Rb = o_["Lc"], o_["Gq"], o_["Rc"], o_["tmpL"], o_["Tb"], o_["GTb"], o_["Rb"]
                    b_L, b_Gq, b_R, b_t, b_Tb, b_GTb, b_Rb = o_["b_L"], o_["b_Gq"], o_["b_R"], o_["b_t"], o_["b_Tb"], o_["b_GTb"], o_["b_Rb"]

                    def bc_sg(x):
                        return x[:, :, gs].rearrange("q s g -> q g s").unsqueeze(3).broadcast_to([128, 8, 8, 16])

                    def bc_gm(x):
                        return x[:, gs, :].unsqueeze(2).broadcast_to([128, 8, 8, 16])

                    S.tt("dve", Lc[:, :, :, :], bc_sg(Dcat), bc_gm(brr), ALU.mult, [b_w, b_par], [b_L])
                    S.tt("dve", tmpL[:, :, 0:8, :], bc_sg(Dsw), bc_gm(bii), ALU.mult, [b_w, b_par], [b_t])
                    S.tt("dve", Lc[:, :, :, :], Lc[:, :, :, :], tmpL[:, :, 0:8, :], ALU.add, [b_L, b_t], [b_L])
                    S.tt("dve", Gq[:, :, :, :], bc_sg(Gc), bc_gm(brr), ALU.mult, [b_w, b_par], [b_Gq])
                    S.tt("dve", tmpL[:, :, 0:8, :], bc_sg(Gs), bc_gm(bii), ALU.mult, [b_w, b_par], [b_t])
                    S.tt("dve", Gq[:, :, :, :], Gq[:, :, :, :], tmpL[:, :, 0:8, :], ALU.add, [b_Gq, b_t], [b_Gq])
                    pcb = Pc[:, :, gs].rearrange("q s g -> q g s").unsqueeze(3).broadcast_to([128, 8, 9, 16])
                    pwb = Pw[:, :, gs].rearrange("q s g -> q g s").unsqueeze(3).broadcast_to([128, 8, 9, 16])
                    crb = crT[:, gs, :].unsqueeze(2).broadcast_to([128, 8, 9, 16])
                    cib = ciT[:, gs, :].unsqueeze(2).broadcast_to([128, 8, 9, 16])
                    S.tt("dve", Rc[:, :, :, :], pcb, crb, ALU.mult, [b_w, b_cT], [b_R])
                    S.tt("dve", tmpL[:, :, :, :], pwb, cib, ALU.mult, [b_w, b_cT], [b_t])
                    S.tt("dve", Rc[:, :, :, :], Rc[:, :, :, :], tmpL[:, :, :, :], ALU.add, [b_R, b_t], [b_R])
                    checkpoint(-4)
                    S.cp("act", Rb[:, :, :], Rc[:, :, :, :].rearrange("q g s n -> q g (s n)"), [b_R], [b_Rb])
                    S.dma(DR[gs, :, :].rearrange("g q c -> q g c"), Rb[:, :, :], reads=[b_Rb], writes=[b_DR], acc=True, eng="act", sem_of=b_Rb)
                    checkpoint(-5)
                    for g4 in range(2):
                        bk = 2 * (g4 % 2)
                        for gi in range(4):
                            g = g4 * 4 + gi
                            S.op("pe", lambda e: e.matmul(PS[bk][:, gi * 128:(gi + 1) * 128],
                                                          lhsT=Lc[:, g, :, :].rearrange("q s m -> q (s m)"),
                                                          rhs=Rc[:, g, 0:8, :].rearrange("q s n -> q (s n)"), start=True, stop=True),
                                 [b_L, b_R], [PB[bk]])
                            S.op("pe", lambda e: e.transpose(out=PS[bk + 1][:, gi * 128:(gi + 1) * 128],
                                                             in_=Gq[:, g, :, :].rearrange("q s m -> q (s m)"), identity=ident),
                                 [b_Gq, b_ct], [PB[bk + 1]])
                        tsl = Tb[:, g4 * 4:(g4 + 1) * 4, :]
                        tm = tmpL[:, 0:4, 0:8, :]
                        S.tt("dve", tm, C("sel").rearrange("q (s n) -> q s n", n=16).unsqueeze(1).broadcast_to([128, 4, 8, 16]),
                             drep[:, hf * 8 + g4 * 4: hf * 8 + g4 * 4 + 4, :].unsqueeze(2).broadcast_to([128, 4, 8, 16]),
                             ALU.mult, [b_ct, b_par], [b_t])
                        tm2 = tmpL[:, 4:8, 0:8, :]
                        S.tt("dve", tm2.rearrange("q g s n -> q g (s n)"), PS[bk][:, :].rearrange("q (g c) -> q g c", g=4),
                             C("cm").unsqueeze(1).broadcast_to([128, 4, 128]), ALU.mult, [PB[bk], b_ct], [b_t])
                        S.tt("dve", tsl, tm.rearrange("q g s n -> q g (s n)"), tm2.rearrange("q g s n -> q g (s n)"), ALU.add, [b_t], [b_Tb])
                        S.cp("act", GTb[:, g4 * 4:(g4 + 1) * 4, :].rearrange("q g c -> q (g c)"), PS[bk + 1][:, :], [PB[bk + 1]], [b_GTb])
                    checkpoint(-6)
                    S.dma(DTm[gs, :, :].rearrange("g q c -> q g c"), Tb[:, :, :], reads=[b_Tb], writes=[b_DT], acc=True, eng="act", sem_of=b_Tb)
                    S.dma(DGT[gs, :, :].rearrange("g q c -> q g c"), GTb[:, :, :], reads=[b_GTb], writes=[b_DGT], acc=True, eng="act", sem_of=b_GTb)
        if debug:
            dbgA = nc.dram_tensor("dbgA", [3, 64, 4, G], F32, kind="ExternalOutput").ap()
            b_dbgA = Buf("dbgA")
            for i_, t_ in enumerate((A8, A1, Am7)):
                S.dma(dbgA[i_], t_[:, :, :], reads=[b_A8], writes=[b_dbgA], acc=True)
        S.barrier()
        checkpoint(0)

        def load_grep(stack, vec):
            GREP[0] = sb(stack, "grep", [128, D])
            S.dma(GREP[0][:, :], vec.partition_broadcast(128), writes=[b_grep])

        def build_hT(ph, srcs, hT, hTb):
            xt = [sb(ph, "xt%d" % i, [128, D]) for i in range(2)]
            b_xt = [Buf("xt%d" % i) for i in range(2)]
            junk = sb(ph, "junk", [128, D], BF16); b_junk = Buf("junk")
            hb = [sb(ph, "hb%d" % i, [128, D], BF16) for i in range(2)]
            b_hb = [Buf("hb%d" % i) for i in range(2)]
            ss = sb(ph, "ss", [128, len(srcs)]); b_ss = [Buf("ss%d" % i) for i in range(len(srcs))]
            for i, (src, nrows) in enumerate(srcs):
                s_ = i % 2
                if nrows < 128:
                    S.op("pool", lambda e: e.memset(xt[s_][:, :], 0.0), [], [b_xt[s_]])
                S.dma(xt[s_][0:nrows, :], src, writes=[b_xt[s_]])
                S.op("pool", lambda e: e.memset(ss[:, i:i + 1], 0.0), [], [b_ss[i]])
                S.act(junk[:, :], xt[s_][:, :], AF.Square, [b_xt[s_]], [b_junk, b_ss[i]], accum_out=ss[:, i:i + 1])
                S.ts("dve", ss[:, i:i + 1], ss[:, i:i + 1], 1.0 / D, ALU.mult, [b_ss[i]], [b_ss[i]], s2=EPS, op1=ALU.add)
                S.act(ss[:, i:i + 1], ss[:, i:i + 1], AF.Sqrt, [b_ss[i]], [b_ss[i]])
                S.op("dve", lambda e: e.reciprocal(out=ss[:, i:i + 1], in_=ss[:, i:i + 1]), [b_ss[i]], [b_ss[i]])
                S.stt("dve", hb[s_][:, :], xt[s_][:, :], ss[:, i:i + 1], GREP[0][:, :], ALU.mult, ALU.mult,
                      [b_xt[s_], b_ss[i], b_grep], [b_hb[s_]])
                for half in range(2):
                    bk = 6 + half
                    for k8 in range(8):
                        kt = half * 8 + k8
                        S.op("pe", lambda e: e.transpose(out=psb(bk)[:, k8 * 128:(k8 + 1) * 128],
                                                         in_=hb[s_][:, kt * 128:(kt + 1) * 128], identity=identb[:, :]),
                             [b_hb[s_], b_id], [PB[bk]])
                    S.cp("act" if half == 0 else "dve", hT[:, half * 8:half * 8 + 8, i * 128:(i + 1) * 128],
                         psb(bk).rearrange("q (k t) -> q k t", k=8), [PB[bk]], [hTb[i]])

        def proj(wb, bwb, KT, rhsT, rbufs, groups, banks):
            for gi, (n0, nw) in enumerate(groups):
                mm_group(banks[gi], nw, [(wb[:, kt, :], rhsT[:, kt, n0:n0 + nw]) for kt in range(KT)],
                         [bwb] + rbufs[gi])

        def rope_evac(bank, nw, cosT, sinX, n0, b_tab, out_ap, b_out, tA, tB, b_tA, b_tB, gi=0):
            tA, tB, b_tA, b_tB = tA[gi], tB[gi], b_tA[gi], b_tB[gi]
            S.tt("dve", tA[:, 0:nw], PS[bank][:, 0:nw], cosT[:, n0:n0 + nw], ALU.mult, [PB[bank], b_tab], [b_tA])
            S.tt("dve", tB[0:64, 0:nw], PS[bank][64:128, 0:nw], sinX[64:128, n0:n0 + nw], ALU.mult, [PB[bank], b_tab], [b_tB])
            S.tt("dve", tB[64:128, 0:nw], PS[bank][0:64, 0:nw], sinX[0:64, n0:n0 + nw], ALU.mult, [PB[bank], b_tab], [b_tB])
            S.tt("dve", out_ap, tA[:, 0:nw], tB[:, 0:nw], ALU.add, [b_tA, b_tB], [b_out])

        S.phase = "p1"
        GR1 = [(0, 512), (512, 512)]
        with contextlib.ExitStack() as ph:
            hT = sb(ph, "hT1", [128, 16, 1024], BF16); hTb = [Buf("hT1_%d" % i) for i in range(8)]
            rope = sb(ph, "rope1", [128, 2, 1024]); b_rope = Buf("rope1")
            S.dma(rope[:, :, :], ropeP.rearrange("a q t -> q a t"), writes=[b_rope])
            with contextlib.ExitStack() as p2:
                load_grep(p2, g_pre)
                build_hT(p2, [(xP[i * 128:(i + 1) * 128, :], 128) for i in range(8)], hT, hTb)
                S.barrier()
            rb1 = [hTb[0:4], hTb[4:8]]
            ktok = sb(ph, "ktokP", [128, 8, NH, 128], BF16); b_ktok = [Buf("ktokP%d" % h) for h in range(NH)]
            vtok = sb(ph, "vtokP", [128, 8, NH, 128], BF16); b_vtok = [Buf("vtokP%d" % h) for h in range(NH)]
            kr = [sb(ph, "kr%d" % i, [128, 1024], BF16) for i in range(2)]; b_kr = [Buf("kr%d" % i) for i in range(2)]
            tA = [sb(ph, "tA%d" % i, [128, 512]) for i in range(2)]; tB = [sb(ph, "tB%d" % i, [128, 512]) for i in range(2)]
            b_tA = [Buf("tA%d" % i) for i in range(2)]; b_tB = [Buf("tB%d" % i) for i in range(2)]
            U2 = [sb(ph, "U2_%d" % i, [128, 8, 128], BF16) for i in range(2)]; b_U2 = [Buf("U2_%d" % i) for i in range(2)]
            blocks = [("k", h) for h in range(NH)] + [("v", h) for h in range(NH)] + [("u", f) for f in range(8)]

            def wsrc(kind, i):
                base = {"k": 8, "v": 16, "u": 32}[kind]
                return w_in[base + i]

            tails = []
            for bi, (kind, idx) in enumerate(blocks):
                wb, bwb = wnext()
                banks = [0, 1] if bi % 2 == 0 else [2, 3]
                proj(wb, bwb, 16, hT, rb1, GR1, banks)
                while tails:
                    tails.pop(0)()
                s_ = bi % 2
                if kind == "k":
                    for gi, (n0, nw) in enumerate(GR1):
                        rope_evac(banks[gi], nw, rope[:, 0, :], rope[:, 1, :], n0, b_rope, kr[s_][:, n0:n0 + nw], b_kr[s_], tA, tB, b_tA, b_tB, gi)
                    def tail(s_=s_, idx=idx):
                        for c in range(8):
                            S.op("pe", lambda e: e.transpose(out=psb(6)[:, c * 128:(c + 1) * 128], in_=kr[s_][:, c * 128:(c + 1) * 128],
                                                             identity=identb[:, :]), [b_kr[s_], b_id], [PB[6]])
                        S.tt("dve", ktok[:, :, idx, :], psb(6).rearrange("q (c d) -> q c d", c=8),
                             C("dkP").rearrange("q (c h) -> q c h", h=NH)[:, :, idx:idx + 1].broadcast_to([128, 8, 128]),
                             ALU.mult, [PB[6], b_ct], [b_ktok[idx]])
                    tails.append(tail)
                elif kind == "v":
                    for gi, (n0, nw) in enumerate(GR1):
                        S.cp("act", kr[s_][:, n0:n0 + nw], PS[banks[gi]][:, 0:nw], [PB[banks[gi]]], [b_kr[s_]])
                    def tail(s_=s_, idx=idx):
                        for c in range(8):
                            S.op("pe", lambda e: e.transpose(out=psb(7)[:, c * 128:(c + 1) * 128], in_=kr[s_][:, c * 128:(c + 1) * 128],
                                                             identity=identb[:, :]), [b_kr[s_], b_id], [PB[7]])
                        S.cp("act", vtok[:, :, idx, :], psb(7).rearrange("q (c d) -> q c d", c=8), [PB[7]], [b_vtok[idx]])
                    tails.append(tail)
                else:
                    for gi, (n0, nw) in enumerate(GR1):
                        S.cp("act", U2[s_][:, :, gi * 64:(gi + 1) * 64],
                             PS[banks[gi]][:, 0:nw].rearrange("q (j s) -> q s j", s=8), [PB[banks[gi]]], [b_U2[s_]])
                    for gl in range(8):
                        S.dma(DU[idx * 8 + gl].rearrange("s m j -> m s j"), U2[s_][gl * 16:(gl + 1) * 16, :, :],
                              reads=[b_U2[s_]], writes=[b_DU], acc=True, sem_of=b_U2[s_], eng="act")
            while tails:
                tails.pop(0)()
            for h in range(NH):
                bk = 4 + h // 4
                for c in range(8):
                    S.op("pe", lambda e: e.matmul(PS[bk][:, (h % 4) * 128:(h % 4 + 1) * 128], lhsT=ktok[:, c, h, :], rhs=vtok[:, c, h, :],
                                                  start=(c == 0), stop=(c == 7)),
                         [b_ktok[h], b_vtok[h]] if c in (0, 7) else (), [PB[bk]] if c in (0, 7) else (), inc=(c == 7))
            for half in range(2):
                S.cp("act", Sst[:, half * 4:(half + 1) * 4, :].rearrange("q h v -> q (h v)"), PS[4 + half][:, :], [PB[4 + half]], [b_Sst])
            S.barrier()
            checkpoint(1)

        def build_apow(stack):
            Ap = sb(stack, "Apow", [64, 7, 4, G]); b_Ap = Buf("Apow")
            tq = sb(stack, "tq", [64, 3, G]); b_tq = Buf("tq")
            S.cp("dve", Ap[:, 0, :, :], A8[:, :, :], [b_A8], [b_Ap])
            for l_ in range(1, 7):
                pr, pi = Ap[:, l_ - 1, 0, :], Ap[:, l_ - 1, 1, :]
                S.tt("dve", tq[:, 0, :], pr, pr, ALU.mult, [b_Ap], [b_tq])
                S.tt("dve", tq[:, 1, :], pi, pi, ALU.mult, [b_Ap], [b_tq])
                S.tt("dve", tq[:, 2, :], pr, pi, ALU.mult, [b_Ap], [b_tq])
                S.tt("dve", Ap[:, l_, 0, :], tq[:, 0, :], tq[:, 1, :], ALU.subtract, [b_tq], [b_Ap])
                S.ts("dve", Ap[:, l_, 1, :], tq[:, 2, :], 2.0, ALU.mult, [b_tq], [b_Ap])
                S.ts("dve", Ap[:, l_, 2, :], tq[:, 2, :], -2.0, ALU.mult, [b_tq], [b_Ap])
                S.cp("dve", Ap[:, l_, 3, :], Ap[:, l_, 0, :], [b_Ap], [b_Ap])
            return Ap, b_Ap

        def bk_scan(BX, b_BX, g0, Ap, b_Ap, t, w, b_t, b_w):
            for gh in range(2):
                gl = slice(gh * 16, gh * 16 + 16)
                gg = slice(g0 + gh * 16, g0 + gh * 16 + 16)

                def cmadd(tgt, src, lvl, m):
                    ar = Ap[:, lvl, 0:1, gg].unsqueeze(3).broadcast_to([64, 2, 16, m])
                    S.tt("dve", t[:, :, :, 0:m], src, ar, ALU.mult, [b_BX, b_Ap], [b_t])
                    S.tt("dve", w[:, 0, :, 0:m], src[:, 1], Ap[:, lvl, 2, gg].unsqueeze(2).broadcast_to([64, 16, m]), ALU.mult, [b_BX, b_Ap], [b_w])
                    S.tt("dve", w[:, 1, :, 0:m], src[:, 0], Ap[:, lvl, 1, gg].unsqueeze(2).broadcast_to([64, 16, m]), ALU.mult, [b_BX, b_Ap], [b_w])
                    S.tt("dve", t[:, :, :, 0:m], t[:, :, :, 0:m], w[:, :, :, 0:m], ALU.add, [b_t, b_w], [b_t])
                    S.tt("dve", tgt, tgt, t[:, :, :, 0:m], ALU.add, [b_BX, b_t], [b_BX])

                cmadd(BX[:, :, gl, 1:2], BX[:, :, gl, 0:1], 0, 1)
                Y = BX[:, :, gl, 1:129]
                for l_ in range(7):
                    s_ = 1 << l_
                    Yv = Y.rearrange("q c g (i t) -> q c g i t", t=2 * s_)
                    cmadd(Yv[:, :, :, :, 2 * s_ - 1], Yv[:, :, :, :, s_ - 1], l_, 128 // (2 * s_))
                for l_ in range(5, -1, -1):
                    s_ = 1 << l_
                    m = 128 // (2 * s_) - 1
                    Zv = BX[:, :, gl, 2 * s_:2 * s_ + 2 * s_ * m].rearrange("q c g (i t) -> q c g i t", t=2 * s_)
                    cmadd(Zv[:, :, :, :, s_], Zv[:, :, :, :, 0], l_, m)

        def scan_steps(BX, b_BX, g0, ng, nsteps, u, w, b_u, b_w2):
            ar2 = A8[:, 0:1, g0:g0 + ng].broadcast_to([64, 2, ng])
            S.nosame = True
            for j in range(nsteps):
                prev = BX[:, :, :, j]
                S.tt("dve", u[:, :, :], prev, ar2, ALU.mult, [b_BX, b_A8], [b_u])
                S.tt("dve", w[:, 0, :], BX[:, 1, :, j], A8[:, 2, g0:g0 + ng], ALU.mult, [b_BX, b_A8], [b_w2])
                S.tt("dve", w[:, 1, :], BX[:, 0, :, j], A8[:, 1, g0:g0 + ng], ALU.mult, [b_BX, b_A8], [b_w2])
                S.tt("dve", u[:, :, :], u[:, :, :], w[:, :, :], ALU.add, [b_u, b_w2], [b_u])
                S.tt("dve", BX[:, :, :, j + 1], BX[:, :, :, j + 1], u[:, :, :], ALU.add, [b_BX, b_u], [b_BX])
            S.nosame = False

        S.phase = "p1s"
        with contextlib.ExitStack() as ph:
            BX = sb(ph, "BX1", [64, 2, G, 129]); b_BX = Buf("BX1")
            pB = contextlib.ExitStack()
            U = sb(pB, "U1", [128, G, 128], BF16); b_U = Buf("U1")
            GTs = sb(pB, "GT1", [128, G, 128], BF16); b_GT = Buf("GT1")
            S.dma(U[:, :, :], DU.rearrange("g s m j -> (s m) g j"), reads=[b_DU], writes=[b_U])
            S.dma(GTs[:, :, :], DGT.rearrange("g q c -> q g c"), reads=[b_DGT], writes=[b_GT])
            S.op("pool", lambda e: e.memset(BX[:, :, :, 0:1], 0.0), [], [b_BX])
            for g in range(G):
                bk = g % 4
                for c in range(2):
                    S.op("pe", lambda e: e.matmul(PS[bk][0:64, c * 128:(c + 1) * 128], lhsT=GTs[:, g, c * 64:(c + 1) * 64], rhs=U[:, g, :],
                                                  start=True, stop=True), [b_GT, b_U], [PB[bk]])
                S.cp("act", BX[:, :, g, 1:129], PS[bk][0:64, 0:256].rearrange("q (c j) -> q c j", c=2), [PB[bk]], [b_BX])
            S.barrier()
            pB.close()
            Ap, b_Ap = build_apow(ph)
            Za = sb(ph, "Za", [64, 2, 32, 64]); Zb = sb(ph, "Zb", [64, 2, 32, 32]); b_Za = Buf("Za"); b_Zb = Buf("Zb")
            tt_ = sb(ph, "trt", [64, 2, 32, 64]); tw_ = sb(ph, "trw", [64, 2, 32, 64]); b_tt = Buf("trt"); b_tw = Buf("trw")
            for hf_ in range(2):
                gsl = slice(hf_ * 32, hf_ * 32 + 32)
                src, b_src = BX[:, :, gsl, 1:129], b_BX
                n_ = 128
                for l_ in range(7):
                    n2 = n_ // 2
                    Lv = src.rearrange("q c g (i two) -> q c g i two", two=2)[:, :, :, :, 0]
                    Rv = src.rearrange("q c g (i two) -> q c g i two", two=2)[:, :, :, :, 1]
                    if l_ % 2 == 0:
                        dst, b_dst = Za[:, :, :, 0:n2], b_Za
                    else:
                        dst, b_dst = Zb[:, :, :, 0:n2], b_Zb
                    ar = Ap[:, l_, 0:1, gsl].unsqueeze(3).broadcast_to([64, 2, 32, n2])
                    S.tt("dve", tt_[:, :, :, 0:n2], Lv, ar, ALU.mult, [b_src, b_Ap], [b_tt])
                    S.tt("dve", tw_[:, 0, :, 0:n2], Lv[:, 1], Ap[:, l_, 2, gsl].unsqueeze(2).broadcast_to([64, 32, n2]), ALU.mult, [b_src, b_Ap], [b_tw])
                    S.tt("dve", tw_[:, 1, :, 0:n2], Lv[:, 0], Ap[:, l_, 1, gsl].unsqueeze(2).broadcast_to([64, 32, n2]), ALU.mult, [b_src, b_Ap], [b_tw])
                    S.tt("dve", tt_[:, :, :, 0:n2], tt_[:, :, :, 0:n2], tw_[:, :, :, 0:n2], ALU.add, [b_tt, b_tw], [b_tt])
                    S.tt("dve", dst, tt_[:, :, :, 0:n2], Rv, ALU.add, [b_tt, b_src], [b_dst])
                    src, b_src, n_ = dst, b_dst, n2
                S.cp("dve", Xmid[:, :, gsl], src[:, :, :, 0], [b_src], [b_Xmid])
            if debug:
                dbgX = nc.dram_tensor("dbgX", [64, 2, G], F32, kind="ExternalOutput").ap(); b_dbgX = Buf("dbgX")
                S.dma(dbgX, Xmid[:, :, :], reads=[b_Xmid], writes=[b_dbgX])
                dbgBX = nc.dram_tensor("dbgBX", [64, 2, G, 129], F32, kind="ExternalOutput").ap(); b_dbgBX = Buf("dbgBX")
                S.dma(dbgBX, BX[:, :, :, :], reads=[b_BX], writes=[b_dbgBX])
            S.barrier()
            checkpoint(2)

        S.phase = "p2h"
        GR2 = [(0, 512), (512, 512), (1024, 16)]
        NT2 = 1040
        with contextlib.ExitStack() as ph:
            hT = sb(ph, "hT2", [128, 16, 1152], BF16); hTb = [Buf("hT2_%d" % i) for i in range(9)]
            with contextlib.ExitStack() as p2:
                load_grep(p2, g_pre)
                build_hT(p2, [(xM[i * 128:(i + 1) * 128, :], 128) for i in range(8)] + [(xS, 16)], hT, hTb)
                S.barrier()
                checkpoint(3)
            rb2 = [hTb[0:4], hTb[4:8], hTb[8:9]]
            yainT = sb(ph, "yainT", [128, 8, NT2], BF16); b_yain = [Buf("yain%d" % i) for i in range(9)]
            b_mrg = [Buf("mrg%d" % i) for i in range(16)]

            S.phase = "ret"
            HH = 4
            b_oret = Buf("o_retP"); out_bufs.append(b_oret)
            b_oretS = [Buf("o_retS%d" % i) for i in range(2)]; out_bufs.extend(b_oretS)
            for hh in range(2):
              with contextlib.ExitStack() as p2:
                h0 = hh * HH
                qT = sb(p2, "qT", [128, HH, NT2], BF16); b_qT = [Buf("qT%d" % h) for h in range(HH)]
                kT = sb(p2, "kT", [128, HH, NT2], BF16); b_kT = [Buf("kT%d" % h) for h in range(HH)]
                ktok = sb(p2, "ktok", [128, 8, HH, 128], BF16); b_ktok = [Buf("ktok%d" % c) for c in range(8)]
                vtok = sb(p2, "vtok", [128, 8, HH, 128], BF16); b_vtok = [Buf("vtok%d" % c) for c in range(8)]
                ztok = sb(p2, "ztok", [128, 8, HH, 128], BF16); b_ztok = [Buf("ztok%d" % c) for c in range(8)]
                ktS = sb(p2, "ktS", [16, HH, 128], BF16); vtS = sb(p2, "vtS", [16, HH, 128], BF16); ztS = sb(p2, "ztS", [128, HH, 128], BF16)
                b_ktS = Buf("ktS"); b_vtS = Buf("vtS"); b_ztS = Buf("ztS")
                S.op("pool", lambda e: e.memset(ztS[:, :, :], 0.0), [], [b_ztS])
                p3 = contextlib.ExitStack()
                rope = sb(p3, "rope2", [128, 2, NT2]); b_rope = Buf("rope2")
                S.dma(rope[:, :, :], ropeM.rearrange("a q t -> q a t"), writes=[b_rope])
                vz = [sb(p3, "vz%d" % i, [128, NT2], BF16) for i in range(2)]; b_vz = [Buf("vz%d" % i) for i in range(2)]
                tA = [sb(p3, "tA%d" % i, [128, 512]) for i in range(3)]; tB = [sb(p3, "tB%d" % i, [128, 512]) for i in range(3)]
                b_tA = [Buf("tA%d" % i) for i in range(3)]; b_tB = [Buf("tB%d" % i) for i in range(3)]
                blocks = []
                for hl in range(HH):
                    blocks += [("q", hl), ("k", hl), ("v", hl), ("z", hl)]

                def wsrc2(kind, hl):
                    base = {"q": 0, "k": 8, "v": 16, "z": 24}[kind] + (h0 + hl)
                    return w_in[base]

                tails = []
                for bi, (kind, h) in enumerate(blocks):
                    wb, bwb = wnext()
                    banks = [0, 1, 2] if bi % 2 == 0 else [3, 4, 5]
                    proj(wb, bwb, 16, hT, rb2, GR2, banks)
                    while tails:
                        tails.pop(0)()
                    s_ = bi % 2
                    if kind in ("q", "k"):
                        dst, bd = (qT, b_qT) if kind == "q" else (kT, b_kT)
                        for gi, (n0, nw) in enumerate(GR2):
                            rope_evac(banks[gi], nw, rope[:, 0, :], rope[:, 1, :], n0, b_rope, dst[:, h, n0:n0 + nw], bd[h], tA, tB, b_tA, b_tB, gi)
                        if kind == "k":
                            def tail(h=h):
                                for c in range(8):
                                    S.op("pe", lambda e: e.transpose(out=psb(6)[:, c * 128:(c + 1) * 128], in_=kT[:, h, c * 128:(c + 1) * 128],
                                                                     identity=identb[:, :]), [b_kT[h], b_id], [PB[6]])
                                S.ts("dve", ktok[:, :, h, :], psb(6).rearrange("q (c d) -> q c d", c=8), C("dkM", h0 + h, h0 + h + 1), ALU.mult,
                                     [PB[6], b_ct], b_ktok)
                                S.op("pe", lambda e: e.transpose(out=psb(7)[0:16, 0:128], in_=kT[:, h, 1024:1040], identity=identb[:, :]),
                                     [b_kT[h], b_id], [PB[7]])
                                S.ts("dve", ktS[:, h, :], psb(7)[0:16, 0:128], C("kS", 0, 1)[0:16, :], ALU.mult, [PB[7], b_ct], [b_ktS])
                            tails.append(tail)
                    else:
                        dtok, bdt, dS, bdS = (vtok, b_vtok, vtS, b_vtS) if kind == "v" else (ztok, b_ztok, ztS, b_ztS)
                        for gi, (n0, nw) in enumerate(GR2):
                            if kind == "v":
                                S.cp("act", vz[s_][:, n0:n0 + nw], PS[banks[gi]][:, 0:nw], [PB[banks[gi]]], [b_vz[s_]])
                            else:
                                S.act(vz[s_][:, n0:n0 + nw], PS[banks[gi]][:, 0:nw], AF.Silu, [PB[banks[gi]]], [b_vz[s_]])
                        def tail(s_=s_, h=h, dtok=dtok, bdt=bdt, dS=dS, bdS=bdS):
                            for c in range(8):
                                S.op("pe", lambda e: e.transpose(out=psb(6)[:, c * 128:(c + 1) * 128], in_=vz[s_][:, c * 128:(c + 1) * 128],
                                                                 identity=identb[:, :]), [b_vz[s_], b_id], [PB[6]])
                            S.cp("act", dtok[:, :, h, :], psb(6).rearrange("q (c d) -> q c d", c=8), [PB[6]], bdt)
                            S.op("pe", lambda e: e.transpose(out=psb(7)[0:16, 0:128], in_=vz[s_][:, 1024:1040], identity=identb[:, :]),
                                 [b_vz[s_], b_id], [PB[7]])
                            S.cp("act", dS[0:16, h, :], psb(7)[0:16, 0:128], [PB[7]], [bdS])
                        tails.append(tail)

                while tails:
                    tails.pop(0)()
                S.barrier()
                p3.close()
                checkpoint(-20)
                Sh = Sst[:, h0:h0 + HH, :]
                Sbf = sb(p2, "Sbf", [128, HH, 128], BF16); b_Sbf = Buf("Sbf")
                sc = sb(p2, "sc", [128, HH, 128], BF16); b_sc = Buf("sc")
                sq = sb(p2, "sq", [128, HH, 128]); b_sq = Buf("sq")
                ob = sb(p2, "ob", [128, HH, 128]); b_ob = Buf("ob")
                ob2 = sb(p2, "ob2", [128, HH, 128]); b_ob2 = Buf("ob2")
                yb16 = sb(p2, "yb16", [128, HH * 128], BF16); b_yb16 = Buf("yb16")
                st8 = sb(p2, "st8", [128, 6, HH]); b_st8 = Buf("st8")
                S.cp("act", Sbf[:, :, :], Sh, [b_Sst], [b_Sbf])

                def group_norm_gate(npart, obank, dq_ap, z_ap, b_z, out16, b_out, via_sb=False):
                    P_ = slice(0, npart)
                    S.force_same = True
                    o3 = PS[obank][P_, :].rearrange("q (h v) -> q h v", h=HH)
                    if via_sb:
                        S.cp("act", ob2[P_, :, :], o3, [PB[obank]], [b_ob2])
                        o3 = ob2[P_, :, :]
                        PBo = b_ob2
                    else:
                        PBo = PB[obank]
                    S.act(sq[P_, :, :], o3, AF.Square, [PBo], [b_sq])
                    S.op("dve", lambda e: e.tensor_reduce(out=st8[P_, 0, :], in_=o3, axis=AX.X, op=ALU.add), [PBo], [b_st8])
                    S.op("dve", lambda e: e.tensor_reduce(out=st8[P_, 1, :], in_=sq[P_, :, :], axis=AX.X, op=ALU.add), [b_sq], [b_st8])
                    S.ts("dve", st8[P_, 0, :], st8[P_, 0, :], 1.0 / 128, ALU.mult, [b_st8], [b_st8])
                    S.ts("dve", st8[P_, 1, :], st8[P_, 1, :], 1.0 / 128, ALU.mult, [b_st8], [b_st8])
                    S.tt("dve", st8[P_, 2, :], st8[P_, 0, :], st8[P_, 0, :], ALU.mult, [b_st8], [b_st8])
                    S.tt("dve", st8[P_, 1, :], st8[P_, 1, :], st8[P_, 2, :], ALU.subtract, [b_st8], [b_st8])
                    S.tt("dve", st8[P_, 2, :], dq_ap, dq_ap, ALU.mult, [b_ct], [b_st8])
                    S.tt("dve", st8[P_, 1, :], st8[P_, 1, :], st8[P_, 2, :], ALU.mult, [b_st8], [b_st8])
                    S.ts("dve", st8[P_, 1, :], st8[P_, 1, :], EPS, ALU.add, [b_st8], [b_st8])
                    S.act(st8[P_, 1, :], st8[P_, 1, :], AF.Sqrt, [b_st8], [b_st8])
                    S.op("dve", lambda e: e.reciprocal(out=st8[P_, 1, :], in_=st8[P_, 1, :]), [b_st8], [b_st8])
                    S.tt("dve", st8[P_, 1, :], st8[P_, 1, :], dq_ap, ALU.mult, [b_st8, b_ct], [b_st8])
                    S.tt("dve", ob[P_, :, :], o3, st8[P_, 0, :].unsqueeze(2).broadcast_to([npart, HH, 128]), ALU.subtract,
                         [PBo, b_st8], [b_ob])
                    S.tt("dve", ob[P_, :, :], ob[P_, :, :], st8[P_, 1, :].unsqueeze(2).broadcast_to([npart, HH, 128]), ALU.mult, [b_ob, b_st8], [b_ob])
                    S.tt("dve", out16, ob[P_, :, :].rearrange("q h v -> q (h v)"), z_ap, ALU.mult, [b_ob, b_z], [b_out])
                    S.force_same = False

                NSL = 4
                sold = [sb(p2, "sold%d" % i, [128, HH, 128]) for i in range(NSL)]; b_sold = [Buf("sold%d" % i) for i in range(NSL)]
                snb = [sb(p2, "snb%d" % i, [128, HH, 128], BF16) for i in range(NSL)]; b_snb = [Buf("snb%d" % i) for i in range(NSL)]
                vmb = [sb(p2, "vmb%d" % i, [16, HH * 128], BF16) for i in range(NSL)]; b_vmb = [Buf("vmb%d" % i) for i in range(NSL)]
                b_oretS4 = [Buf("o_retS4_%d" % i) for i in range(NSL)]; out_bufs.extend(b_oretS4)
                def chunk_gen():
                    for c in range(8):
                        cs = slice(c * 128, (c + 1) * 128)
                        bS, bO, bT = c % 2, 2 + c % 2, 4
                        for h in range(HH):
                            S.op("pe", lambda e: e.matmul(PS[bS][:, h * 128:(h + 1) * 128], lhsT=kT[:, h, cs], rhs=qT[:, h, cs], start=True, stop=True),
                                 [b_kT[h], b_qT[h]], [PB[bS]])
                        S.tt("dve", sc[:, :, :].rearrange("q h k -> q (h k)"), PS[bS][:, :],
                             C("maskR", h0 * 128, (h0 + HH) * 128), ALU.mult, [PB[bS], b_ct], [b_sc])
                        for h in range(HH):
                            osl = PS[bO][:, h * 128:(h + 1) * 128]
                            rd = [b_sc, b_vtok[c], b_qT[h], b_Sbf]
                            S.op("pe", lambda e: e.matmul(osl, lhsT=sc[:, h, :], rhs=vtok[:, c, h, :], start=True, stop=False), rd, [PB[bO]], inc=False)
                            S.op("pe", lambda e: e.matmul(osl, lhsT=qT[:, h, cs], rhs=Sbf[:, h, :], start=False, stop=True), rd, [PB[bO]])
                        for h in range(HH):
                            S.op("pe", lambda e: e.matmul(PS[bT][:, h * 128:(h + 1) * 128], lhsT=ktok[:, c, h, :], rhs=vtok[:, c, h, :], start=True, stop=True),
                                 [b_ktok[c], b_vtok[c]], [PB[bT]])
                        S.tt("dve", Sh, Sh, C("gC", h0, h0 + HH).unsqueeze(2).broadcast_to([128, HH, 128]), ALU.mult, [b_Sst, b_ct], [b_Sst])
                        S.tt("dve", Sh, Sh, PS[bT][:, :].rearrange("q (h v) -> q h v", h=HH), ALU.add, [b_Sst, PB[bT]], [b_Sst])
                        S.cp("act", Sbf[:, :, :], Sh, [b_Sst], [b_Sbf])
                        group_norm_gate(128, bO, C("dqM", h0, h0 + HH), ztok[:, c, :, :].rearrange("q h v -> q (h v)"), b_ztok[c], yb16[:, :], b_yb16)
                        for f in range(HH):
                            S.op("pe", lambda e: e.transpose(out=psb(6)[:, f * 128:(f + 1) * 128], in_=yb16[:, f * 128:(f + 1) * 128], identity=identb[:, :]),
                                 [b_yb16, b_id], [PB[6]])
                        S.cp("act", yainT[:, h0:h0 + HH, cs], psb(6)[:, 0:HH * 128].rearrange("q (f t) -> q f t", f=HH), [PB[6]], [b_yain[c]])
                        yield
                def sample_gen():
                    for b in range(16):
                        s_ = b % NSL
                        kvb = 7
                        if b == 0:
                            for b2 in range(NSL - 1):
                                S.dma(sold[b2][:, :, :], sret[b2, h0:h0 + HH].rearrange("h d v -> d h v"), writes=[b_sold[b2]])
                        if b + NSL - 1 < 16:
                            S.dma(sold[(b + NSL - 1) % NSL][:, :, :], sret[b + NSL - 1, h0:h0 + HH].rearrange("h d v -> d h v"),
                                  writes=[b_sold[(b + NSL - 1) % NSL]])
                        S.ts("dve", vmb[s_][:, :], vtS[0:16, :, :].rearrange("q h v -> q (h v)"), C("oh16", b, b + 1)[0:16, :], ALU.mult, [b_vtS, b_ct], [b_vmb[s_]])
                        for h in range(HH):
                            S.op("pe", lambda e: e.matmul(PS[kvb][:, h * 128:(h + 1) * 128], lhsT=ktS[:, h, :], rhs=vmb[s_][:, h * 128:(h + 1) * 128],
                                                          start=True, stop=True), [b_ktS, b_vmb[s_]], [PB[kvb]])
                        S.tt("dve", sold[s_][:, :, :], sold[s_][:, :, :], C("gam1", h0, h0 + HH).unsqueeze(2).broadcast_to([128, HH, 128]), ALU.mult, [b_sold[s_], b_ct], [b_sold[s_]])
                        S.tt("dve", sold[s_][:, :, :], sold[s_][:, :, :], PS[kvb][:, :].rearrange("q (h v) -> q h v", h=HH), ALU.add, [b_sold[s_], PB[kvb]], [b_sold[s_]])
                        S.cp("act", snb[s_][:, :, :], sold[s_][:, :, :], [b_sold[s_]], [b_snb[s_]])
                        S.dma(retS[b, h0:h0 + HH].rearrange("h d v -> d h v"), sold[s_][:, :, :], reads=[b_sold[s_]], writes=[b_oretS4[s_]], eng="act")
                        for h in range(HH):
                            S.op("pe", lambda e: e.matmul(PS[5][:, h * 16 + b:h * 16 + b + 1], lhsT=snb[s_][:, h, :], rhs=qT[:, h, 1024 + b:1025 + b],
                                                          start=True, stop=True), [b_qT[h], b_snb[s_]], [PB[5]])
                        yield

                cg, sg_ = chunk_gen(), sample_gen()
                for _ in range(8):
                    next(cg)
                    next(sg_)
                    next(sg_)
                S.dma(retP[h0:h0 + HH].rearrange("h d v -> d h v"), Sh, reads=[b_Sst], writes=[b_oret], acc=True, eng="act")
                checkpoint(-22)
                checkpoint(-22)
                oTs = sb(p2, "oTs", [128, HH * 16]); b_oTs = Buf("oTs")
                S.cp("act", oTs[:, :], PS[5][:, 0:HH * 16], [PB[5]], [b_oTs])
                for h in range(HH):
                    S.op("pe", lambda e: e.transpose(out=PS[3][0:16, h * 128:(h + 1) * 128], in_=oTs[:, h * 16:(h + 1) * 16], identity=ident),
                         [b_oTs, b_ct], [PB[3]])
                checkpoint(-24)
                group_norm_gate(128, 3, C("one", 0, HH), ztS[:, :, :].rearrange("q h v -> q (h v)"), b_ztS, yb16[:, :], b_yb16, via_sb=True)
                checkpoint(-25)
                for f in range(HH):
                    S.op("pe", lambda e: e.transpose(out=psb(6)[:, f * 16:(f + 1) * 16], in_=yb16[0:16, f * 128:(f + 1) * 128], identity=identb[0:16, 0:16]),
                         [b_yb16, b_id], [PB[6]])
                S.cp("act", yainT[:, h0:h0 + HH, 1024:1040], psb(6)[:, 0:HH * 16].rearrange("q (f t) -> q f t", f=HH), [PB[6]], [b_yain[8]])
                S.barrier()
                checkpoint(4 + hh)

            S.phase = "s5"
            NGB = 16
            with contextlib.ExitStack() as p2:
                with contextlib.ExitStack() as p3:
                    U2 = [sb(p3, "U2_%d" % i, [128, 8, 128], BF16) for i in range(2)]; b_U2 = [Buf("U2_%d" % i) for i in range(2)]
                    uS = [sb(p3, "uS%d" % i, [128, 16], BF16) for i in range(2)]; b_uS = [Buf("uS%d" % i) for i in range(2)]
                    for f in range(8):
                        wb, bwb = wnext()
                        banks = [0, 1, 2] if f % 2 == 0 else [3, 4, 5]
                        proj(wb, bwb, 16, hT, rb2, GR2, banks)
                        s_ = f % 2
                        for gi in range(2):
                            S.cp("act", U2[s_][:, :, gi * 64:(gi + 1) * 64],
                                 PS[banks[gi]][:, 0:512].rearrange("q (j s) -> q s j", s=8), [PB[banks[gi]]], [b_U2[s_]])
                        S.cp("act", uS[s_][:, :], PS[banks[2]][:, 0:16], [PB[banks[2]]], [b_uS[s_]])
                        for gl in range(8):
                            S.dma(DU[f * 8 + gl].rearrange("s m j -> m s j"), U2[s_][gl * 16:(gl + 1) * 16, :, :],
                                  reads=[b_U2[s_]], writes=[b_DU], acc=True, sem_of=b_U2[s_], eng="act")
                            S.dma(DUS[f * 8 + gl], uS[s_][gl * 16:(gl + 1) * 16, :], reads=[b_uS[s_]], writes=[b_DUS], acc=True, sem_of=b_uS[s_], eng="act")
                    S.barrier()
                    checkpoint(6)
                xold = sb(p2, "xold", [64, 2, G, 16]); b_xold = Buf("xold")
                xnew = sb(p2, "xnew", [64, 2, G, 16]); b_xnew = Buf("xnew")
                Xfin = sb(p2, "Xfin", [64, 2, G]); b_Xfin = Buf("Xfin")
                with contextlib.ExitStack() as p3:
                    snat = sb(p3, "snat", [128, 2, 8, 64]); b_snat = Buf("snat")
                    S.dma(snat[:, 0, :, :], s5re.rearrange("(t b2) g p -> (b2 g) t p", b2=2), writes=[b_snat], acc=True)
                    S.dma(snat[:, 1, :, :], s5im.rearrange("(t b2) g p -> (b2 g) t p", b2=2), writes=[b_snat], acc=True)
                    for c in range(2):
                        for t in range(8):
                            bk = 6 + (t % 2)
                            S.op("pe", lambda e: e.transpose(out=PS[bk][0:64, 0:128], in_=snat[:, c, t, :], identity=ident), [b_snat, b_ct], [PB[bk]])
                            S.cp("act", xold[:, c, :, 2 * t:2 * t + 2].rearrange("q g b -> q b g"), PS[bk][0:64, 0:128].rearrange("q (b g) -> q b g", b=2),
                                 [PB[bk]], [b_xold])
                    S.barrier()
                    checkpoint(7)
                Ap2, b_Ap2 = build_apow(p2)
                bkt = sb(p2, "bkt", [64, 2, 16, 64]); bkw = sb(p2, "bkw", [64, 2, 16, 64]); b_bkt = Buf("bkt"); b_bkw = Buf("bkw")
                WSL = AR_WSLOT[0]
                for hf in range(2):
                    G0 = hf * 32
                    BXv = arena[0:64, WSL:WSL + 2 * 32 * 129].rearrange("q (c g j) -> q c g j", c=2, g=32)
                    b_BX = Buf("BXh")
                    with contextlib.ExitStack() as p3:
                        BSh = sb(p3, "BSh", [64, 2, 32, 16]); b_BS = Buf("BSh")
                        u = sb(p3, "scu", [64, 2, 32]); w = sb(p3, "scw", [64, 2, 32]); b_u = Buf("scu"); b_w2 = Buf("scw")
                        S.cp("dve", BXv[:, :, :, 0], Xmid[:, :, G0:G0 + 32], [b_Xmid], [b_BX])
                        for q2 in range(2):
                            g0 = G0 + q2 * NGB
                            gs = slice(g0, g0 + NGB)
                            with contextlib.ExitStack() as p4:
                                U = sb(p4, "U", [128, NGB, 144], BF16); b_U = Buf("U")
                                GTs = sb(p4, "GTs", [128, NGB, 128], BF16); b_GTs = Buf("GTs")
                                S.op("pool", lambda e: e.memset(U[:, :, 128:144], 0.0), [], [b_U])
                                S.dma(U[:, :, 0:128], DU[gs].rearrange("g s m j -> (s m) g j"), reads=[b_DU], writes=[b_U])
                                S.dma(U[0:16, :, 128:144], DUS[gs].rearrange("g m b -> m g b"), reads=[b_DUS], writes=[b_U])
                                S.dma(GTs[:, :, :], DGT[gs].rearrange("g q c -> q g c"), reads=[b_DGT], writes=[b_GTs])
                                for gl in range(NGB):
                                    bk = gl % 4
                                    for c in range(2):
                                        S.op("pe", lambda e: e.matmul(PS[bk][0:64, c * 144:(c + 1) * 144], lhsT=GTs[:, gl, c * 64:(c + 1) * 64], rhs=U[:, gl, :],
                                                                      start=True, stop=True), [b_GTs, b_U], [PB[bk]])
                                    pv = PS[bk][0:64, 0:288].rearrange("q (c j) -> q c j", c=2)
                                    S.cp("act", BXv[:, :, q2 * NGB + gl, 1:129], pv[:, :, 0:128], [PB[bk]], [b_BX])
                                    S.cp("act", BSh[:, :, q2 * NGB + gl, :], pv[:, :, 128:144], [PB[bk]], [b_BS])
                                S.barrier()
                        bk_scan(BXv, b_BX, G0, Ap2, b_Ap2, bkt, bkw, b_bkt, b_bkw)
                        S.cp("dve", Xfin[:, :, G0:G0 + 32], BXv[:, :, :, 128], [b_BX], [b_Xfin])
                        with contextlib.ExitStack() as p4:
                            t1 = sb(p4, "st1", [64, 2, 32, 16]); t2 = sb(p4, "st2", [64, 2, 32, 16]); b_t1 = Buf("st1"); b_t2 = Buf("st2")
                            hs = slice(G0, G0 + 32)
                            xo = xold[:, :, hs, :]; xn = xnew[:, :, hs, :]

                            def cm4(outv, coef, src, b_src, first):
                                cr2 = coef[:, 0:1, hs].unsqueeze(3).broadcast_to([64, 2, 32, 16])
                                S.tt("dve", t1[:, :, :, :], src, cr2, ALU.mult, [b_src, b_A8], [b_t1])
                                S.tt("dve", t2[:, 0, :, :], src[:, 1, :, :], coef[:, 2, hs].unsqueeze(2).broadcast_to([64, 32, 16]), ALU.mult, [b_src, b_A8], [b_t2])
                                S.tt("dve", t2[:, 1, :, :], src[:, 0, :, :], coef[:, 1, hs].unsqueeze(2).broadcast_to([64, 32, 16]), ALU.mult, [b_src, b_A8], [b_t2])
                                S.tt("dve", t1[:, :, :, :], t1[:, :, :, :], t2[:, :, :, :], ALU.add, [b_t1, b_t2], [b_t1])
                                if first:
                                    S.cp("dve", outv, t1[:, :, :, :], [b_t1], [b_xnew])
                                else:
                                    S.tt("dve", outv, outv, t1[:, :, :, :], ALU.add, [b_t1, b_xnew], [b_xnew])

                            cm4(xn, A1, xo, b_xold, True)
                            cm4(xn, Am7, BSh[:, :, :, :], b_BS, False)
                            S.barrier()
                        for q2 in range(2):
                            g0 = G0 + q2 * NGB
                            gs = slice(g0, g0 + NGB)
                            qs = slice(q2 * NGB, (q2 + 1) * NGB)
                            with contextlib.ExitStack() as p4:
                                U = sb(p4, "U", [128, NGB, 144], BF16); b_U = Buf("U")
                                Ts = sb(p4, "Ts", [128, NGB, 128], BF16); Rs = sb(p4, "Rs", [128, NGB, 144], BF16); b_Ts = Buf("Ts"); b_Rs = Buf("Rs")
                                Xc = sb(p4, "Xc", [128, NGB, 144], BF16); b_Xc = Buf("Xc")
                                ysb = sb(p4, "ysb", [128, NGB, 144], BF16); b_ysb = Buf("ysb")
                                ge1s = [sb(p4, "ge1_%d" % i, [128, 432]) for i in range(2)]; ge2s = [sb(p4, "ge2_%d" % i, [128, 432]) for i in range(2)]
                                b_ge1s = [Buf("ge1_%d" % i) for i in range(2)]; b_ge2s = [Buf("ge2_%d" % i) for i in range(2)]
                                S.op("pool", lambda e: e.memset(U[:, :, 128:144], 0.0), [], [b_U])
                                S.dma(U[:, :, 0:128], DU[gs].rearrange("g s m j -> (s m) g j"), reads=[b_DU], writes=[b_U])
                                S.dma(U[0:16, :, 128:144], DUS[gs].rearrange("g m b -> m g b"), reads=[b_DUS], writes=[b_U])
                                S.dma(Ts[:, :, :], DTm[gs].rearrange("g q c -> q g c"), reads=[b_DT], writes=[b_Ts])
                                S.dma(Rs[:, :, :], DR[gs].rearrange("g q c -> q g c"), reads=[b_DR], writes=[b_Rs])
                                S.cp("act", Xc[0:64, :, 0:128], BXv[:, 0, qs, 0:128], [b_BX], [b_Xc])
                                S.cp("dve", Xc[64:128, :, 0:128], BXv[:, 1, qs, 0:128], [b_BX], [b_Xc])
                                S.cp("act", Xc[0:64, :, 128:144], xold[:, 0, gs, :], [b_xold], [b_Xc])
                                S.cp("dve", Xc[64:128, :, 128:144], xold[:, 1, gs, :], [b_xold], [b_Xc])
                                for g3 in range((NGB + 2) // 3):
                                    bk = 4 + (g3 % 2)
                                    gls = list(range(g3 * 3, min(g3 * 3 + 3, NGB)))
                                    rd = [b_Ts, b_U, b_Rs, b_Xc]
                                    for i3, gl in enumerate(gls):
                                        osl = PS[bk][:, i3 * 144:(i3 + 1) * 144]
                                        S.op("pe", lambda e: e.matmul(osl, lhsT=Ts[:, gl, :], rhs=U[:, gl, :], start=True, stop=False), rd, [PB[bk]], inc=False)
                                        S.op("pe", lambda e: e.matmul(osl, lhsT=Rs[:, gl, 16:144], rhs=Xc[:, gl, :], start=False, stop=True), rd, [PB[bk]])
                                    nn = len(gls) * 144
                                    ge1, ge2, b_ge1, b_ge2 = ge1s[g3 % 2], ge2s[g3 % 2], b_ge1s[g3 % 2], b_ge2s[g3 % 2]
                                    S.act(ge1[:, 0:nn], PS[bk][:, 0:nn], AF.Square, [PB[bk]], [b_ge1])
                                    S.ts("dve", ge1[:, 0:nn], ge1[:, 0:nn], 0.044715, ALU.mult, [b_ge1], [b_ge1], s2=1.0, op1=ALU.add)
                                    S.tt("dve", ge1[:, 0:nn], ge1[:, 0:nn], PS[bk][:, 0:nn], ALU.mult, [b_ge1, PB[bk]], [b_ge1])
                                    S.act(ge2[:, 0:nn], ge1[:, 0:nn], AF.Sigmoid, [b_ge1], [b_ge2], scale=1.5957691216057308)
                                    S.tt("dve", ysb[:, gls[0]:gls[0] + len(gls), :].rearrange("q g j -> q (g j)"), ge2[:, 0:nn], PS[bk][:, 0:nn], ALU.mult,
                                         [b_ge2, PB[bk]], [b_ysb])
                                for s_i in range(8):
                                    S.dma(DY[gs, :, s_i, :].rearrange("g n j -> n g j"), ysb[s_i * 16:(s_i + 1) * 16, :, 0:128], reads=[b_ysb], writes=[b_DY], acc=True, eng="act")
                                S.dma(DYS[gs].rearrange("g n b -> n g b"), ysb[0:16, :, 128:144], reads=[b_ysb], writes=[b_DYS], acc=True, eng="act")
                                S.barrier()
                        checkpoint(8 + hf)
                with contextlib.ExitStack() as p3:
                    xo_t = sb(p3, "xo_t", [128, 2, 8, 64]); b_xot = Buf("xo_t")
                    xf_t = sb(p3, "xf_t", [64, 2, 64]); b_xft = Buf("xf_t")
                    xn2 = sb(p3, "xn2", [64, 2, 8, 2, G]); b_xn2 = Buf("xn2")
                    for c in range(2):
                        S.cp("dve", xn2[:, c, :, :, :].rearrange("q t b g -> q (t b) g"), xnew[:, c, :, :].rearrange("q g b -> q b g"), [b_xnew], [b_xn2])
                        for t in range(8):
                            bk = 6 + (t % 2)
                            S.op("pe", lambda e: e.transpose(out=PS[bk][:, 0:64], in_=xn2[:, c, t, :, :].rearrange("q b g -> q (b g)"), identity=ident[0:64, 0:64]),
                                 [b_xn2, b_ct], [PB[bk]])
                            S.cp("act", xo_t[:, c, t, :], PS[bk][:, 0:64], [PB[bk]], [b_xot])
                        S.op("pe", lambda e: e.transpose(out=PS[6][0:64, 64:128], in_=Xfin[:, c, :], identity=ident[0:64, 0:64]), [b_Xfin, b_ct], [PB[6]])
                        S.cp("act", xf_t[:, c, :], PS[6][0:64, 64:128], [PB[6]], [b_xft])
                    for nm, dst, c in (("s5reS", s5reS, 0), ("s5imS", s5imS, 1)):
                        bo = Buf("o_" + nm); out_bufs.append(bo)
                        S.dma(dst.rearrange("(t b2) g p -> (b2 g) t p", b2=2), xo_t[:, c, :, :], reads=[b_xot], writes=[bo])
                    for nm, dst, c in (("s5reP", s5reP, 0), ("s5imP", s5imP, 1)):
                        bo = Buf("o_" + nm); out_bufs.append(bo)
                        S.dma(dst, xf_t[:, c, :], reads=[b_xft], writes=[bo])
                    S.barrier()
                    checkpoint(12)

            S.phase = "glu"
            ybinT = sb(ph, "ybinT", [128, 8, NT2], BF16); b_ybin = [Buf("ybin%d" % i) for i in range(8)]
            with contextlib.ExitStack() as p2:
                yT = sb(p2, "yT", [128, 8, NT2], BF16); b_yT = Buf("yT")
                S.dma(yT[:, :, 0:1024], DY.rearrange("(f gl) n s j -> (gl n) f (s j)", gl=8), reads=[b_DY], writes=[b_yT], acc=True)
                S.dma(yT[:, :, 1024:1040], DYS.rearrange("(f gl) n b -> (gl n) f b", gl=8), reads=[b_DYS], writes=[b_yT], acc=True)
                bglu = sb(p2, "bglu", [128, 8]); b_bglu = Buf("bglu")
                S.dma(bglu[:, :], b_glu.rearrange("(f q) -> q f", q=128), writes=[b_bglu], allow_slow_non_contiguous=True)
                sg = [sb(p2, "sg%d" % i, [128, NT2], BF16) for i in range(2)]; b_sg = [Buf("sg%d" % i) for i in range(2)]
                zs = [sb(p2, "zs%d" % i, [128, NT2], BF16) for i in range(2)]; b_zs = [Buf("zs%d" % i) for i in range(2)]
                rbY = [[b_yT], [b_yT], [b_yT]]
                for f in range(8):
                    s_ = f % 2
                    wb, bwb = wnext()
                    proj(wb, bwb, 8, yT, rbY, GR2, [0, 1, 2])
                    for gi, (n0, nw) in enumerate(GR2):
                        S.act(sg[s_][:, n0:n0 + nw], PS[gi][:, 0:nw], AF.Sigmoid, [PB[gi], b_bglu], [b_sg[s_]], bias=bglu[:, f:f + 1])
                    S.tt("dve", sg[s_][:, :], sg[s_][:, :], yT[:, f, :], ALU.mult, [b_sg[s_], b_yT], [b_sg[s_]])
                    wb, bwb = wnext()
                    proj(wb, bwb, 16, hT, rb2, GR2, [3, 4, 5])
                    for gi, (n0, nw) in enumerate(GR2):
                        S.act(zs[s_][:, n0:n0 + nw], PS[3 + gi][:, 0:nw], AF.Silu, [PB[3 + gi]], [b_zs[s_]])
                    S.tt("dve", ybinT[:, f, 0:1024].rearrange("q (s j) -> q s j", s=8), sg[s_][:, 0:1024].rearrange("q (s j) -> q s j", s=8),
                         zs[s_][:, 0:1024].rearrange("q (j s) -> q s j", s=8), ALU.mult, [b_sg[s_], b_zs[s_]], [b_ybin[f]])
                    S.tt("dve", ybinT[:, f, 1024:1040], sg[s_][:, 1024:1040], zs[s_][:, 1024:1040], ALU.mult, [b_sg[s_], b_zs[s_]], [b_ybin[f]])
                S.barrier()
                checkpoint(13)

            S.phase = "mrg"
            mergedT = sb_top("mergedT", [128, 16, NT2], BF16)
            with contextlib.ExitStack() as p2:
                sga = sb(p2, "sga", [128, NT2]); sgb = sb(p2, "sgb", [128, NT2]); m1 = sb(p2, "m1", [128, NT2]); m2 = sb(p2, "m2", [128, NT2])
                b_sga = Buf("sga"); b_sgb = Buf("sgb"); b_m1 = Buf("m1"); b_m2 = Buf("m2")
                rbA = [list(b_yain[0:4]), list(b_yain[4:8]), [b_yain[8]]]
                rbB = [list(b_ybin)] * 3

                def srcs(dti):
                    return [(w_in[48 + dti], 16), (w_pa[dti], 8), (w_in[64 + dti], 16), (w_pb[dti], 8)]

                nxt = wnext

                for dti in range(16):
                    wb, bwb = nxt()
                    proj(wb, bwb, 16, hT, rb2, GR2, [0, 1, 2])
                    for gi, (n0, nw) in enumerate(GR2):
                        S.act(sga[:, n0:n0 + nw], PS[gi][:, 0:nw], AF.Sigmoid, [PB[gi]], [b_sga])
                    wb, bwb = nxt()
                    proj(wb, bwb, 8, yainT, rbA, GR2, [3, 4, 5])
                    for gi, (n0, nw) in enumerate(GR2):
                        S.tt("dve", m1[:, n0:n0 + nw], sga[:, n0:n0 + nw], PS[3 + gi][:, 0:nw], ALU.mult, [b_sga, PB[3 + gi]], [b_m1])
                    wb, bwb = nxt()
                    proj(wb, bwb, 16, hT, rb2, GR2, [0, 1, 2])
                    for gi, (n0, nw) in enumerate(GR2):
                        S.act(sgb[:, n0:n0 + nw], PS[gi][:, 0:nw], AF.Sigmoid, [PB[gi]], [b_sgb])
                    wb, bwb = nxt()
                    proj(wb, bwb, 8, ybinT, rbB, GR2, [3, 4, 5])
                    for gi in range(2):
                        ssl = slice(gi * 4, gi * 4 + 4)
                        S.tt("dve", m2[:, gi * 512:(gi + 1) * 512].rearrange("q (s j) -> q s j", s=4),
                             sgb[:, 0:1024].rearrange("q (j s) -> q s j", s=8)[:, ssl, :],
                             PS[3 + gi][:, 0:512].rearrange("q (s j) -> q s j", s=4), ALU.mult, [b_sgb, PB[3 + gi]], [b_m2])
                    S.tt("dve", m2[:, 1024:1040], sgb[:, 1024:1040], PS[5][:, 0:16], ALU.mult, [b_sgb, PB[5]], [b_m2])
                    S.tt("dve", mergedT[:, dti, 0:1024].rearrange("q (j s) -> q s j", s=8), m1[:, 0:1024].rearrange("q (j s) -> q s j", s=8),
                         m2[:, 0:1024].rearrange("q (s j) -> q s j", s=8), ALU.add, [b_m1, b_m2], [b_mrg[dti]])
                    S.tt("dve", mergedT[:, dti, 1024:1040], m1[:, 1024:1040], m2[:, 1024:1040], ALU.add, [b_m1, b_m2], [b_mrg[dti]])
                S.barrier()
                checkpoint(14)

        S.phase = "fin"
        with contextlib.ExitStack() as p2:
            wo = sb(p2, "wo", [128, 16, D], BF16); b_wo = [Buf("wo%d" % i) for i in range(16)]
            load_grep(p2, g_post)
            for i in range(16):
                s_ = i % 2
                S.dma(wst[s_][:, :, :], w_out[i], writes=[b_wst[s_]])
                S.cp("pool" if i % 2 == 0 else "act", wo[:, :, i * 128:(i + 1) * 128], wst[s_][:, :, :], [b_wst[s_]], [b_wo[i]])
            xr = [sb(p2, "xr%d" % i, [128, D]) for i in range(2)]; b_xr = [Buf("xr%d" % i) for i in range(2)]
            yo = [sb(p2, "yo%d" % i, [128, 512]) for i in range(2)]; b_yo = [Buf("yo%d" % i) for i in range(2)]
            junk = sb(p2, "junk2", [128, 512], BF16); b_junk = Buf("junk2")
            ss = sb(p2, "ss2", [128, 9, 5]); b_ss = Buf("ss2")
            S.op("pool", lambda e: e.memset(ss[:, :, :], 0.0), [], [b_ss])
            b_oy = [Buf("o_y%d" % i) for i in range(2)]; out_bufs.extend(b_oy)
            for tt_ in range(9):
                s_ = tt_ % 2
                np_ = 128 if tt_ < 8 else 16
                P_ = slice(0, np_)
                tsl = slice(tt_ * 128, tt_ * 128 + np_)
                S.dma(xr[s_][P_, :], xM[tt_ * 128:(tt_ + 1) * 128, :] if tt_ < 8 else xS, writes=[b_xr[s_]])
                for cg in range(4):
                    bk = (tt_ % 2) * 4 + cg
                    mm_group(bk, 512, [(mergedT[:, kt, tsl], wo[:, kt, cg * 512:(cg + 1) * 512]) for kt in range(16)],
                             list(b_mrg) + b_wo[cg * 4:(cg + 1) * 4], m=np_)
                    S.act(junk[P_, :], PS[bk][P_, :], AF.Square, [PB[bk]], [b_junk, b_ss], accum_out=ss[P_, tt_, cg:cg + 1])
                S.op("dve", lambda e: e.tensor_reduce(out=ss[P_, tt_, 4:5], in_=ss[P_, tt_, 0:4], axis=AX.X, op=ALU.add), [b_ss], [b_ss])
                S.ts("dve", ss[P_, tt_, 4:5], ss[P_, tt_, 4:5], 1.0 / D, ALU.mult, [b_ss], [b_ss], s2=EPS, op1=ALU.add)
                S.act(ss[P_, tt_, 4:5], ss[P_, tt_, 4:5], AF.Sqrt, [b_ss], [b_ss])
                S.op("dve", lambda e: e.reciprocal(out=ss[P_, tt_, 4:5], in_=ss[P_, tt_, 4:5]), [b_ss], [b_ss])
                for cg in range(4):
                    bk = (tt_ % 2) * 4 + cg
                    csl = slice(cg * 512, (cg + 1) * 512)
                    y_ = cg % 2
                    S.stt("dve", yo[y_][P_, :], PS[bk][P_, :], ss[P_, tt_, 4:5], GREP[0][P_, csl], ALU.mult, ALU.mult, [PB[bk], b_ss, b_grep], [b_yo[y_]])
                    S.tt("dve", xr[s_][P_, csl], xr[s_][P_, csl], yo[y_][P_, :], ALU.add, [b_yo[y_], b_xr[s_]], [b_xr[s_]])
                S.dma(yM[tt_ * 128:(tt_ + 1) * 128, :] if tt_ < 8 else yS, xr[s_][P_, :], reads=[b_xr[s_]], writes=[b_oy[s_]], eng="pool")
            S.barrier()
        S.barrier()
    except _Stop:
        pass
    return nc


_PROG = [None]


def prep_inputs(x_prompt, x_sample, state_ret, state_s5_re, state_s5_im, g_pre, w_in, w_pa, w_pb, w_out, g_post,
                s5_lam_re, s5_lam_im, s5_log_dt, s5_b_re, s5_b_im, s5_c_re, s5_c_im, s5_d, s5_w_glu, s5_b_glu):
    f = lambda a: np.ascontiguousarray(np.asarray(a, dtype=np.float32))
    x_prompt = f(x_prompt); x_sample = f(x_sample)
    def blk(w):
        K_, N_ = w.shape
        return np.ascontiguousarray(np.asarray(w, np.float32).reshape(K_ // 128, 128, N_ // 128, 128).transpose(2, 1, 0, 3))

    shared = {
        "w_in": blk(w_in[0]), "w_pa": blk(w_pa[0]), "w_pb": blk(w_pb[0]), "w_out": blk(w_out[0]), "w_glu": blk(s5_w_glu[0]),
        "g_pre": f(g_pre[0]), "g_post": f(g_post[0]), "b_glu": f(s5_b_glu[0]),
        "lam_re": f(s5_lam_re[0]), "lam_im": f(s5_lam_im[0]), "log_dt": f(s5_log_dt[0]),
        "b_re": f(s5_b_re[0]), "b_im": f(s5_b_im[0]), "c_re": f(s5_c_re[0]), "c_im": f(s5_c_im[0]), "s5d": f(s5_d[0]),
        "ctab": make_ctab(),
    }
    ropeP = make_rope(np.arange(1024))
    in_maps = []
    for c in range(8):
        b, half = c // 2, c % 2
        m = dict(shared)
        m["xM"] = f(x_prompt[b, half * 1024:(half + 1) * 1024])
        m["xP"] = f(x_prompt[b, 0:1024]) if half == 1 else np.zeros((1024, D), np.float32)
        m["xS"] = f(x_sample[c * 16:(c + 1) * 16, 0])
        m["sret"] = f(state_ret[0, c * 16:(c + 1) * 16])
        m["s5re"] = f(state_s5_re[0, c * 16:(c + 1) * 16])
        m["s5im"] = f(state_s5_im[0, c * 16:(c + 1) * 16])
        m["ropeP"] = ropeP
        m["ropeM"] = make_rope(np.concatenate([np.arange(1024) + half * 1024, np.full(16, PAST)]))
        in_maps.append(m)
    return in_maps


def assemble(R):
    y_prompt = np.stack([np.concatenate([R[2 * b]["yM"], R[2 * b + 1]["yM"]], 0) for b in range(4)]).astype(np.float32)
    y_sample = np.concatenate([R[c]["yS"] for c in range(8)], 0)[:, None, :].astype(np.float32)
    ret_p = np.stack([R[2 * b + 1]["retP"] for b in range(4)])[None].astype(np.float32)
    re_p = np.stack([R[2 * b + 1]["s5reP"] for b in range(4)])[None].astype(np.float32)
    im_p = np.stack([R[2 * b + 1]["s5imP"] for b in range(4)])[None].astype(np.float32)
    ret_s = np.concatenate([R[c]["retS"] for c in range(8)], 0)[None].astype(np.float32)
    re_s = np.concatenate([R[c]["s5reS"] for c in range(8)], 0)[None].astype(np.float32)
    im_s = np.concatenate([R[c]["s5imS"] for c in range(8)], 0)[None].astype(np.float32)
    return (y_prompt, y_sample, ret_p, re_p, im_p, ret_s, re_s, im_s)


def kernel(**inputs):
    in_maps = prep_inputs(**inputs)
    if _PROG[0] is None:
        _PROG[0] = build_program()
    res = run_bass_kernel_spmd(_PROG[0], in_maps, core_ids=list(range(8)))
    return assemble(res.results)
```

```python
import contextlib
import math
import numpy as np
import concourse.bass as bass
import concourse.mybir as mybir
from concourse.bass_utils import run_bass_kernel_spmd

F32 = mybir.dt.float32
BF16 = mybir.dt.bfloat16
ALU = mybir.AluOpType
AF = mybir.ActivationFunctionType
AX = mybir.AxisListType

D = 2048
NH = 8
G = 64
EPS = 1e-6
PAST = 16384
TWO_PI = 2.0 * math.pi


NOSAME_PHASES = {"p0", "p1", "p1s", "p2h", "ret", "s5", "glu", "mrg"}


class Buf:
    __slots__ = ("name", "w", "r", "dsem", "dcnt", "uid")
    _n = [0]

    def __init__(self, name):
        Buf._n[0] += 1
        self.uid = Buf._n[0]
        self.name = name
        self.w = []
        self.r = []
        self.dsem = None
        self.dcnt = 0


class Sched:
    ENG = ("pe", "act", "dve", "pool", "sp")

    def __init__(self, nc, stack):
        self.nc = nc
        self.stack = stack
        self.e = {"pe": nc.tensor, "act": nc.scalar, "dve": nc.vector,
                  "pool": nc.gpsimd, "sp": nc.sync}
        self.sem = {k: stack.enter_context(nc.semaphore("s_" + k)) for k in self.ENG}
        self.bsem = stack.enter_context(nc.semaphore("s_bar"))
        self.bcnt = 0
        self.cnt = {k: 0 for k in self.ENG}
        self.known = {k: {} for k in self.ENG}
        self.semobj = {("e", k): self.sem[k] for k in self.ENG}
        self.dmas = {}
        self.nsem = 0
        self.phase = ""
        self.force_same = False
        self.nosame = False
        self.waited = {}

    def _dsem(self, buf):
        if buf.dsem is None:
            buf.dsem = self.stack.enter_context(self.nc.semaphore("d%d" % self.nsem))
            self.semobj[("d", buf.uid)] = buf.dsem
            self.nsem += 1
        return buf.dsem

    def _need(self, eng, dep, waits):
        kind, key, val = dep
        k = (kind, key)
        if kind == "e" and key == eng and (eng == "pe" or self.nosame or
                                           (eng == "dve" and self.phase in NOSAME_PHASES and not self.force_same)):
            return
        if self.known[eng].get(k, 0) >= val:
            return
        waits[k] = max(waits.get(k, 0), val)

    def _deps(self, eng, reads, writes, skip_w_key=None):
        waits = {}
        for b in reads:
            for t in b.w:
                self._need(eng, t, waits)
        for b in writes:
            for t in b.w:
                if skip_w_key is not None and t[0] == "d":
                    continue
                self._need(eng, t, waits)
            for d in b.r:
                self._need(eng, d, waits)
        for k, v in waits.items():
            self.e[eng].wait_ge(self.semobj[k], v)
            self.known[eng][k] = v
            if k[0] == "d":
                self.waited[k] = max(self.waited.get(k, 0), v)

    def op(self, eng, fn, reads=(), writes=(), inc=True):
        self._deps(eng, reads, writes)
        ins = fn(self.e[eng])
        if inc:
            self.cnt[eng] += 1
            ins.then_inc(self.sem[eng], 1)
            tag = ("e", eng, self.cnt[eng])
        else:
            tag = ("e", eng, self.cnt[eng] + 1)
        for b in reads:
            b.r.append(tag)
        for b in writes:
            b.w = [tag]
            b.r = []
        return ins

    def dma(self, out_ap, in_ap, reads=(), writes=(), acc=False, eng="sp", sem_of=None, **kw):
        assert len(writes) == 1
        wb = writes[0]
        sb_ = sem_of if sem_of is not None else wb
        self._deps(eng, reads, writes, skip_w_key=(sb_.uid if acc else None))
        sem = self._dsem(sb_)
        kk = ("d", sb_.uid)
        wv = self.waited.get(kk, 0)
        if wv > self.known[eng].get(kk, 0):
            self.e[eng].wait_ge(sem, wv)
            self.known[eng][kk] = wv
        sb_.dcnt += 16
        self.e[eng].dma_start(out=out_ap, in_=in_ap, **kw).then_inc(sem, 16)
        tag = ("d", sb_.uid, sb_.dcnt)
        self.dmas[kk] = sb_.dcnt
        for b in reads:
            b.r.append(tag)
        if acc:
            wb.w = [t for t in wb.w if not (t[0] == "d" and t[1] == sb_.uid)] + [tag]
        else:
            wb.w = [tag]
            wb.r = []

    def barrier(self):
        sp = self.e["sp"]
        for k in ("pe", "act", "dve", "pool"):
            if self.cnt[k] > self.known["sp"].get(("e", k), 0):
                sp.wait_ge(self.sem[k], self.cnt[k])
        for key, cnt in self.dmas.items():
            if cnt > self.known["sp"].get(key, 0):
                sp.wait_ge(self.semobj[key], cnt)
                self.waited[key] = max(self.waited.get(key, 0), cnt)
        self.bcnt += 1
        sp.sem_inc(self.bsem, 1)
        for k in ("pe", "act", "dve", "pool"):
            self.e[k].wait_ge(self.bsem, self.bcnt)
        for k in self.ENG:
            for k2 in self.ENG:
                self.known[k][("e", k2)] = self.cnt[k2]
            for key, cnt in self.dmas.items():
                self.known[k][key] = cnt

    def tt(self, eng, out, in0, in1, op, r, w):
        return self.op(eng, lambda e: e.tensor_tensor(out=out, in0=in0, in1=in1, op=op), r, w)

    def ts(self, eng, out, in0, s1, op0, r, w, s2=None, op1=None):
        if op1 is None:
            return self.op(eng, lambda e: e.tensor_scalar(out=out, in0=in0, scalar1=s1, scalar2=None,
                                                          op0=op0), r, w)
        return self.op(eng, lambda e: e.tensor_scalar(out=out, in0=in0, scalar1=s1, scalar2=s2,
                                                      op0=op0, op1=op1), r, w)

    def stt(self, eng, out, in0, scalar, in1, op0, op1, r, w):
        return self.op(eng, lambda e: e.scalar_tensor_tensor(out=out, in0=in0, scalar=scalar, in1=in1,
                                                             op0=op0, op1=op1), r, w)

    def cp(self, eng, out, in_, r, w):
        if eng == "act":
            return self.op(eng, lambda e: e.copy(out=out, in_=in_), r, w)
        return self.op(eng, lambda e: e.tensor_copy(out=out, in_=in_), r, w)

    def act(self, out, in_, func, r, w, **kw):
        return self.op("act", lambda e: e.activation(out=out, in_=in_, func=func, **kw), r, w)


CT = {}


def _ct_layout():
    off = 0
    for name, n in (("ident", 128), ("maskR", 1024), ("dqM", 8), ("dkM", 8), ("gC", 8), ("dkP", 64),
                    ("gam1", 8), ("ohT", 256), ("cm", 128), ("sel", 128), ("kv", 9), ("pvec", 4),
                    ("oh16", 16), ("kS", 8), ("one", 8)):
        CT[name] = (off, n)
        off += n
    return off


NCT = _ct_layout()


def make_ctab():
    t = np.zeros((128, NCT), np.float64)

    def put(name, arr):
        o, n = CT[name]
        t[:, o:o + n] = np.asarray(arr, np.float64).reshape(128, n)

    gam = 1.0 - 2.0 ** (-5.0 - np.arange(NH))
    kk = np.arange(128)
    put("ident", np.eye(128))
    m = np.zeros((128, NH, 128))
    for h in range(NH):
        m[:, h, :] = (gam[h] ** (-(kk[:, None] + 1.0))) * (128 ** -0.5) * (kk[None, :] >= kk[:, None])
    put("maskR", m)
    put("dqM", gam[None, :] ** (kk[:, None] + 1.0))
    put("dkM", gam[None, :] ** (127.0 - kk[:, None]) * 128 ** -0.5)
    put("gC", np.broadcast_to(gam[None, :] ** 128.0, (128, NH)))
    dkp = np.zeros((128, 8, NH))
    for c in range(8):
        dkp[:, c, :] = gam[None, :] ** (1023.0 - (c * 128 + kk[:, None])) * 128 ** -0.5
    put("dkP", dkp)
    put("gam1", np.broadcast_to(gam[None, :], (128, NH)))
    oh = np.zeros((128, 16, 16))
    oh[:, np.arange(16), np.arange(16)] = 1.0
    put("ohT", oh)
    s_of = np.arange(128) // 16
    m_of = np.arange(128) % 16
    put("cm", (s_of[None, :] >= s_of[:, None]).astype(np.float64))
    put("sel", ((s_of[None, :] == s_of[:, None]) & (m_of[None, :] == m_of[:, None])).astype(np.float64))
    put("kv", np.broadcast_to(np.arange(9.0)[None, :], (128, 9)))
    pv = np.zeros((128, 4))
    pv[:64, 0] = 1.0
    pv[64:, 1] = 1.0
    pv[:64, 2] = -1.0
    pv[64:, 3] = -1.0
    put("pvec", pv)
    o16 = np.zeros((128, 16))
    o16[np.arange(16), np.arange(16)] = 1.0
    put("oh16", o16)
    put("kS", np.full((128, 8), 128 ** -0.5))
    put("one", np.ones((128, 8)))
    return t.astype(np.float32)


def make_rope(pos):
    inv = (np.float32(10000.0) ** (-np.arange(64, dtype=np.float32) / np.float32(64))).astype(np.float32)
    ang = (pos.astype(np.float32)[None, :] * inv[:, None]).astype(np.float32).astype(np.float64)
    c = np.cos(ang)
    s = np.sin(ang)
    cosT = np.concatenate([c, c], 0)
    sinX = np.concatenate([s, -s], 0)
    return np.stack([cosT, sinX]).astype(np.float32)


class _Stop(Exception):
    pass


def build_program(stop=None, debug=False):
    nc = bass.Bass("TRN2", target_bir_lowering=False)

    def din(name, shape, dt=F32):
        return nc.dram_tensor(name, list(shape), dt, kind="ExternalInput").ap()

    def dout(name, shape):
        return nc.dram_tensor(name, list(shape), F32, kind="ExternalOutput").ap()

    def dscr(name, shape, dt=BF16):
        return nc.dram_tensor(name, list(shape), dt, kind="ExternalOutput" if debug else "Internal").ap()

    xP = din("xP", [1024, D]); xM = din("xM", [1024, D]); xS = din("xS", [16, D])
    w_in = din("w_in", [80, 128, 16, 128]); w_pa = din("w_pa", [16, 128, 8, 128]); w_pb = din("w_pb", [16, 128, 8, 128])
    w_out = din("w_out", [16, 128, 16, 128]); w_glu = din("w_glu", [8, 128, 8, 128])
    g_pre = din("g_pre", [D]); g_post = din("g_post", [D]); b_glu = din("b_glu", [1024])
    sret = din("sret", [16, NH, 128, 128]); s5re = din("s5re", [16, G, 64]); s5im = din("s5im", [16, G, 64])
    lam_re = din("lam_re", [G, 64]); lam_im = din("lam_im", [G, 64]); log_dt = din("log_dt", [G])
    b_re = din("b_re", [G, 64, 16]); b_im = din("b_im", [G, 64, 16])
    c_re = din("c_re", [G, 16, 64]); c_im = din("c_im", [G, 16, 64]); s5d = din("s5d", [G, 16])
    ctab = din("ctab", [128, NCT]); ropeP = din("ropeP", [2, 128, 1024]); ropeM = din("ropeM", [2, 128, 1040])

    yM = dout("yM", [1024, D]); yS = dout("yS", [16, D]); retP = dout("retP", [NH, 128, 128])
    s5reP = dout("s5reP", [G, 64]); s5imP = dout("s5imP", [G, 64])
    retS = dout("retS", [16, NH, 128, 128]); s5reS = dout("s5reS", [16, G, 64]); s5imS = dout("s5imS", [16, G, 64])

    DU = dscr("DU", [G, 8, 16, 128]); DUS = dscr("DUS", [G, 16, 16])
    DTm = dscr("DTm", [G, 128, 128]); DGT = dscr("DGT", [G, 128, 128]); DR = dscr("DR", [G, 128, 144])
    DY = dscr("DY", [G, 16, 8, 128]); DYS = dscr("DYS", [G, 16, 16])
    b_DU = Buf("DU"); b_DUS = Buf("DUS"); b_DT = Buf("DT"); b_DGT = Buf("DGT"); b_DR = Buf("DR")
    b_DY = Buf("DY"); b_DYS = Buf("DYS")
    out_bufs = []

    try:
      with contextlib.ExitStack() as st:
        S = Sched(nc, st)

        def checkpoint(k):
            if stop == k:
                S.barrier()
                raise _Stop()

        ARW = 49152 - 1024
        arena = st.enter_context(nc.sbuf_tensor("arena", [128, ARW], F32))
        AR = {"off": 0, "top": ARW, "peak": 0}

        def _view(off, shape, dt):
            n = int(np.prod(shape[1:]))
            words = n if dt == F32 else (n + 1) // 2
            v = arena[0:shape[0], off:off + words]
            if dt != F32:
                v = v.bitcast(dt)[:, 0:n]
            if len(shape) == 3:
                v = v.rearrange("q (a b) -> q a b", a=shape[1])
            elif len(shape) == 4:
                v = v.rearrange("q (a b c) -> q a b c", a=shape[1], b=shape[2])
            elif len(shape) == 5:
                v = v.rearrange("q (a b c d) -> q a b c d", a=shape[1], b=shape[2], c=shape[3])
            return v, (words + 7) // 8 * 8

        def sb(stack, name, shape, dt=F32):
            start = AR["off"]
            v, words = _view(start, shape, dt)
            AR["off"] = start + words
            AR["peak"] = max(AR["peak"], AR["off"])
            assert AR["off"] <= AR["top"], ("SBUF arena overflow", name, AR["off"], AR["top"])
            stack.callback(lambda: AR.__setitem__("off", min(AR["off"], start)))
            return v

        def sb_top(name, shape, dt=F32):
            n = int(np.prod(shape[1:]))
            words = ((n if dt == F32 else (n + 1) // 2) + 7) // 8 * 8
            AR["top"] -= words
            assert AR["off"] <= AR["top"], ("SBUF arena overflow (top)", name)
            return _view(AR["top"], shape, dt)[0]

        PS = [st.enter_context(nc.psum_tensor("ps%d" % i, [128, 512], F32)) for i in range(8)]
        PB = [Buf("ps%d" % i) for i in range(8)]

        def psb(i):
            return PS[i][:, :].bitcast(BF16)

        ct = sb(st, "ct", [128, NCT]); b_ct = Buf("ct")
        S.dma(ct[:, :], ctab, writes=[b_ct])

        def C(name, lo=0, hi=None):
            o, n = CT[name]
            return ct[:, o + lo:o + (n if hi is None else hi)]

        identb = sb(st, "identb", [128, 128], BF16); b_id = Buf("identb")
        S.cp("dve", identb[:, :], C("ident"), [b_ct], [b_id])
        ident = C("ident")
        Sst = sb(st, "Sst", [128, NH, 128]); b_Sst = Buf("Sst")
        Xmid = sb(st, "Xmid", [64, 2, G]); b_Xmid = Buf("Xmid")
        A8 = sb(st, "A8", [64, 4, G]); b_A8 = Buf("A8")
        A1 = sb(st, "A1", [64, 4, G])
        Am7 = sb(st, "Am7", [64, 4, G])
        GREP = [None]; b_grep = Buf("grep")
        NSLOT = 3
        AR_WSLOT = [AR["off"]]
        wst = [sb(st, "wst%d" % i, [128, 16, 128]) for i in range(NSLOT)]
        wbf = [sb(st, "wbf%d" % i, [128, 16, 128], BF16) for i in range(NSLOT)]
        b_wst = [Buf("wst%d" % i) for i in range(NSLOT)]
        b_wbf = [Buf("wbf%d" % i) for i in range(NSLOT)]
        wctr = [0]

        def wload(src, KT):
            i = wctr[0] % NSLOT
            wctr[0] += 1
            S.dma(wst[i][:, 0:KT, :], src, writes=[b_wst[i]])
            S.cp("pool" if wctr[0] % 5 == 0 else "act", wbf[i][:, 0:KT, :], wst[i][:, 0:KT, :], [b_wst[i]], [b_wbf[i]])
            return wbf[i], b_wbf[i]

        WQ = []
        for h_ in range(NH):
            WQ.append((w_in[8 + h_], 16))
        for h_ in range(NH):
            WQ.append((w_in[16 + h_], 16))
        for f_ in range(8):
            WQ.append((w_in[32 + f_], 16))
        for hh_ in range(2):
            for hl_ in range(4):
                for base_ in (0, 8, 16, 24):
                    WQ.append((w_in[base_ + hh_ * 4 + hl_], 16))
        for f_ in range(8):
            WQ.append((w_in[32 + f_], 16))
        WQ.append(None)
        for f_ in range(8):
            WQ.append((w_glu[f_], 8)); WQ.append((w_in[40 + f_], 16))
        for d_ in range(16):
            WQ += [(w_in[48 + d_], 16), (w_pa[d_], 8), (w_in[64 + d_], 16), (w_pb[d_], 8)]
        wq_pos = [0]
        wq_pend = []

        def wq_fill():
            while len(wq_pend) < 2 and wq_pos[0] < len(WQ) and WQ[wq_pos[0]] is not None:
                src_, kt_ = WQ[wq_pos[0]]
                wq_pos[0] += 1
                wq_pend.append(wload(src_, kt_) + (src_,))

        def wnext(expect=None):
            if not wq_pend:
                if wq_pos[0] < len(WQ) and WQ[wq_pos[0]] is None:
                    wq_pos[0] += 1
                wq_fill()
            wb_, bwb_, src_ = wq_pend.pop(0)
            wq_fill()
            return wb_, bwb_

        def mm_group(bank, ncols, pairs, reads, m=128):
            n = len(pairs)
            for i, (l, r) in enumerate(pairs):
                last = i == n - 1
                edge = last or i == 0
                S.op("pe", lambda e: e.matmul(PS[bank][0:m, 0:ncols], lhsT=l, rhs=r, start=(i == 0), stop=last),
                     reads if edge else (), [PB[bank]] if edge else (), inc=last)

        S.phase = "p0"
        with contextlib.ExitStack() as ph:
            lr = sb(ph, "lr", [128, G]); li = sb(ph, "li", [128, G]); ldt = sb(ph, "ldt", [128, G])
            brr = sb(ph, "brr", [128, G, 16]); bii = sb(ph, "bii", [128, G, 16])
            cnat = sb(ph, "cnat", [128, 2, 8, 2, 64])
            crT = sb(ph, "crT", [128, G, 16]); ciT = sb(ph, "ciT", [128, G, 16])
            drep = sb(ph, "drep", [128, G, 16])
            b_par = Buf("par")
            lrT = lam_re.rearrange("g p -> p g"); liT = lam_im.rearrange("g p -> p g")
            for hlf in range(2):
                ps_ = slice(hlf * 64, hlf * 64 + 64)
                S.dma(lr[ps_, :], lrT, writes=[b_par], acc=True, allow_slow_non_contiguous=True)
                S.dma(li[ps_, :], liT, writes=[b_par], acc=True, allow_slow_non_contiguous=True)
                S.dma(brr[ps_, :, :], b_re.rearrange("g p m -> p g m"), writes=[b_par], acc=True)
                S.dma(bii[ps_, :, :], b_im.rearrange("g p m -> p g m"), writes=[b_par], acc=True)
                S.dma(cnat[:, 0, :, hlf, :], c_re.rearrange("(ft gl) n p -> (gl n) ft p", gl=8), writes=[b_par], acc=True)
                S.dma(cnat[:, 1, :, hlf, :], c_im.rearrange("(ft gl) n p -> (gl n) ft p", gl=8), writes=[b_par], acc=True)
            S.dma(ldt[:, :], log_dt.partition_broadcast(128), writes=[b_par], acc=True)
            S.dma(drep[:, :, :], s5d.rearrange("g n -> (g n)").partition_broadcast(128).rearrange("p (g n) -> p g n", n=16),
                  writes=[b_par], acc=True)

            checkpoint(-1)
            b_cT = Buf("cT")
            for ri, dst in ((0, crT), (1, ciT)):
                for ft in range(8):
                    bk = 6 + (ft % 2)
                    S.op("pe", lambda e: e.transpose(out=PS[bk][:, 0:128], in_=cnat[:, ri, ft, :, :].rearrange("q a p -> q (a p)"),
                                                     identity=ident), [b_par, b_ct], [PB[bk]])
                    S.cp("act", dst[:, ft * 8:(ft + 1) * 8, :].rearrange("q g n -> q (g n)"), PS[bk][:, 0:128], [PB[bk]], [b_cT])

            checkpoint(-2)
            b_w = Buf("p0w")
            dtt = sb(ph, "dtt", [128, G]); aa = sb(ph, "aa", [128, G]); th = sb(ph, "th", [128, G])
            S.act(dtt[:, :], ldt[:, :], AF.Exp, [b_par], [b_w])
            S.tt("dve", aa[:, :], lr[:, :], dtt[:, :], ALU.mult, [b_par, b_w], [b_w])
            S.tt("dve", th[:, :], li[:, :], dtt[:, :], ALU.mult, [b_par, b_w], [b_w])
            KA = sb(ph, "KA", [128, 9, G]); ANG = sb(ph, "ANG", [128, 2, 9, G]); RN = sb(ph, "RN", [128, 2, 9, G])
            kvb = C("kv").unsqueeze(2).broadcast_to([128, 9, G])
            S.tt("dve", KA[:, :, :], aa[:, :].unsqueeze(1).broadcast_to([128, 9, G]), kvb, ALU.mult, [b_w, b_ct], [b_w])
            S.tt("dve", ANG[:, 0, :, :], th[:, :].unsqueeze(1).broadcast_to([128, 9, G]), kvb, ALU.mult, [b_w, b_ct], [b_w])
            S.ts("dve", ANG[:, 1, :, :], ANG[:, 0, :, :], math.pi / 2, ALU.add, [b_w], [b_w])
            MAGIC = 12582912.0
            S.ts("dve", RN[:, :, :, :], ANG[:, :, :, :], 1.0 / TWO_PI, ALU.mult, [b_w], [b_w])
            S.ts("dve", RN[:, :, :, :], RN[:, :, :, :], MAGIC, ALU.add, [b_w], [b_w])
            S.ts("dve", RN[:, :, :, :], RN[:, :, :, :], MAGIC, ALU.subtract, [b_w], [b_w])
            S.stt("dve", ANG[:, :, :, :], RN[:, :, :, :], -TWO_PI, ANG[:, :, :, :], ALU.mult, ALU.add, [b_w], [b_w])
            S.ts("dve", ANG[:, :, :, :], ANG[:, :, :, :], 3.1415925, ALU.min, [b_w], [b_w], s2=-3.1415925, op1=ALU.max)
            SC = sb(ph, "SC", [128, 2, 9, G])
            S.act(SC[:, :, :, :], ANG[:, :, :, :], AF.Sin, [b_w], [b_w])
            EP = sb(ph, "EP", [128, 9, G]); EN = sb(ph, "EN", [128, 9, G])
            S.act(EP[:, :, :], KA[:, :, :], AF.Exp, [b_w], [b_w])
            S.act(EN[:, :, :], KA[:, :, :], AF.Exp, [b_w], [b_w], scale=-1.0)
            Pr = sb(ph, "Pr", [128, 9, G]); Pi = sb(ph, "Pi", [128, 9, G])
            Nr = sb(ph, "Nr", [128, 9, G]); Ni = sb(ph, "Ni", [128, 9, G])
            S.tt("dve", Pr[:, :, :], EP[:, :, :], SC[:, 1, :, :], ALU.mult, [b_w], [b_w])
            S.tt("dve", Pi[:, :, :], EP[:, :, :], SC[:, 0, :, :], ALU.mult, [b_w], [b_w])
            S.tt("dve", Nr[:, :, :], EN[:, :, :], SC[:, 1, :, :], ALU.mult, [b_w], [b_w])
            S.stt("dve", Ni[:, :, :], EN[:, :, :], -1.0, SC[:, 0, :, :], ALU.mult, ALU.mult, [b_w], [b_w])
            nr = sb(ph, "nr", [128, G]); den = sb(ph, "den", [128, G]); t0 = sb(ph, "t0", [128, G]); t1 = sb(ph, "t1", [128, G])
            cfr = sb(ph, "cfr", [128, G]); cfi = sb(ph, "cfi", [128, G])
            S.ts("dve", nr[:, :], Pr[:, 1, :], -1.0, ALU.add, [b_w], [b_w])
            S.tt("dve", den[:, :], lr[:, :], lr[:, :], ALU.mult, [b_par], [b_w])
            S.tt("dve", t0[:, :], li[:, :], li[:, :], ALU.mult, [b_par], [b_w])
            S.tt("dve", den[:, :], den[:, :], t0[:, :], ALU.add, [b_w], [b_w])
            S.op("dve", lambda e: e.reciprocal(out=den[:, :], in_=den[:, :]), [b_w], [b_w])
            S.tt("dve", t0[:, :], nr[:, :], lr[:, :], ALU.mult, [b_w, b_par], [b_w])
            S.tt("dve", t1[:, :], Pi[:, 1, :], li[:, :], ALU.mult, [b_w, b_par], [b_w])
            S.tt("dve", t0[:, :], t0[:, :], t1[:, :], ALU.add, [b_w], [b_w])
            S.tt("dve", cfr[:, :], t0[:, :], den[:, :], ALU.mult, [b_w], [b_w])
            S.tt("dve", t0[:, :], Pi[:, 1, :], lr[:, :], ALU.mult, [b_w, b_par], [b_w])
            S.tt("dve", t1[:, :], nr[:, :], li[:, :], ALU.mult, [b_w, b_par], [b_w])
            S.tt("dve", t0[:, :], t0[:, :], t1[:, :], ALU.subtract, [b_w], [b_w])
            S.tt("dve", cfi[:, :], t0[:, :], den[:, :], ALU.mult, [b_w], [b_w])

            def cmul(outr, outi, ar, ai, br_, bi_, shape, tmpa, tmpb):
                S.tt("dve", tmpa, ar, br_, ALU.mult, [b_w], [b_w])
                S.tt("dve", tmpb, ai, bi_, ALU.mult, [b_w], [b_w])
                S.tt("dve", outr, tmpa, tmpb, ALU.subtract, [b_w], [b_w])
                S.tt("dve", tmpa, ar, bi_, ALU.mult, [b_w], [b_w])
                S.tt("dve", tmpb, ai, br_, ALU.mult, [b_w], [b_w])
                S.tt("dve", outi, tmpa, tmpb, ALU.add, [b_w], [b_w])

            Dr = sb(ph, "Dr", [128, 8, G]); Di = sb(ph, "Di", [128, 8, G])
            DPr = sb(ph, "DPr", [128, 8, G]); DPi = sb(ph, "DPi", [128, 8, G])
            ta = sb(ph, "ta", [128, 8, G]); tb = sb(ph, "tb", [128, 8, G])
            cfrb = cfr[:, :].unsqueeze(1).broadcast_to([128, 8, G]); cfib = cfi[:, :].unsqueeze(1).broadcast_to([128, 8, G])
            cmul(Dr[:, :, :], Di[:, :, :], Nr[:, 0:8, :], Ni[:, 0:8, :], cfrb, cfib, None, ta[:, :, :], tb[:, :, :])
            PRr = sb(ph, "PRr", [128, 8, G]); PRi = sb(ph, "PRi", [128, 8, G])
            for s_ in range(8):
                S.cp("dve", PRr[:, s_, :], Pr[:, 7 - s_, :], [b_w], [b_w])
                S.cp("dve", PRi[:, s_, :], Pi[:, 7 - s_, :], [b_w], [b_w])
            cmul(DPr[:, :, :], DPi[:, :, :], PRr[:, :, :], PRi[:, :, :], cfrb, cfib, None, ta[:, :, :], tb[:, :, :])

            mre = C("pvec", 0, 1); mim = C("pvec", 1, 2); nmre = C("pvec", 2, 3); nmim = C("pvec", 3, 4)

            def cat(out, xr, m0, xi, m1):
                S.ts("dve", out, xr, m0, ALU.mult, [b_w, b_ct], [b_w])
                S.stt("dve", out, xi, m1, out, ALU.mult, ALU.add, [b_w, b_ct], [b_w])

            Dcat = sb(ph, "Dcat", [128, 8, G]); Dsw = sb(ph, "Dsw", [128, 8, G])
            Gc = sb(ph, "Gc", [128, 8, G]); Gs = sb(ph, "Gs", [128, 8, G])
            Pc = sb(ph, "Pc", [128, 9, G]); Pw = sb(ph, "Pw", [128, 9, G])
            cat(Dcat[:, :, :], Dr[:, :, :], mre, Di[:, :, :], mim)
            cat(Dsw[:, :, :], Di[:, :, :], nmre, Dr[:, :, :], mim)
            cat(Gc[:, :, :], DPr[:, :, :], mre, DPi[:, :, :], mim)
            cat(Gs[:, :, :], DPi[:, :, :], nmre, DPr[:, :, :], mim)
            cat(Pc[:, :, :], Pr[:, :, :], mre, Pi[:, :, :], nmim)
            cat(Pw[:, :, :], Pi[:, :, :], nmre, Pr[:, :, :], nmim)
            for dst, srcr, srci, k in ((A8, Pr, Pi, 8), (A1, Pr, Pi, 1), (Am7, Nr, Ni, 7)):
                S.cp("dve", dst[:, 0, :], srcr[0:64, k, :], [b_w], [b_A8])
                S.cp("dve", dst[:, 1, :], srci[0:64, k, :], [b_w], [b_A8])
                S.ts("dve", dst[:, 2, :], srci[0:64, k, :], -1.0, ALU.mult, [b_w], [b_A8])
                S.cp("dve", dst[:, 3, :], srcr[0:64, k, :], [b_w], [b_A8])

            checkpoint(-3)
            osets = []
            for i_ in range(2):
                osets.append(dict(
                    Lc=sb(ph, "Lc%d" % i_, [128, 8, 8, 16]), Gq=sb(ph, "Gq%d" % i_, [128, 8, 8, 16]),
                    Rc=sb(ph, "Rc%d" % i_, [128, 8, 9, 16]), tmpL=sb(ph, "tmpL%d" % i_, [128, 8, 9, 16]),
                    Tb=sb(ph, "Tb%d" % i_, [128, 8, 128], BF16), GTb=sb(ph, "GTb%d" % i_, [128, 8, 128], BF16),
                    Rb=sb(ph, "Rb%d" % i_, [128, 8, 144], BF16),
                    b_L=Buf("L"), b_Gq=Buf("Gq"), b_R=Buf("R"), b_t=Buf("tmpL"), b_Tb=Buf("Tb"), b_GTb=Buf("GTb"), b_Rb=Buf("Rb")))
            for hf in range(8):
                gs = slice(hf * 8, hf * 8 + 8)
                if True:
                    o_ = osets[hf % 2]
                    Lc, Gq, Rc, tmpL, Tb, GTb, Rb = o_["Lc"], o_["Gq"], o_["Rc"], o_["tmpL"], o_["Tb"], o_["GTb"], o_["Rb"]
                    b_L, b_Gq, b_R, b_t, b_Tb, b_GTb, b_Rb = o_["b_L"], o_["b_Gq"], o_["b_R"], o_["b_t"], o_["b_Tb"], o_["b_GTb"], o_["b_Rb"]

                    def bc_sg(x):
                        return x[:, :, gs].rearrange("q s g -> q g s").unsqueeze(3).broadcast_to([128, 8, 8, 16])

                    def bc_gm(x):
                        return x[:, gs, :].unsqueeze(2).broadcast_to([128, 8, 8, 16])

                    S.tt("dve", Lc[:, :, :, :], bc_sg(Dcat), bc_gm(brr), ALU.mult, [b_w, b_par], [b_L])
                    S.tt("dve", tmpL[:, :, 0:8, :], bc_sg(Dsw), bc_gm(bii), ALU.mult, [b_w, b_par], [b_t])
                    S.tt("dve", Lc[:, :, :, :], Lc[:, :, :, :], tmpL[:, :, 0:8, :], ALU.add, [b_L, b_t], [b_L])
                    S.tt("dve", Gq[:, :, :, :], bc_sg(Gc), bc_gm(brr), ALU.mult, [b_w, b_par], [b_Gq])
                    S.tt("dve", tmpL[:, :, 0:8, :], bc_sg(Gs), bc_gm(bii), ALU.mult, [b_w, b_par], [b_t])
                    S.tt("dve", Gq[:, :, :, :], Gq[:, :, :, :], tmpL[:, :, 0:8, :], ALU.add, [b_Gq, b_t], [b_Gq])
                    pcb = Pc[:, :, gs].rearrange("q s g -> q g s").unsqueeze(3).broadcast_to([128, 8, 9, 16])
                    pwb = Pw[:, :, gs].rearrange("q s g -> q g s").unsqueeze(3).broadcast_to([128, 8, 9, 16])
                    crb = crT[:, gs, :].unsqueeze(2).broadcast_to([128, 8, 9, 16])
                    cib = ciT[:, gs, :].unsqueeze(2).broadcast_to([128, 8, 9, 16])
                    S.tt("dve", Rc[:, :, :, :], pcb, crb, ALU.mult, [b_w, b_cT], [b_R])
                    S.tt("dve", tmpL[:, :, :, :], pwb, cib, ALU.mult, [b_w, b_cT], [b_t])
                    S.tt("dve", Rc[:, :, :, :], Rc[:, :, :, :], tmpL[:, :, :, :], ALU.add, [b_R, b_t], [b_R])
                    checkpoint(-4)
                    S.cp("act", Rb[:, :, :], Rc[:, :, :, :].rearrange("q g s n -> q g (s n)"), [b_R], [b_Rb])
                    S.dma(DR[gs, :, :].rearrange("g q c -> q g c"), Rb[:, :, :], reads=[b_Rb], writes=[b_DR], acc=True, eng="act", sem_of=b_Rb)
                    checkpoint(-5)
                    for g4 in range(2):
                        bk = 2 * (g4 % 2)
                        for gi in range(4):
                            g = g4 * 4 + gi
                            S.op("pe", lambda e: e.matmul(PS[bk][:, gi * 128:(gi + 1) * 128],
                                                          lhsT=Lc[:, g, :, :].rearrange("q s m -> q (s m)"),
                                                          rhs=Rc[:, g, 0:8, :].rearrange("q s n -> q (s n)"), start=True, stop=True),
                                 [b_L, b_R], [PB[bk]])
                            S.op("pe", lambda e: e.transpose(out=PS[bk + 1][:, gi * 128:(gi + 1) * 128],
                                                             in_=Gq[:, g, :, :].rearrange("q s m -> q (s m)"), identity=ident),
                                 [b_Gq, b_ct], [PB[bk + 1]])
                        tsl = Tb[:, g4 * 4:(g4 + 1) * 4, :]
                        tm = tmpL[:, 0:4, 0:8, :]
                        S.tt("dve", tm, C("sel").rearrange("q (s n) -> q s n", n=16).unsqueeze(1).broadcast_to([128, 4, 8, 16]),
                             drep[:, hf * 8 + g4 * 4: hf * 8 + g4 * 4 + 4, :].unsqueeze(2).broadcast_to([128, 4, 8, 16]),
                             ALU.mult, [b_ct, b_par], [b_t])
                        tm2 = tmpL[:, 4:8, 0:8, :]
                        S.tt("dve", tm2.rearrange("q g s n -> q g (s n)"), PS[bk][:, :].rearrange("q (g c) -> q g c", g=4),
                             C("cm").unsqueeze(1).broadcast_to([128, 4, 128]), ALU.mult, [PB[bk], b_ct], [b_t])
                        S.tt("dve", tsl, tm.rearrange("q g s n -> q g (s n)"), tm2.rearrange("q g s n -> q g (s n)"), ALU.add, [b_t], [b_Tb])
                        S.cp("act", GTb[:, g4 * 4:(g4 + 1) * 4, :].rearrange("q g c -> q (g c)"), PS[bk + 1][:, :], [PB[bk + 1]], [b_GTb])
                    checkpoint(-6)
                    S.dma(DTm[gs, :, :].rearrange("g q c -> q g c"), Tb[:, :, :], reads=[b_Tb], writes=[b_DT], acc=True, eng="act", sem_of=b_Tb)
                    S.dma(DGT[gs, :, :].rearrange("g q c -> q g c"), GTb[:, :, :], reads=[b_GTb], writes=[b_DGT], acc=True, eng="act", sem_of=b_GTb)
        if debug:
            dbgA = nc.dram_tensor("dbgA", [3, 64, 4, G], F32, kind="ExternalOutput").ap()
            b_dbgA = Buf("dbgA")
            for i_, t_ in enumerate((A8, A1, Am7)):
                S.dma(dbgA[i_], t_[:, :, :], reads=[b_A8], writes=[b_dbgA], acc=True)
        S.barrier()
        checkpoint(0)

        def load_grep(stack, vec):
            GREP[0] = sb(stack, "grep", [128, D])
            S.dma(GREP[0][:, :], vec.partition_broadcast(128), writes=[b_grep])

        def build_hT(ph, srcs, hT, hTb):
            xt = [sb(ph, "xt%d" % i, [128, D]) for i in range(2)]
            b_xt = [Buf("xt%d" % i) for i in range(2)]
            junk = sb(ph, "junk", [128, D], BF16); b_junk = Buf("junk")
            hb = [sb(ph, "hb%d" % i, [128, D], BF16) for i in range(2)]
            b_hb = [Buf("hb%d" % i) for i in range(2)]
            ss = sb(ph, "ss", [128, len(srcs)]); b_ss = [Buf("ss%d" % i) for i in range(len(srcs))]
            for i, (src, nrows) in enumerate(srcs):
                s_ = i % 2
                if nrows < 128:
                    S.op("pool", lambda e: e.memset(xt[s_][:, :], 0.0), [], [b_xt[s_]])
                S.dma(xt[s_][0:nrows, :], src, writes=[b_xt[s_]])
                S.op("pool", lambda e: e.memset(ss[:, i:i + 1], 0.0), [], [b_ss[i]])
                S.act(junk[:, :], xt[s_][:, :], AF.Square, [b_xt[s_]], [b_junk, b_ss[i]], accum_out=ss[:, i:i + 1])
                S.ts("dve", ss[:, i:i + 1], ss[:, i:i + 1], 1.0 / D, ALU.mult, [b_ss[i]], [b_ss[i]], s2=EPS, op1=ALU.add)
                S.act(ss[:, i:i + 1], ss[:, i:i + 1], AF.Sqrt, [b_ss[i]], [b_ss[i]])
                S.op("dve", lambda e: e.reciprocal(out=ss[:, i:i + 1], in_=ss[:, i:i + 1]), [b_ss[i]], [b_ss[i]])
                S.stt("dve", hb[s_][:, :], xt[s_][:, :], ss[:, i:i + 1], GREP[0][:, :], ALU.mult, ALU.mult,
                      [b_xt[s_], b_ss[i], b_grep], [b_hb[s_]])
                for half in range(2):
                    bk = 6 + half
                    for k8 in range(8):
                        kt = half * 8 + k8
                        S.op("pe", lambda e: e.transpose(out=psb(bk)[:, k8 * 128:(k8 + 1) * 128],
                                                         in_=hb[s_][:, kt * 128:(kt + 1) * 128], identity=identb[:, :]),
                             [b_hb[s_], b_id], [PB[bk]])
                    S.cp("act" if half == 0 else "dve", hT[:, half * 8:half * 8 + 8, i * 128:(i + 1) * 128],
                         psb(bk).rearrange("q (k t) -> q k t", k=8), [PB[bk]], [hTb[i]])

        def proj(wb, bwb, KT, rhsT, rbufs, groups, banks):
            for gi, (n0, nw) in enumerate(groups):
                mm_group(banks[gi], nw, [(wb[:, kt, :], rhsT[:, kt, n0:n0 + nw]) for kt in range(KT)],
                         [bwb] + rbufs[gi])

        def rope_evac(bank, nw, cosT, sinX, n0, b_tab, out_ap, b_out, tA, tB, b_tA, b_tB, gi=0):
            tA, tB, b_tA, b_tB = tA[gi], tB[gi], b_tA[gi], b_tB[gi]
            S.tt("dve", tA[:, 0:nw], PS[bank][:, 0:nw], cosT[:, n0:n0 + nw], ALU.mult, [PB[bank], b_tab], [b_tA])
            S.tt("dve", tB[0:64, 0:nw], PS[bank][64:128, 0:nw], sinX[64:128, n0:n0 + nw], ALU.mult, [PB[bank], b_tab], [b_tB])
            S.tt("dve", tB[64:128, 0:nw], PS[bank][0:64, 0:nw], sinX[0:64, n0:n0 + nw], ALU.mult, [PB[bank], b_tab], [b_tB])
            S.tt("dve", out_ap, tA[:, 0:nw], tB[:, 0:nw], ALU.add, [b_tA, b_tB], [b_out])

        S.phase = "p1"
        GR1 = [(0, 512), (512, 512)]
        with contextlib.ExitStack() as ph:
            hT = sb(ph, "hT1", [128, 16, 1024], BF16); hTb = [Buf("hT1_%d" % i) for i in range(8)]
            rope = sb(ph, "rope1", [128, 2, 1024]); b_rope = Buf("rope1")
            S.dma(rope[:, :, :], ropeP.rearrange("a q t -> q a t"), writes=[b_rope])
            with contextlib.ExitStack() as p2:
                load_grep(p2, g_pre)
                build_hT(p2, [(xP[i * 128:(i + 1) * 128, :], 128) for i in range(8)], hT, hTb)
                S.barrier()
            rb1 = [hTb[0:4], hTb[4:8]]
            ktok = sb(ph, "ktokP", [128, 8, NH, 128], BF16); b_ktok = [Buf("ktokP%d" % h) for h in range(NH)]
            vtok = sb(ph, "vtokP", [128, 8, NH, 128], BF16); b_vtok = [Buf("vtokP%d" % h) for h in range(NH)]
            kr = [sb(ph, "kr%d" % i, [128, 1024], BF16) for i in range(2)]; b_kr = [Buf("kr%d" % i) for i in range(2)]
            tA = [sb(ph, "tA%d" % i, [128, 512]) for i in range(2)]; tB = [sb(ph, "tB%d" % i, [128, 512]) for i in range(2)]
            b_tA = [Buf("tA%d" % i) for i in range(2)]; b_tB = [Buf("tB%d" % i) for i in range(2)]
            U2 = [sb(ph, "U2_%d" % i, [128, 8, 128], BF16) for i in range(2)]; b_U2 = [Buf("U2_%d" % i) for i in range(2)]
            blocks = [("k", h) for h in range(NH)] + [("v", h) for h in range(NH)] + [("u", f) for f in range(8)]

            def wsrc(kind, i):
                base = {"k": 8, "v": 16, "u": 32}[kind]
                return w_in[base + i]

            tails = []
            for bi, (kind, idx) in enumerate(blocks):
                wb, bwb = wnext()
                banks = [0, 1] if bi % 2 == 0 else [2, 3]
                proj(wb, bwb, 16, hT, rb1, GR1, banks)
                while tails:
                    tails.pop(0)()
                s_ = bi % 2
                if kind == "k":
                    for gi, (n0, nw) in enumerate(GR1):
                        rope_evac(banks[gi], nw, rope[:, 0, :], rope[:, 1, :], n0, b_rope, kr[s_][:, n0:n0 + nw], b_kr[s_], tA, tB, b_tA, b_tB, gi)
                    def tail(s_=s_, idx=idx):
                        for c in range(8):
                            S.op("pe", lambda e: e.transpose(out=psb(6)[:, c * 128:(c + 1) * 128], in_=kr[s_][:, c * 128:(c + 1) * 128],
                                                             identity=identb[:, :]), [b_kr[s_], b_id], [PB[6]])
                        S.tt("dve", ktok[:, :, idx, :], psb(6).rearrange("q (c d) -> q c d", c=8),
                             C("dkP").rearrange("q (c h) -> q c h", h=NH)[:, :, idx:idx + 1].broadcast_to([128, 8, 128]),
                             ALU.mult, [PB[6], b_ct], [b_ktok[idx]])
                    tails.append(tail)
                elif kind == "v":
                    for gi, (n0, nw) in enumerate(GR1):
                        S.cp("act", kr[s_][:, n0:n0 + nw], PS[banks[gi]][:, 0:nw], [PB[banks[gi]]], [b_kr[s_]])
                    def tail(s_=s_, idx=idx):
                        for c in range(8):
                            S.op("pe", lambda e: e.transpose(out=psb(7)[:, c * 128:(c + 1) * 128], in_=kr[s_][:, c * 128:(c + 1) * 128],
                                                             identity=identb[:, :]), [b_kr[s_], b_id], [PB[7]])
                        S.cp("act", vtok[:, :, idx, :], psb(7).rearrange("q (c d) -> q c d", c=8), [PB[7]], [b_vtok[idx]])
                    tails.append(tail)
                else:
                    for gi, (n0, nw) in enumerate(GR1):
                        S.cp("act", U2[s_][:, :, gi * 64:(gi + 1) * 64],
                             PS[banks[gi]][:, 0:nw].rearrange("q (j s) -> q s j", s=8), [PB[banks[gi]]], [b_U2[s_]])
                    for gl in range(8):
                        S.dma(DU[idx * 8 + gl].rearrange("s m j -> m s j"), U2[s_][gl * 16:(gl + 1) * 16, :, :],
                              reads=[b_U2[s_]], writes=[b_DU], acc=True, sem_of=b_U2[s_], eng="act")
            while tails:
                tails.pop(0)()
            for h in range(NH):
                bk = 4 + h // 4
                for c in range(8):
                    S.op("pe", lambda e: e.matmul(PS[bk][:, (h % 4) * 128:(h % 4 + 1) * 128], lhsT=ktok[:, c, h, :], rhs=vtok[:, c, h, :],
                                                  start=(c == 0), stop=(c == 7)),
                         [b_ktok[h], b_vtok[h]] if c in (0, 7) else (), [PB[bk]] if c in (0, 7) else (), inc=(c == 7))
            for half in range(2):
                S.cp("act", Sst[:, half * 4:(half + 1) * 4, :].rearrange("q h v -> q (h v)"), PS[4 + half][:, :], [PB[4 + half]], [b_Sst])
            S.barrier()
            checkpoint(1)

        def build_apow(stack):
            Ap = sb(stack, "Apow", [64, 7, 4, G]); b_Ap = Buf("Apow")
            tq = sb(stack, "tq", [64, 3, G]); b_tq = Buf("tq")
            S.cp("dve", Ap[:, 0, :, :], A8[:, :, :], [b_A8], [b_Ap])
            for l_ in range(1, 7):
                pr, pi = Ap[:, l_ - 1, 0, :], Ap[:, l_ - 1, 1, :]
                S.tt("dve", tq[:, 0, :], pr, pr, ALU.mult, [b_Ap], [b_tq])
                S.tt("dve", tq[:, 1, :], pi, pi, ALU.mult, [b_Ap], [b_tq])
                S.tt("dve", tq[:, 2, :], pr, pi, ALU.mult, [b_Ap], [b_tq])
                S.tt("dve", Ap[:, l_, 0, :], tq[:, 0, :], tq[:, 1, :], ALU.subtract, [b_tq], [b_Ap])
                S.ts("dve", Ap[:, l_, 1, :], tq[:, 2, :], 2.0, ALU.mult, [b_tq], [b_Ap])
                S.ts("dve", Ap[:, l_, 2, :], tq[:, 2, :], -2.0, ALU.mult, [b_tq], [b_Ap])
                S.cp("dve", Ap[:, l_, 3, :], Ap[:, l_, 0, :], [b_Ap], [b_Ap])
            return Ap, b_Ap

        def bk_scan(BX, b_BX, g0, Ap, b_Ap, t, w, b_t, b_w):
            for gh in range(2):
                gl = slice(gh * 16, gh * 16 + 16)
                gg = slice(g0 + gh * 16, g0 + gh * 16 + 16)

                def cmadd(tgt, src, lvl, m):
                    ar = Ap[:, lvl, 0:1, gg].unsqueeze(3).broadcast_to([64, 2, 16, m])
                    S.tt("dve", t[:, :, :, 0:m], src, ar, ALU.mult, [b_BX, b_Ap], [b_t])
                    S.tt("dve", w[:, 0, :, 0:m], src[:, 1], Ap[:, lvl, 2, gg].unsqueeze(2).broadcast_to([64, 16, m]), ALU.mult, [b_BX, b_Ap], [b_w])
                    S.tt("dve", w[:, 1, :, 0:m], src[:, 0], Ap[:, lvl, 1, gg].unsqueeze(2).broadcast_to([64, 16, m]), ALU.mult, [b_BX, b_Ap], [b_w])
                    S.tt("dve", t[:, :, :, 0:m], t[:, :, :, 0:m], w[:, :, :, 0:m], ALU.add, [b_t, b_w], [b_t])
                    S.tt("dve", tgt, tgt, t[:, :, :, 0:m], ALU.add, [b_BX, b_t], [b_BX])

                cmadd(BX[:, :, gl, 1:2], BX[:, :, gl, 0:1], 0, 1)
                Y = BX[:, :, gl, 1:129]
                for l_ in range(7):
                    s_ = 1 << l_
                    Yv = Y.rearrange("q c g (i t) -> q c g i t", t=2 * s_)
                    cmadd(Yv[:, :, :, :, 2 * s_ - 1], Yv[:, :, :, :, s_ - 1], l_, 128 // (2 * s_))
                for l_ in range(5, -1, -1):
                    s_ = 1 << l_
                    m = 128 // (2 * s_) - 1
                    Zv = BX[:, :, gl, 2 * s_:2 * s_ + 2 * s_ * m].rearrange("q c g (i t) -> q c g i t", t=2 * s_)
                    cmadd(Zv[:, :, :, :, s_], Zv[:, :, :, :, 0], l_, m)

        def scan_steps(BX, b_BX, g0, ng, nsteps, u, w, b_u, b_w2):
            ar2 = A8[:, 0:1, g0:g0 + ng].broadcast_to([64, 2, ng])
            S.nosame = True
            for j in range(nsteps):
                prev = BX[:, :, :, j]
                S.tt("dve", u[:, :, :], prev, ar2, ALU.mult, [b_BX, b_A8], [b_u])
                S.tt("dve", w[:, 0, :], BX[:, 1, :, j], A8[:, 2, g0:g0 + ng], ALU.mult, [b_BX, b_A8], [b_w2])
                S.tt("dve", w[:, 1, :], BX[:, 0, :, j], A8[:, 1, g0:g0 + ng], ALU.mult, [b_BX, b_A8], [b_w2])
                S.tt("dve", u[:, :, :], u[:, :, :], w[:, :, :], ALU.add, [b_u, b_w2], [b_u])
                S.tt("dve", BX[:, :, :, j + 1], BX[:, :, :, j + 1], u[:, :, :], ALU.add, [b_BX, b_u], [b_BX])
            S.nosame = False

        S.phase = "p1s"
        with contextlib.ExitStack() as ph:
            BX = sb(ph, "BX1", [64, 2, G, 129]); b_BX = Buf("BX1")
            pB = contextlib.ExitStack()
            U = sb(pB, "U1", [128, G, 128], BF16); b_U = Buf("U1")
            GTs = sb(pB, "GT1", [128, G, 128], BF16); b_GT = Buf("GT1")
            S.dma(U[:, :, :], DU.rearrange("g s m j -> (s m) g j"), reads=[b_DU], writes=[b_U])
            S.dma(GTs[:, :, :], DGT.rearrange("g q c -> q g c"), reads=[b_DGT], writes=[b_GT])
            S.op("pool", lambda e: e.memset(BX[:, :, :, 0:1], 0.0), [], [b_BX])
            for g in range(G):
                bk = g % 4
                for c in range(2):
                    S.op("pe", lambda e: e.matmul(PS[bk][0:64, c * 128:(c + 1) * 128], lhsT=GTs[:, g, c * 64:(c + 1) * 64], rhs=U[:, g, :],
                                                  start=True, stop=True), [b_GT, b_U], [PB[bk]])
                S.cp("act", BX[:, :, g, 1:129], PS[bk][0:64, 0:256].rearrange("q (c j) -> q c j", c=2), [PB[bk]], [b_BX])
            S.barrier()
            pB.close()
            Ap, b_Ap = build_apow(ph)
            Za = sb(ph, "Za", [64, 2, 32, 64]); Zb = sb(ph, "Zb", [64, 2, 32, 32]); b_Za = Buf("Za"); b_Zb = Buf("Zb")
            tt_ = sb(ph, "trt", [64, 2, 32, 64]); tw_ = sb(ph, "trw", [64, 2, 32, 64]); b_tt = Buf("trt"); b_tw = Buf("trw")
            for hf_ in range(2):
                gsl = slice(hf_ * 32, hf_ * 32 + 32)
                src, b_src = BX[:, :, gsl, 1:129], b_BX
                n_ = 128
                for l_ in range(7):
                    n2 = n_ // 2
                    Lv = src.rearrange("q c g (i two) -> q c g i two", two=2)[:, :, :, :, 0]
                    Rv = src.rearrange("q c g (i two) -> q c g i two", two=2)[:, :, :, :, 1]
                    if l_ % 2 == 0:
                        dst, b_dst = Za[:, :, :, 0:n2], b_Za
                    else:
                        dst, b_dst = Zb[:, :, :, 0:n2], b_Zb
                    ar = Ap[:, l_, 0:1, gsl].unsqueeze(3).broadcast_to([64, 2, 32, n2])
                    S.tt("dve", tt_[:, :, :, 0:n2], Lv, ar, ALU.mult, [b_src, b_Ap], [b_tt])
                    S.tt("dve", tw_[:, 0, :, 0:n2], Lv[:, 1], Ap[:, l_, 2, gsl].unsqueeze(2).broadcast_to([64, 32, n2]), ALU.mult, [b_src, b_Ap], [b_tw])
                    S.tt("dve", tw_[:, 1, :, 0:n2], Lv[:, 0], Ap[:, l_, 1, gsl].unsqueeze(2).broadcast_to([64, 32, n2]), ALU.mult, [b_src, b_Ap], [b_tw])
                    S.tt("dve", tt_[:, :, :, 0:n2], tt_[:, :, :, 0:n2], tw_[:, :, :, 0:n2], ALU.add, [b_tt, b_tw], [b_tt])
                    S.tt("dve", dst, tt_[:, :, :, 0:n2], Rv, ALU.add, [b_tt, b_src], [b_dst])
                    src, b_src, n_ = dst, b_dst, n2
                S.cp("dve", Xmid[:, :, gsl], src[:, :, :, 0], [b_src], [b_Xmid])
            if debug:
                dbgX = nc.dram_tensor("dbgX", [64, 2, G], F32, kind="ExternalOutput").ap(); b_dbgX = Buf("dbgX")
                S.dma(dbgX, Xmid[:, :, :], reads=[b_Xmid], writes=[b_dbgX])
                dbgBX = nc.dram_tensor("dbgBX", [64, 2, G, 129], F32, kind="ExternalOutput").ap(); b_dbgBX = Buf("dbgBX")
                S.dma(dbgBX, BX[:, :, :, :], reads=[b_BX], writes=[b_dbgBX])
            S.barrier()
            checkpoint(2)

        S.phase = "p2h"
        GR2 = [(0, 512), (512, 512), (1024, 16)]
        NT2 = 1040
        with contextlib.ExitStack() as ph:
            hT = sb(ph, "hT2", [128, 16, 1152], BF16); hTb = [Buf("hT2_%d" % i) for i in range(9)]
            with contextlib.ExitStack() as p2:
                load_grep(p2, g_pre)
                build_hT(p2, [(xM[i * 128:(i + 1) * 128, :], 128) for i in range(8)] + [(xS, 16)], hT, hTb)
                S.barrier()
                checkpoint(3)
            rb2 = [hTb[0:4], hTb[4:8], hTb[8:9]]
            yainT = sb(ph, "yainT", [128, 8, NT2], BF16); b_yain = [Buf("yain%d" % i) for i in range(9)]
            b_mrg = [Buf("mrg%d" % i) for i in range(16)]

            S.phase = "ret"
            HH = 4
            b_oret = Buf("o_retP"); out_bufs.append(b_oret)
            b_oretS = [Buf("o_retS%d" % i) for i in range(2)]; out_bufs.extend(b_oretS)
            for hh in range(2):
              with contextlib.ExitStack() as p2:
                h0 = hh * HH
                qT = sb(p2, "qT", [128, HH, NT2], BF16); b_qT = [Buf("qT%d" % h) for h in range(HH)]
                kT = sb(p2, "kT", [128, HH, NT2], BF16); b_kT = [Buf("kT%d" % h) for h in range(HH)]
                ktok = sb(p2, "ktok", [128, 8, HH, 128], BF16); b_ktok = [Buf("ktok%d" % c) for c in range(8)]
                vtok = sb(p2, "vtok", [128, 8, HH, 128], BF16); b_vtok = [Buf("vtok%d" % c) for c in range(8)]
                ztok = sb(p2, "ztok", [128, 8, HH, 128], BF16); b_ztok = [Buf("ztok%d" % c) for c in range(8)]
                ktS = sb(p2, "ktS", [16, HH, 128], BF16); vtS = sb(p2, "vtS", [16, HH, 128], BF16); ztS = sb(p2, "ztS", [128, HH, 128], BF16)
                b_ktS = Buf("ktS"); b_vtS = Buf("vtS"); b_ztS = Buf("ztS")
                S.op("pool", lambda e: e.memset(ztS[:, :, :], 0.0), [], [b_ztS])
                p3 = contextlib.ExitStack()
                rope = sb(p3, "rope2", [128, 2, NT2]); b_rope = Buf("rope2")
                S.dma(rope[:, :, :], ropeM.rearrange("a q t -> q a t"), writes=[b_rope])
                vz = [sb(p3, "vz%d" % i, [128, NT2], BF16) for i in range(2)]; b_vz = [Buf("vz%d" % i) for i in range(2)]
                tA = [sb(p3, "tA%d" % i, [128, 512]) for i in range(3)]; tB = [sb(p3, "tB%d" % i, [128, 512]) for i in range(3)]
                b_tA = [Buf("tA%d" % i) for i in range(3)]; b_tB = [Buf("tB%d" % i) for i in range(3)]
                blocks = []
                for hl in range(HH):
                    blocks += [("q", hl), ("k", hl), ("v", hl), ("z", hl)]

                def wsrc2(kind, hl):
                    base = {"q": 0, "k": 8, "v": 16, "z": 24}[kind] + (h0 + hl)
                    return w_in[base]

                tails = []
                for bi, (kind, h) in enumerate(blocks):
                    wb, bwb = wnext()
                    banks = [0, 1, 2] if bi % 2 == 0 else [3, 4, 5]
                    proj(wb, bwb, 16, hT, rb2, GR2, banks)
                    while tails:
                        tails.pop(0)()
                    s_ = bi % 2
                    if kind in ("q", "k"):
                        dst, bd = (qT, b_qT) if kind == "q" else (kT, b_kT)
                        for gi, (n0, nw) in enumerate(GR2):
                            rope_evac(banks[gi], nw, rope[:, 0, :], rope[:, 1, :], n0, b_rope, dst[:, h, n0:n0 + nw], bd[h], tA, tB, b_tA, b_tB, gi)
                        if kind == "k":
                            def tail(h=h):
                                for c in range(8):
                                    S.op("pe", lambda e: e.transpose(out=psb(6)[:, c * 128:(c + 1) * 128], in_=kT[:, h, c * 128:(c + 1) * 128],
                                                                     identity=identb[:, :]), [b_kT[h], b_id], [PB[6]])
                                S.ts("dve", ktok[:, :, h, :], psb(6).rearrange("q (c d) -> q c d", c=8), C("dkM", h0 + h, h0 + h + 1), ALU.mult,
                                     [PB[6], b_ct], b_ktok)
                                S.op("pe", lambda e: e.transpose(out=psb(7)[0:16, 0:128], in_=kT[:, h, 1024:1040], identity=identb[:, :]),
                                     [b_kT[h], b_id], [PB[7]])
                                S.ts("dve", ktS[:, h, :], psb(7)[0:16, 0:128], C("kS", 0, 1)[0:16, :], ALU.mult, [PB[7], b_ct], [b_ktS])
                            tails.append(tail)
                    else:
                        dtok, bdt, dS, bdS = (vtok, b_vtok, vtS, b_vtS) if kind == "v" else (ztok, b_ztok, ztS, b_ztS)
                        for gi, (n0, nw) in enumerate(GR2):
                            if kind == "v":
                                S.cp("act", vz[s_][:, n0:n0 + nw], PS[banks[gi]][:, 0:nw], [PB[banks[gi]]], [b_vz[s_]])
                            else:
                                S.act(vz[s_][:, n0:n0 + nw], PS[banks[gi]][:, 0:nw], AF.Silu, [PB[banks[gi]]], [b_vz[s_]])
                        def tail(s_=s_, h=h, dtok=dtok, bdt=bdt, dS=dS, bdS=bdS):
                            for c in range(8):
                                S.op("pe", lambda e: e.transpose(out=psb(6)[:, c * 128:(c + 1) * 128], in_=vz[s_][:, c * 128:(c + 1) * 128],
                                                                 identity=identb[:, :]), [b_vz[s_], b_id], [PB[6]])
                            S.cp("act", dtok[:, :, h, :], psb(6).rearrange("q (c d) -> q c d", c=8), [PB[6]], bdt)
                            S.op("pe", lambda e: e.transpose(out=psb(7)[0:16, 0:128], in_=vz[s_][:, 1024:1040], identity=identb[:, :]),
                                 [b_vz[s_], b_id], [PB[7]])
                            S.cp("act", dS[0:16, h, :], psb(7)[0:16, 0:128], [PB[7]], [bdS])
                        tails.append(tail)

                while tails:
                    tails.pop(0)()
                S.barrier()
                p3.close()
                checkpoint(-20)
                Sh = Sst[:, h0:h0 + HH, :]
                Sbf = sb(p2, "Sbf", [128, HH, 128], BF16); b_Sbf = Buf("Sbf")
                sc = sb(p2, "sc", [128, HH, 128], BF16); b_sc = Buf("sc")
                sq = sb(p2, "sq", [128, HH, 128]); b_sq = Buf("sq")
                ob = sb(p2, "ob", [128, HH, 128]); b_ob = Buf("ob")
                ob2 = sb(p2, "ob2", [128, HH, 128]); b_ob2 = Buf("ob2")
                yb16 = sb(p2, "yb16", [128, HH * 128], BF16); b_yb16 = Buf("yb16")
                st8 = sb(p2, "st8", [128, 6, HH]); b_st8 = Buf("st8")
                S.cp("act", Sbf[:, :, :], Sh, [b_Sst], [b_Sbf])

                def group_norm_gate(npart, obank, dq_ap, z_ap, b_z, out16, b_out, via_sb=False):
                    P_ = slice(0, npart)
                    S.force_same = True
                    o3 = PS[obank][P_, :].rearrange("q (h v) -> q h v", h=HH)
                    if via_sb:
                        S.cp("act", ob2[P_, :, :], o3, [PB[obank]], [b_ob2])
                        o3 = ob2[P_, :, :]
                        PBo = b_ob2
                    else:
                        PBo = PB[obank]
                    S.act(sq[P_, :, :], o3, AF.Square, [PBo], [b_sq])
                    S.op("dve", lambda e: e.tensor_reduce(out=st8[P_, 0, :], in_=o3, axis=AX.X, op=ALU.add), [PBo], [b_st8])
                    S.op("dve", lambda e: e.tensor_reduce(out=st8[P_, 1, :], in_=sq[P_, :, :], axis=AX.X, op=ALU.add), [b_sq], [b_st8])
                    S.ts("dve", st8[P_, 0, :], st8[P_, 0, :], 1.0 / 128, ALU.mult, [b_st8], [b_st8])
                    S.ts("dve", st8[P_, 1, :], st8[P_, 1, :], 1.0 / 128, ALU.mult, [b_st8], [b_st8])
                    S.tt("dve", st8[P_, 2, :], st8[P_, 0, :], st8[P_, 0, :], ALU.mult, [b_st8], [b_st8])
                    S.tt("dve", st8[P_, 1, :], st8[P_, 1, :], st8[P_, 2, :], ALU.subtract, [b_st8], [b_st8])
                    S.tt("dve", st8[P_, 2, :], dq_ap, dq_ap, ALU.mult, [b_ct], [b_st8])
                    S.tt("dve", st8[P_, 1, :], st8[P_, 1, :], st8[P_, 2, :], ALU.mult, [b_st8], [b_st8])
                    S.ts("dve", st8[P_, 1, :], st8[P_, 1, :], EPS, ALU.add, [b_st8], [b_st8])
                    S.act(st8[P_, 1, :], st8[P_, 1, :], AF.Sqrt, [b_st8], [b_st8])
                    S.op("dve", lambda e: e.reciprocal(out=st8[P_, 1, :], in_=st8[P_, 1, :]), [b_st8], [b_st8])
                    S.tt("dve", st8[P_, 1, :], st8[P_, 1, :], dq_ap, ALU.mult, [b_st8, b_ct], [b_st8])
                    S.tt("dve", ob[P_, :, :], o3, st8[P_, 0, :].unsqueeze(2).broadcast_to([npart, HH, 128]), ALU.subtract,
                         [PBo, b_st8], [b_ob])
                    S.tt("dve", ob[P_, :, :], ob[P_, :, :], st8[P_, 1, :].unsqueeze(2).broadcast_to([npart, HH, 128]), ALU.mult, [b_ob, b_st8], [b_ob])
                    S.tt("dve", out16, ob[P_, :, :].rearrange("q h v -> q (h v)"), z_ap, ALU.mult, [b_ob, b_z], [b_out])
                    S.force_same = False

                NSL = 4
                sold = [sb(p2, "sold%d" % i, [128, HH, 128]) for i in range(NSL)]; b_sold = [Buf("sold%d" % i) for i in range(NSL)]
                snb = [sb(p2, "snb%d" % i, [128, HH, 128], BF16) for i in range(NSL)]; b_snb = [Buf("snb%d" % i) for i in range(NSL)]
                vmb = [sb(p2, "vmb%d" % i, [16, HH * 128], BF16) for i in range(NSL)]; b_vmb = [Buf("vmb%d" % i) for i in range(NSL)]
                b_oretS4 = [Buf("o_retS4_%d" % i) for i in range(NSL)]; out_bufs.extend(b_oretS4)
                def chunk_gen():
                    for c in range(8):
                        cs = slice(c * 128, (c + 1) * 128)
                        bS, bO, bT = c % 2, 2 + c % 2, 4
                        for h in range(HH):
                            S.op("pe", lambda e: e.matmul(PS[bS][:, h * 128:(h + 1) * 128], lhsT=kT[:, h, cs], rhs=qT[:, h, cs], start=True, stop=True),
                                 [b_kT[h], b_qT[h]], [PB[bS]])
                        S.tt("dve", sc[:, :, :].rearrange("q h k -> q (h k)"), PS[bS][:, :],
                             C("maskR", h0 * 128, (h0 + HH) * 128), ALU.mult, [PB[bS], b_ct], [b_sc])
                        for h in range(HH):
                            osl = PS[bO][:, h * 128:(h + 1) * 128]
                            rd = [b_sc, b_vtok[c], b_qT[h], b_Sbf]
                            S.op("pe", lambda e: e.matmul(osl, lhsT=sc[:, h, :], rhs=vtok[:, c, h, :], start=True, stop=False), rd, [PB[bO]], inc=False)
                            S.op("pe", lambda e: e.matmul(osl, lhsT=qT[:, h, cs], rhs=Sbf[:, h, :], start=False, stop=True), rd, [PB[bO]])
                        for h in range(HH):
                            S.op("pe", lambda e: e.matmul(PS[bT][:, h * 128:(h + 1) * 128], lhsT=ktok[:, c, h, :], rhs=vtok[:, c, h, :], start=True, stop=True),
                                 [b_ktok[c], b_vtok[c]], [PB[bT]])
                        S.tt("dve", Sh, Sh, C("gC", h0, h0 + HH).unsqueeze(2).broadcast_to([128, HH, 128]), ALU.mult, [b_Sst, b_ct], [b_Sst])
                        S.tt("dve", Sh, Sh, PS[bT][:, :].rearrange("q (h v) -> q h v", h=HH), ALU.add, [b_Sst, PB[bT]], [b_Sst])
                        S.cp("act", Sbf[:, :, :], Sh, [b_Sst], [b_Sbf])
                        group_norm_gate(128, bO, C("dqM", h0, h0 + HH), ztok[:, c, :, :].rearrange("q h v -> q (h v)"), b_ztok[c], yb16[:, :], b_yb16)
                        for f in range(HH):
                            S.op("pe", lambda e: e.transpose(out=psb(6)[:, f * 128:(f + 1) * 128], in_=yb16[:, f * 128:(f + 1) * 128], identity=identb[:, :]),
                                 [b_yb16, b_id], [PB[6]])
                        S.cp("act", yainT[:, h0:h0 + HH, cs], psb(6)[:, 0:HH * 128].rearrange("q (f t) -> q f t", f=HH), [PB[6]], [b_yain[c]])
                        yield
                def sample_gen():
                    for b in range(16):
                        s_ = b % NSL
                        kvb = 7
                        if b == 0:
                            for b2 in range(NSL - 1):
                                S.dma(sold[b2][:, :, :], sret[b2, h0:h0 + HH].rearrange("h d v -> d h v"), writes=[b_sold[b2]])
                        if b + NSL - 1 < 16:
                            S.dma(sold[(b + NSL - 1) % NSL][:, :, :], sret[b + NSL - 1, h0:h0 + HH].rearrange("h d v -> d h v"),
                                  writes=[b_sold[(b + NSL - 1) % NSL]])
                        S.ts("dve", vmb[s_][:, :], vtS[0:16, :, :].rearrange("q h v -> q (h v)"), C("oh16", b, b + 1)[0:16, :], ALU.mult, [b_vtS, b_ct], [b_vmb[s_]])
                        for h in range(HH):
                            S.op("pe", lambda e: e.matmul(PS[kvb][:, h * 128:(h + 1) * 128], lhsT=ktS[:, h, :], rhs=vmb[s_][:, h * 128:(h + 1) * 128],
                                                          start=True, stop=True), [b_ktS, b_vmb[s_]], [PB[kvb]])
                        S.tt("dve", sold[s_][:, :, :], sold[s_][:, :, :], C("gam1", h0, h0 + HH).unsqueeze(2).broadcast_to([128, HH, 128]), ALU.mult, [b_sold[s_], b_ct], [b_sold[s_]])
                        S.tt("dve", sold[s_][:, :, :], sold[s_][:, :, :], PS[kvb][:, :].rearrange("q (h v) -> q h v", h=HH), ALU.add, [b_sold[s_], PB[kvb]], [b_sold[s_]])
                        S.cp("act", snb[s_][:, :, :], sold[s_][:, :, :], [b_sold[s_]], [b_snb[s_]])
                        S.dma(retS[b, h0:h0 + HH].rearrange("h d v -> d h v"), sold[s_][:, :, :], reads=[b_sold[s_]], writes=[b_oretS4[s_]], eng="act")
                        for h in range(HH):
                            S.op("pe", lambda e: e.matmul(PS[5][:, h * 16 + b:h * 16 + b + 1], lhsT=snb[s_][:, h, :], rhs=qT[:, h, 1024 + b:1025 + b],
                                                          start=True, stop=True), [b_qT[h], b_snb[s_]], [PB[5]])
                        yield

                cg, sg_ = chunk_gen(), sample_gen()
                for _ in range(8):
                    next(cg)
                    next(sg_)
                    next(sg_)
                S.dma(retP[h0:h0 + HH].rearrange("h d v -> d h v"), Sh, reads=[b_Sst], writes=[b_oret], acc=True, eng="act")
                checkpoint(-22)
                checkpoint(-22)
                oTs = sb(p2, "oTs", [128, HH * 16]); b_oTs = Buf("oTs")
                S.cp("act", oTs[:, :], PS[5][:, 0:HH * 16], [PB[5]], [b_oTs])
                for h in range(HH):
                    S.op("pe", lambda e: e.transpose(out=PS[3][0:16, h * 128:(h + 1) * 128], in_=oTs[:, h * 16:(h + 1) * 16], identity=ident),
                         [b_oTs, b_ct], [PB[3]])
                checkpoint(-24)
                group_norm_gate(128, 3, C("one", 0, HH), ztS[:, :, :].rearrange("q h v -> q (h v)"), b_ztS, yb16[:, :], b_yb16, via_sb=True)
                checkpoint(-25)
                for f in range(HH):
                    S.op("pe", lambda e: e.transpose(out=psb(6)[:, f * 16:(f + 1) * 16], in_=yb16[0:16, f * 128:(f + 1) * 128], identity=identb[0:16, 0:16]),
                         [b_yb16, b_id], [PB[6]])
                S.cp("act", yainT[:, h0:h0 + HH, 1024:1040], psb(6)[:, 0:HH * 16].rearrange("q (f t) -> q f t", f=HH), [PB[6]], [b_yain[8]])
                S.barrier()
                checkpoint(4 + hh)

            S.phase = "s5"
            NGB = 16
            with contextlib.ExitStack() as p2:
                with contextlib.ExitStack() as p3:
                    U2 = [sb(p3, "U2_%d" % i, [128, 8, 128], BF16) for i in range(2)]; b_U2 = [Buf("U2_%d" % i) for i in range(2)]
                    uS = [sb(p3, "uS%d" % i, [128, 16], BF16) for i in range(2)]; b_uS = [Buf("uS%d" % i) for i in range(2)]
                    for f in range(8):
                        wb, bwb = wnext()
                        banks = [0, 1, 2] if f % 2 == 0 else [3, 4, 5]
                        proj(wb, bwb, 16, hT, rb2, GR2, banks)
                        s_ = f % 2
                        for gi in range(2):
                            S.cp("act", U2[s_][:, :, gi * 64:(gi + 1) * 64],
                                 PS[banks[gi]][:, 0:512].rearrange("q (j s) -> q s j", s=8), [PB[banks[gi]]], [b_U2[s_]])
                        S.cp("act", uS[s_][:, :], PS[banks[2]][:, 0:16], [PB[banks[2]]], [b_uS[s_]])
                        for gl in range(8):
                            S.dma(DU[f * 8 + gl].rearrange("s m j -> m s j"), U2[s_][gl * 16:(gl + 1) * 16, :, :],
                                  reads=[b_U2[s_]], writes=[b_DU], acc=True, sem_of=b_U2[s_], eng="act")
                            S.dma(DUS[f * 8 + gl], uS[s_][gl * 16:(gl + 1) * 16, :], reads=[b_uS[s_]], writes=[b_DUS], acc=True, sem_of=b_uS[s_], eng="act")
                    S.barrier()
                    checkpoint(6)
                xold = sb(p2, "xold", [64, 2, G, 16]); b_xold = Buf("xold")
                xnew = sb(p2, "xnew", [64, 2, G, 16]); b_xnew = Buf("xnew")
                Xfin = sb(p2, "Xfin", [64, 2, G]); b_Xfin = Buf("Xfin")
                with contextlib.ExitStack() as p3:
                    snat = sb(p3, "snat", [128, 2, 8, 64]); b_snat = Buf("snat")
                    S.dma(snat[:, 0, :, :], s5re.rearrange("(t b2) g p -> (b2 g) t p", b2=2), writes=[b_snat], acc=True)
                    S.dma(snat[:, 1, :, :], s5im.rearrange("(t b2) g p -> (b2 g) t p", b2=2), writes=[b_snat], acc=True)
                    for c in range(2):
                        for t in range(8):
                            bk = 6 + (t % 2)
                            S.op("pe", lambda e: e.transpose(out=PS[bk][0:64, 0:128], in_=snat[:, c, t, :], identity=ident), [b_snat, b_ct], [PB[bk]])
                            S.cp("act", xold[:, c, :, 2 * t:2 * t + 2].rearrange("q g b -> q b g"), PS[bk][0:64, 0:128].rearrange("q (b g) -> q b g", b=2),
                                 [PB[bk]], [b_xold])
                    S.barrier()
                    checkpoint(7)
                Ap2, b_Ap2 = build_apow(p2)
                bkt = sb(p2, "bkt", [64, 2, 16, 64]); bkw = sb(p2, "bkw", [64, 2, 16, 64]); b_bkt = Buf("bkt"); b_bkw = Buf("bkw")
                WSL = AR_WSLOT[0]
                for hf in range(2):
                    G0 = hf * 32
                    BXv = arena[0:64, WSL:WSL + 2 * 32 * 129].rearrange("q (c g j) -> q c g j", c=2, g=32)
                    b_BX = Buf("BXh")
                    with contextlib.ExitStack() as p3:
                        BSh = sb(p3, "BSh", [64, 2, 32, 16]); b_BS = Buf("BSh")
                        u = sb(p3, "scu", [64, 2, 32]); w = sb(p3, "scw", [64, 2, 32]); b_u = Buf("scu"); b_w2 = Buf("scw")
                        S.cp("dve", BXv[:, :, :, 0], Xmid[:, :, G0:G0 + 32], [b_Xmid], [b_BX])
                        pU = contextlib.ExitStack()
                        UB = [sb(pU, "UB%d" % i, [128, NGB, 144], BF16) for i in range(2)]; b_UB = [Buf("UB%d" % i) for i in range(2)]
                        GB = [sb(pU, "GB%d" % i, [128, NGB, 128], BF16) for i in range(2)]; b_GB = [Buf("GB%d" % i) for i in range(2)]
                        for q2 in range(2):
                            g0 = G0 + q2 * NGB
                            gs = slice(g0, g0 + NGB)
                            if True:
                                U, b_U, GTs, b_GTs = UB[q2], b_UB[q2], GB[q2], b_GB[q2]
                                S.op("pool", lambda e: e.memset(U[:, :, 128:144], 0.0), [], [b_U])
                                S.dma(U[:, :, 0:128], DU[gs].rearrange("g s m j -> (s m) g j"), reads=[b_DU], writes=[b_U])
                                S.dma(U[0:16, :, 128:144], DUS[gs].rearrange("g m b -> m g b"), reads=[b_DUS], writes=[b_U])
                                S.dma(GTs[:, :, :], DGT[gs].rearrange("g q c -> q g c"), reads=[b_DGT], writes=[b_GTs])
                                for gl in range(NGB):
                                    bk = gl % 4
                                    for c in range(2):
                                        S.op("pe", lambda e: e.matmul(PS[bk][0:64, c * 144:(c + 1) * 144], lhsT=GTs[:, gl, c * 64:(c + 1) * 64], rhs=U[:, gl, :],
                                                                      start=True, stop=True), [b_GTs, b_U], [PB[bk]])
                                    pv = PS[bk][0:64, 0:288].rearrange("q (c j) -> q c j", c=2)
                                    S.cp("act", BXv[:, :, q2 * NGB + gl, 1:129], pv[:, :, 0:128], [PB[bk]], [b_BX])
                                    S.cp("act", BSh[:, :, q2 * NGB + gl, :], pv[:, :, 128:144], [PB[bk]], [b_BS])
                        S.barrier()
                        pU.close()
                        bk_scan(BXv, b_BX, G0, Ap2, b_Ap2, bkt, bkw, b_bkt, b_bkw)
                        S.cp("dve", Xfin[:, :, G0:G0 + 32], BXv[:, :, :, 128], [b_BX], [b_Xfin])
                        with contextlib.ExitStack() as p4:
                            t1 = sb(p4, "st1", [64, 2, 32, 16]); t2 = sb(p4, "st2", [64, 2, 32, 16]); b_t1 = Buf("st1"); b_t2 = Buf("st2")
                            hs = slice(G0, G0 + 32)
                            xo = xold[:, :, hs, :]; xn = xnew[:, :, hs, :]

                            def cm4(outv, coef, src, b_src, first):
                                cr2 = coef[:, 0:1, hs].unsqueeze(3).broadcast_to([64, 2, 32, 16])
                                S.tt("dve", t1[:, :, :, :], src, cr2, ALU.mult, [b_src, b_A8], [b_t1])
                                S.tt("dve", t2[:, 0, :, :], src[:, 1, :, :], coef[:, 2, hs].unsqueeze(2).broadcast_to([64, 32, 16]), ALU.mult, [b_src, b_A8], [b_t2])
                                S.tt("dve", t2[:, 1, :, :], src[:, 0, :, :], coef[:, 1, hs].unsqueeze(2).broadcast_to([64, 32, 16]), ALU.mult, [b_src, b_A8], [b_t2])
                                S.tt("dve", t1[:, :, :, :], t1[:, :, :, :], t2[:, :, :, :], ALU.add, [b_t1, b_t2], [b_t1])
                                if first:
                                    S.cp("dve", outv, t1[:, :, :, :], [b_t1], [b_xnew])
                                else:
                                    S.tt("dve", outv, outv, t1[:, :, :, :], ALU.add, [b_t1, b_xnew], [b_xnew])

                            cm4(xn, A1, xo, b_xold, True)
                            cm4(xn, Am7, BSh[:, :, :, :], b_BS, False)
                            S.barrier()
                        for q2 in range(2):
                            g0 = G0 + q2 * NGB
                            gs = slice(g0, g0 + NGB)
                            qs = slice(q2 * NGB, (q2 + 1) * NGB)
                            with contextlib.ExitStack() as p4:
                                U = sb(p4, "U", [128, NGB, 144], BF16); b_U = Buf("U")
                                Ts = sb(p4, "Ts", [128, NGB, 128], BF16); Rs = sb(p4, "Rs", [128, NGB, 144], BF16); b_Ts = Buf("Ts"); b_Rs = Buf("Rs")
                                Xc = sb(p4, "Xc", [128, NGB, 144], BF16); b_Xc = Buf("Xc")
                                ysb = sb(p4, "ysb", [128, NGB, 144], BF16); b_ysb = Buf("ysb")
                                ge1s = [sb(p4, "ge1_%d" % i, [128, 432]) for i in range(2)]; ge2s = [sb(p4, "ge2_%d" % i, [128, 432]) for i in range(2)]
                                b_ge1s = [Buf("ge1_%d" % i) for i in range(2)]; b_ge2s = [Buf("ge2_%d" % i) for i in range(2)]
                                S.op("pool", lambda e: e.memset(U[:, :, 128:144], 0.0), [], [b_U])
                                S.dma(U[:, :, 0:128], DU[gs].rearrange("g s m j -> (s m) g j"), reads=[b_DU], writes=[b_U])
                                S.dma(U[0:16, :, 128:144], DUS[gs].rearrange("g m b -> m g b"), reads=[b_DUS], writes=[b_U])
                                S.dma(Ts[:, :, :], DTm[gs].rearrange("g q c -> q g c"), reads=[b_DT], writes=[b_Ts])
                                S.dma(Rs[:, :, :], DR[gs].rearrange("g q c -> q g c"), reads=[b_DR], writes=[b_Rs])
                                S.cp("act", Xc[0:64, :, 0:128], BXv[:, 0, qs, 0:128], [b_BX], [b_Xc])
                                S.cp("dve", Xc[64:128, :, 0:128], BXv[:, 1, qs, 0:128], [b_BX], [b_Xc])
                                S.cp("act", Xc[0:64, :, 128:144], xold[:, 0, gs, :], [b_xold], [b_Xc])
                                S.cp("dve", Xc[64:128, :, 128:144], xold[:, 1, gs, :], [b_xold], [b_Xc])
                                for g3 in range((NGB + 2) // 3):
                                    bk = 4 + (g3 % 2)
                                    gls = list(range(g3 * 3, min(g3 * 3 + 3, NGB)))
                                    rd = [b_Ts, b_U, b_Rs, b_Xc]
                                    for i3, gl in enumerate(gls):
                                        osl = PS[bk][:, i3 * 144:(i3 + 1) * 144]
                                        S.op("pe", lambda e: e.matmul(osl, lhsT=Ts[:, gl, :], rhs=U[:, gl, :], start=True, stop=False), rd, [PB[bk]], inc=False)
                                        S.op("pe", lambda e: e.matmul(osl, lhsT=Rs[:, gl, 16:144], rhs=Xc[:, gl, :], start=False, stop=True), rd, [PB[bk]])
                                    nn = len(gls) * 144
                                    ge1, ge2, b_ge1, b_ge2 = ge1s[g3 % 2], ge2s[g3 % 2], b_ge1s[g3 % 2], b_ge2s[g3 % 2]
                                    S.act(ge1[:, 0:nn], PS[bk][:, 0:nn], AF.Square, [PB[bk]], [b_ge1])
                                    S.ts("dve", ge1[:, 0:nn], ge1[:, 0:nn], 0.044715, ALU.mult, [b_ge1], [b_ge1], s2=1.0, op1=ALU.add)
                                    S.tt("dve", ge1[:, 0:nn], ge1[:, 0:nn], PS[bk][:, 0:nn], ALU.mult, [b_ge1, PB[bk]], [b_ge1])
                                    S.act(ge2[:, 0:nn], ge1[:, 0:nn], AF.Sigmoid, [b_ge1], [b_ge2], scale=1.5957691216057308)
                                    S.tt("dve", ysb[:, gls[0]:gls[0] + len(gls), :].rearrange("q g j -> q (g j)"), ge2[:, 0:nn], PS[bk][:, 0:nn], ALU.mult,
                                         [b_ge2, PB[bk]], [b_ysb])
                                for s_i in range(8):
                                    S.dma(DY[gs, :, s_i, :].rearrange("g n j -> n g j"), ysb[s_i * 16:(s_i + 1) * 16, :, 0:128], reads=[b_ysb], writes=[b_DY], acc=True, eng="act")
                                S.dma(DYS[gs].rearrange("g n b -> n g b"), ysb[0:16, :, 128:144], reads=[b_ysb], writes=[b_DYS], acc=True, eng="act")
                                S.barrier()
                        checkpoint(8 + hf)
                with contextlib.ExitStack() as p3:
                    xo_t = sb(p3, "xo_t", [128, 2, 8, 64]); b_xot = Buf("xo_t")
                    xf_t = sb(p3, "xf_t", [64, 2, 64]); b_xft = Buf("xf_t")
                    xn2 = sb(p3, "xn2", [64, 2, 8, 2, G]); b_xn2 = Buf("xn2")
                    for c in range(2):
                        S.cp("dve", xn2[:, c, :, :, :].rearrange("q t b g -> q (t b) g"), xnew[:, c, :, :].rearrange("q g b -> q b g"), [b_xnew], [b_xn2])
                        for t in range(8):
                            bk = 6 + (t % 2)
                            S.op("pe", lambda e: e.transpose(out=PS[bk][:, 0:64], in_=xn2[:, c, t, :, :].rearrange("q b g -> q (b g)"), identity=ident[0:64, 0:64]),
                                 [b_xn2, b_ct], [PB[bk]])
                            S.cp("act", xo_t[:, c, t, :], PS[bk][:, 0:64], [PB[bk]], [b_xot])
                        S.op("pe", lambda e: e.transpose(out=PS[6][0:64, 64:128], in_=Xfin[:, c, :], identity=ident[0:64, 0:64]), [b_Xfin, b_ct], [PB[6]])
                        S.cp("act", xf_t[:, c, :], PS[6][0:64, 64:128], [PB[6]], [b_xft])
                    for nm, dst, c in (("s5reS", s5reS, 0), ("s5imS", s5imS, 1)):
                        bo = Buf("o_" + nm); out_bufs.append(bo)
                        S.dma(dst.rearrange("(t b2) g p -> (b2 g) t p", b2=2), xo_t[:, c, :, :], reads=[b_xot], writes=[bo])
                    for nm, dst, c in (("s5reP", s5reP, 0), ("s5imP", s5imP, 1)):
                        bo = Buf("o_" + nm); out_bufs.append(bo)
                        S.dma(dst, xf_t[:, c, :], reads=[b_xft], writes=[bo])
                    S.barrier()
                    checkpoint(12)

            S.phase = "glu"
            ybinT = sb(ph, "ybinT", [128, 8, NT2], BF16); b_ybin = [Buf("ybin%d" % i) for i in range(8)]
            with contextlib.ExitStack() as p2:
                yT = sb(p2, "yT", [128, 8, NT2], BF16); b_yT = Buf("yT")
                S.dma(yT[:, :, 0:1024], DY.rearrange("(f gl) n s j -> (gl n) f (s j)", gl=8), reads=[b_DY], writes=[b_yT], acc=True)
                S.dma(yT[:, :, 1024:1040], DYS.rearrange("(f gl) n b -> (gl n) f b", gl=8), reads=[b_DYS], writes=[b_yT], acc=True)
                bglu = sb(p2, "bglu", [128, 8]); b_bglu = Buf("bglu")
                S.dma(bglu[:, :], b_glu.rearrange("(f q) -> q f", q=128), writes=[b_bglu], allow_slow_non_contiguous=True)
                sg = [sb(p2, "sg%d" % i, [128, NT2], BF16) for i in range(2)]; b_sg = [Buf("sg%d" % i) for i in range(2)]
                zs = [sb(p2, "zs%d" % i, [128, NT2], BF16) for i in range(2)]; b_zs = [Buf("zs%d" % i) for i in range(2)]
                rbY = [[b_yT], [b_yT], [b_yT]]
                for f in range(8):
                    s_ = f % 2
                    wb, bwb = wnext()
                    proj(wb, bwb, 8, yT, rbY, GR2, [0, 1, 2])
                    for gi, (n0, nw) in enumerate(GR2):
                        S.act(sg[s_][:, n0:n0 + nw], PS[gi][:, 0:nw], AF.Sigmoid, [PB[gi], b_bglu], [b_sg[s_]], bias=bglu[:, f:f + 1])
                    S.tt("dve", sg[s_][:, :], sg[s_][:, :], yT[:, f, :], ALU.mult, [b_sg[s_], b_yT], [b_sg[s_]])
                    wb, bwb = wnext()
                    proj(wb, bwb, 16, hT, rb2, GR2, [3, 4, 5])
                    for gi, (n0, nw) in enumerate(GR2):
                        S.act(zs[s_][:, n0:n0 + nw], PS[3 + gi][:, 0:nw], AF.Silu, [PB[3 + gi]], [b_zs[s_]])
                    S.tt("dve", ybinT[:, f, 0:1024].rearrange("q (s j) -> q s j", s=8), sg[s_][:, 0:1024].rearrange("q (s j) -> q s j", s=8),
                         zs[s_][:, 0:1024].rearrange("q (j s) -> q s j", s=8), ALU.mult, [b_sg[s_], b_zs[s_]], [b_ybin[f]])
                    S.tt("dve", ybinT[:, f, 1024:1040], sg[s_][:, 1024:1040], zs[s_][:, 1024:1040], ALU.mult, [b_sg[s_], b_zs[s_]], [b_ybin[f]])
                S.barrier()
                checkpoint(13)

            S.phase = "mrg"
            mergedT = sb_top("mergedT", [128, 16, NT2], BF16)
            with contextlib.ExitStack() as p2:
                sga = sb(p2, "sga", [128, NT2]); sgb = sb(p2, "sgb", [128, NT2]); m1 = sb(p2, "m1", [128, NT2]); m2 = sb(p2, "m2", [128, NT2])
                b_sga = Buf("sga"); b_sgb = Buf("sgb"); b_m1 = Buf("m1"); b_m2 = Buf("m2")
                rbA = [list(b_yain[0:4]), list(b_yain[4:8]), [b_yain[8]]]
                rbB = [list(b_ybin)] * 3

                def srcs(dti):
                    return [(w_in[48 + dti], 16), (w_pa[dti], 8), (w_in[64 + dti], 16), (w_pb[dti], 8)]

                nxt = wnext

                for dti in range(16):
                    wb, bwb = nxt()
                    proj(wb, bwb, 16, hT, rb2, GR2, [0, 1, 2])
                    for gi, (n0, nw) in enumerate(GR2):
                        S.act(sga[:, n0:n0 + nw], PS[gi][:, 0:nw], AF.Sigmoid, [PB[gi]], [b_sga])
                    wb, bwb = nxt()
                    proj(wb, bwb, 8, yainT, rbA, GR2, [3, 4, 5])
                    for gi, (n0, nw) in enumerate(GR2):
                        S.tt("dve", m1[:, n0:n0 + nw], sga[:, n0:n0 + nw], PS[3 + gi][:, 0:nw], ALU.mult, [b_sga, PB[3 + gi]], [b_m1])
                    wb, bwb = nxt()
                    proj(wb, bwb, 16, hT, rb2, GR2, [0, 1, 2])
                    for gi, (n0, nw) in enumerate(GR2):
                        S.act(sgb[:, n0:n0 + nw], PS[gi][:, 0:nw], AF.Sigmoid, [PB[gi]], [b_sgb])
                    wb, bwb = nxt()
                    proj(wb, bwb, 8, ybinT, rbB, GR2, [3, 4, 5])
                    for gi in range(2):
                        ssl = slice(gi * 4, gi * 4 + 4)
                        S.tt("dve", m2[:, gi * 512:(gi + 1) * 512].rearrange("q (s j) -> q s j", s=4),
                             sgb[:, 0:1024].rearrange("q (j s) -> q s j", s=8)[:, ssl, :],
                             PS[3 + gi][:, 0:512].rearrange("q (s j) -> q s j", s=4), ALU.mult, [b_sgb, PB[3 + gi]], [b_m2])
                    S.tt("dve", m2[:, 1024:1040], sgb[:, 1024:1040], PS[5][:, 0:16], ALU.mult, [b_sgb, PB[5]], [b_m2])
                    S.tt("dve", mergedT[:, dti, 0:1024].rearrange("q (j s) -> q s j", s=8), m1[:, 0:1024].rearrange("q (j s) -> q s j", s=8),
                         m2[:, 0:1024].rearrange("q (s j) -> q s j", s=8), ALU.add, [b_m1, b_m2], [b_mrg[dti]])
                    S.tt("dve", mergedT[:, dti, 1024:1040], m1[:, 1024:1040], m2[:, 1024:1040], ALU.add, [b_m1, b_m2], [b_mrg[dti]])
                S.barrier()
                checkpoint(14)

        S.phase = "fin"
        with contextlib.ExitStack() as p2:
            wo = sb(p2, "wo", [128, 16, D], BF16); b_wo = [Buf("wo%d" % i) for i in range(16)]
            load_grep(p2, g_post)
            for i in range(16):
                s_ = i % 2
                S.dma(wst[s_][:, :, :], w_out[i], writes=[b_wst[s_]])
                S.cp("pool" if i % 2 == 0 else "act", wo[:, :, i * 128:(i + 1) * 128], wst[s_][:, :, :], [b_wst[s_]], [b_wo[i]])
            xr = [sb(p2, "xr%d" % i, [128, D]) for i in range(2)]; b_xr = [Buf("xr%d" % i) for i in range(2)]
            yo = [sb(p2, "yo%d" % i, [128, 512]) for i in range(2)]; b_yo = [Buf("yo%d" % i) for i in range(2)]
            junk = sb(p2, "junk2", [128, 512], BF16); b_junk = Buf("junk2")
            ss = sb(p2, "ss2", [128, 9, 5]); b_ss = Buf("ss2")
            S.op("pool", lambda e: e.memset(ss[:, :, :], 0.0), [], [b_ss])
            b_oy = [Buf("o_y%d" % i) for i in range(2)]; out_bufs.extend(b_oy)
            for tt_ in range(9):
                s_ = tt_ % 2
                np_ = 128 if tt_ < 8 else 16
                P_ = slice(0, np_)
                tsl = slice(tt_ * 128, tt_ * 128 + np_)
                S.dma(xr[s_][P_, :], xM[tt_ * 128:(tt_ + 1) * 128, :] if tt_ < 8 else xS, writes=[b_xr[s_]])
                for cg in range(4):
                    bk = (tt_ % 2) * 4 + cg
                    mm_group(bk, 512, [(mergedT[:, kt, tsl], wo[:, kt, cg * 512:(cg + 1) * 512]) for kt in range(16)],
                             list(b_mrg) + b_wo[cg * 4:(cg + 1) * 4], m=np_)
                    S.act(junk[P_, :], PS[bk][P_, :], AF.Square, [PB[bk]], [b_junk, b_ss], accum_out=ss[P_, tt_, cg:cg + 1])
                S.op("dve", lambda e: e.tensor_reduce(out=ss[P_, tt_, 4:5], in_=ss[P_, tt_, 0:4], axis=AX.X, op=ALU.add), [b_ss], [b_ss])
                S.ts("dve", ss[P_, tt_, 4:5], ss[P_, tt_, 4:5], 1.0 / D, ALU.mult, [b_ss], [b_ss], s2=EPS, op1=ALU.add)
                S.act(ss[P_, tt_, 4:5], ss[P_, tt_, 4:5], AF.Sqrt, [b_ss], [b_ss])
                S.op("dve", lambda e: e.reciprocal(out=ss[P_, tt_, 4:5], in_=ss[P_, tt_, 4:5]), [b_ss], [b_ss])
                for cg in range(4):
                    bk = (tt_ % 2) * 4 + cg
                    csl = slice(cg * 512, (cg + 1) * 512)
                    y_ = cg % 2
                    S.stt("dve", yo[y_][P_, :], PS[bk][P_, :], ss[P_, tt_, 4:5], GREP[0][P_, csl], ALU.mult, ALU.mult, [PB[bk], b_ss, b_grep], [b_yo[y_]])
                    S.tt("dve", xr[s_][P_, csl], xr[s_][P_, csl], yo[y_][P_, :], ALU.add, [b_yo[y_], b_xr[s_]], [b_xr[s_]])
                S.dma(yM[tt_ * 128:(tt_ + 1) * 128, :] if tt_ < 8 else yS, xr[s_][P_, :], reads=[b_xr[s_]], writes=[b_oy[s_]], eng="pool")
            S.barrier()
        S.barrier()
    except _Stop:
        pass
    return nc


_PROG = [None]


def prep_inputs(x_prompt, x_sample, state_ret, state_s5_re, state_s5_im, g_pre, w_in, w_pa, w_pb, w_out, g_post,
                s5_lam_re, s5_lam_im, s5_log_dt, s5_b_re, s5_b_im, s5_c_re, s5_c_im, s5_d, s5_w_glu, s5_b_glu):
    f = lambda a: np.ascontiguousarray(np.asarray(a, dtype=np.float32))
    x_prompt = f(x_prompt); x_sample = f(x_sample)
    def blk(w):
        K_, N_ = w.shape
        return np.ascontiguousarray(np.asarray(w, np.float32).reshape(K_ // 128, 128, N_ // 128, 128).transpose(2, 1, 0, 3))

    shared = {
        "w_in": blk(w_in[0]), "w_pa": blk(w_pa[0]), "w_pb": blk(w_pb[0]), "w_out": blk(w_out[0]), "w_glu": blk(s5_w_glu[0]),
        "g_pre": f(g_pre[0]), "g_post": f(g_post[0]), "b_glu": f(s5_b_glu[0]),
        "lam_re": f(s5_lam_re[0]), "lam_im": f(s5_lam_im[0]), "log_dt": f(s5_log_dt[0]),
        "b_re": f(s5_b_re[0]), "b_im": f(s5_b_im[0]), "c_re": f(s5_c_re[0]), "c_im": f(s5_c_im[0]), "s5d": f(s5_d[0]),
        "ctab": make_ctab(),
    }
    ropeP = make_rope(np.arange(1024))
    in_maps = []
    for c in range(8):
        b, half = c // 2, c % 2
        m = dict(shared)
        m["xM"] = f(x_prompt[b, half * 1024:(half + 1) * 1024])
        m["xP"] = f(x_prompt[b, 0:1024]) if half == 1 else np.zeros((1024, D), np.float32)
        m["xS"] = f(x_sample[c * 16:(c + 1) * 16, 0])
        m["sret"] = f(state_ret[0, c * 16:(c + 1) * 16])
        m["s5re"] = f(state_s5_re[0, c * 16:(c + 1) * 16])
        m["s5im"] = f(state_s5_im[0, c * 16:(c + 1) * 16])
        m["ropeP"] = ropeP
        m["ropeM"] = make_rope(np.concatenate([np.arange(1024) + half * 1024, np.full(16, PAST)]))
        in_maps.append(m)
    return in_maps


def assemble(R):
    y_prompt = np.stack([np.concatenate([R[2 * b]["yM"], R[2 * b + 1]["yM"]], 0) for b in range(4)]).astype(np.float32)
    y_sample = np.concatenate([R[c]["yS"] for c in range(8)], 0)[:, None, :].astype(np.float32)
    ret_p = np.stack([R[2 * b + 1]["retP"] for b in range(4)])[None].astype(np.float32)
    re_p = np.stack([R[2 * b + 1]["s5reP"] for b in range(4)])[None].astype(np.float32)
    im_p = np.stack([R[2 * b + 1]["s5imP"] for b in range(4)])[None].astype(np.float32)
    ret_s = np.concatenate([R[c]["retS"] for c in range(8)], 0)[None].astype(np.float32)
    re_s = np.concatenate([R[c]["s5reS"] for c in range(8)], 0)[None].astype(np.float32)
    im_s = np.concatenate([R[c]["s5imS"] for c in range(8)], 0)[None].astype(np.float32)
    return (y_prompt, y_sample, ret_p, re_p, im_p, ret_s, re_s, im_s)


def kernel(**inputs):
    in_maps = prep_inputs(**inputs)
    if _PROG[0] is None:
        _PROG[0] = build_program()
    res = run_bass_kernel_spmd(_PROG[0], in_maps, core_ids=list(range(8)))
    return assemble(res.results)
```
